# Optimizing a Trainium2 kernel written in Bass

```python
import math
import jax, jax.numpy as jnp
from jax import lax
import numpy as np

D_MODEL = 2048
BATCH = 16
SEQ = 2048
DEPTH = 4
DEC_BATCH = 1
DEC_SEQ = 8192
PAST_LEN = 128

HEAD_DIM = 128
N_HEADS_A = 8
N_KV_A = 2
GROUP_A = N_HEADS_A // N_KV_A
WINDOW = 128
BLOCK = 128
N_BUCKETS = 32
MAX_DISTANCE = 128
N_HEADS_B = 8
Q_LORA = 512
KV_LORA = 256
D_NOPE = 128
D_ROPE = 64
D_V = 128
ROPE_THETA = 10000.0
MIX_WIDTH = N_HEADS_A * HEAD_DIM + N_HEADS_B * D_V
D_FF = 5632
CONV_WIDTH = 3
ALPHA = (2 * DEPTH) ** 0.25
BETA = (8 * DEPTH) ** -0.25
LN_EPS = 1e-5
RMS_EPS = 1e-6
NEG_INF = -1e30
SPLITS = (N_HEADS_A * HEAD_DIM, N_KV_A * HEAD_DIM, N_KV_A * HEAD_DIM, Q_LORA, KV_LORA, D_ROPE)
IN_COLS = N_HEADS_A * HEAD_DIM + 2 * N_KV_A * HEAD_DIM + Q_LORA + KV_LORA + D_ROPE

kernel_name = "hymba_swa_mla_deepnorm_encoder"


def layer_norm(x, g, b):
    xf = x.astype(jnp.float32)
    mu = xf.mean(-1, keepdims=True)
    var = jnp.square(xf - mu).mean(-1, keepdims=True)
    y = (xf - mu) * lax.rsqrt(var + LN_EPS) * g.astype(jnp.float32) + b.astype(jnp.float32)
    return y.astype(x.dtype)


def rms_norm(x, g):
    xf = x.astype(jnp.float32)
    y = xf * lax.rsqrt(jnp.mean(xf * xf, -1, keepdims=True) + RMS_EPS) * g.astype(jnp.float32)
    return y.astype(x.dtype)


def t5_bucket(rel):
    half = N_BUCKETS // 2
    max_exact = half // 2
    ret = (rel > 0).astype(jnp.int32) * half
    n = jnp.abs(rel)
    large = max_exact + (jnp.log(jnp.maximum(n, 1).astype(jnp.float32) / max_exact)
                         / math.log(MAX_DISTANCE / max_exact) * (half - max_exact)).astype(jnp.int32)
    large = jnp.minimum(large, half - 1)
    return ret + jnp.where(n < max_exact, n, large)


def rope_tables(S, dtype):
    inv = 1.0 / (ROPE_THETA ** (jnp.arange(0, D_ROPE, 2, dtype=jnp.float32) / D_ROPE))
    ang = jnp.arange(S, dtype=jnp.float32)[:, None] * inv[None, :]
    return jnp.cos(ang).astype(dtype), jnp.sin(ang).astype(dtype)


def apply_rope(x, cos, sin):
    x1, x2 = jnp.split(x, 2, axis=-1)
    return jnp.concatenate([x1 * cos - x2 * sin, x2 * cos + x1 * sin], axis=-1)


def window_gqa(q, k, v, sink, rel_bias):
    B, S = q.shape[0], q.shape[1]
    nb = S // BLOCK
    pad = ((0, 0), (BLOCK, BLOCK), (0, 0), (0, 0))
    kp = jnp.pad(k, pad).reshape(B, nb + 2, BLOCK, N_KV_A, HEAD_DIM)
    vp = jnp.pad(v, pad).reshape(B, nb + 2, BLOCK, N_KV_A, HEAD_DIM)
    kw = jnp.concatenate([kp[:, :-2], kp[:, 1:-1], kp[:, 2:]], axis=2)
    vw = jnp.concatenate([vp[:, :-2], vp[:, 1:-1], vp[:, 2:]], axis=2)
    qb = q.reshape(B, nb, BLOCK, N_KV_A, GROUP_A, HEAD_DIM)
    s = jnp.einsum('bnqhgd,bnkhd->bnhgqk', qb, kw).astype(jnp.float32) * (HEAD_DIM ** -0.5)
    qi = jnp.arange(BLOCK, dtype=jnp.int32)
    ki = jnp.arange(3 * BLOCK, dtype=jnp.int32)
    rel = ki[None, :] - BLOCK - qi[:, None]
    bias = rel_bias.astype(jnp.float32)[t5_bucket(rel)]
    bias = bias.transpose(2, 0, 1).reshape(N_KV_A, GROUP_A, BLOCK, 3 * BLOCK)
    kpos = jnp.arange(nb, dtype=jnp.int32)[:, None] * BLOCK - BLOCK + ki[None, :]
    valid = (jnp.abs(rel) <= WINDOW)[None] & ((kpos >= 0) & (kpos < S))[:, None, :]
    s = jnp.where(valid[None, :, None, None], s + bias[None, None], NEG_INF)
    sk = jnp.broadcast_to(sink.astype(jnp.float32).reshape(N_KV_A, GROUP_A, 1, 1), s.shape[:-1] + (1,))
    p = jax.nn.softmax(jnp.concatenate([s, sk], axis=-1), axis=-1)[..., :-1]
    o = jnp.einsum('bnhgqk,bnkhd->bnqhgd', p.astype(v.dtype), vw)
    return o.reshape(B, S, N_HEADS_A * HEAD_DIM)


def latent_attention(c_q, c_kv, k_rope, q_norm_g, w_uq, kv_norm_g, w_ukv):
    B, S = c_q.shape[0], c_q.shape[1]
    nb = S // BLOCK
    q = (rms_norm(c_q, q_norm_g) @ w_uq).reshape(B, S, N_HEADS_B, D_NOPE + D_ROPE)
    kv = (rms_norm(c_kv, kv_norm_g) @ w_ukv).reshape(B, S, N_HEADS_B, D_NOPE + D_V)
    q_nope, q_rope = q[..., :D_NOPE], q[..., D_NOPE:]
    k_nope, v = kv[..., :D_NOPE], kv[..., D_NOPE:]
    cos, sin = rope_tables(S, q.dtype)
    q_rope = apply_rope(q_rope, cos[:, None, :], sin[:, None, :])
    k_rope = apply_rope(k_rope, cos, sin)
    scale = (D_NOPE + D_ROPE) ** -0.5
    qn = q_nope.reshape(B, nb, BLOCK, N_HEADS_B, D_NOPE).transpose(1, 0, 2, 3, 4)
    qr = q_rope.reshape(B, nb, BLOCK, N_HEADS_B, D_ROPE).transpose(1, 0, 2, 3, 4)

    def query_block(args):
        qn_b, qr_b = args
        s = (jnp.einsum('bqhd,bkhd->bhqk', qn_b, k_nope)
             + jnp.einsum('bqhd,bkd->bhqk', qr_b, k_rope)).astype(jnp.float32) * scale
        p = jax.nn.softmax(s, axis=-1)
        return jnp.einsum('bhqk,bkhd->bqhd', p.astype(v.dtype), v)

    o = lax.map(query_block, (qn, qr))
    return o.transpose(1, 0, 2, 3, 4).reshape(B, S, N_HEADS_B * D_V)


def conv_glu(x, w_up, conv_w, conv_b, w_down):
    u = x @ w_up
    up = jnp.pad(u, ((0, 0), (1, 1), (0, 0)))
    u = up[:, :-2] * conv_w[0] + up[:, 1:-1] * conv_w[1] + up[:, 2:] * conv_w[2] + conv_b
    g, val = u[..., :D_FF], u[..., D_FF:]
    return (jax.nn.silu(g) * val) @ w_down


def encoder_layer(x, rel_bias, w_in, sink, q_norm_g, w_uq, kv_norm_g, w_ukv, w_o,
                  ln1_g, ln1_b, w_up, conv_w, conv_b, w_down, ln2_g, ln2_b):
    B, S = x.shape[0], x.shape[1]
    h = x @ w_in
    cuts = [int(c) for c in np.cumsum(SPLITS)[:-1]]
    qa, ka, va, c_q, c_kv, k_rope = jnp.split(h, cuts, axis=-1)
    oa = window_gqa(qa.reshape(B, S, N_HEADS_A, HEAD_DIM),
                    ka.reshape(B, S, N_KV_A, HEAD_DIM),
                    va.reshape(B, S, N_KV_A, HEAD_DIM), sink, rel_bias)
    ob = latent_attention(c_q, c_kv, k_rope, q_norm_g, w_uq, kv_norm_g, w_ukv)
    attn = jnp.concatenate([oa, ob], axis=-1) @ w_o
    x = layer_norm(ALPHA * x + attn, ln1_g, ln1_b)
    x = layer_norm(ALPHA * x + conv_glu(x, w_up, conv_w, conv_b, w_down), ln2_g, ln2_b)
    return x


def run_trunk(x, rel_bias, w_in, sink, q_norm_g, w_uq, kv_norm_g, w_ukv, w_o,
              ln1_g, ln1_b, w_up, conv_w, conv_b, w_down, ln2_g, ln2_b):
    for l in range(DEPTH):
        x = encoder_layer(x, rel_bias, w_in[l], sink[l], q_norm_g[l], w_uq[l], kv_norm_g[l],
                          w_ukv[l], w_o[l], ln1_g[l], ln1_b[l], w_up[l], conv_w[l], conv_b[l],
                          w_down[l], ln2_g[l], ln2_b[l])
    return x


def setup_inputs(seed: int = 0) -> dict:
    key = jax.random.key(seed)
    ks = jax.random.split(key, 20)
    f32 = jnp.float32
    nrm = lambda k, shape, scale: jax.random.normal(k, shape, f32) * scale
    col_scale = jnp.concatenate([
        jnp.ones((SPLITS[0] + SPLITS[1],), f32),
        jnp.full((SPLITS[2],), BETA, f32),
        jnp.ones((SPLITS[3] + SPLITS[4] + SPLITS[5],), f32)])
    w_in = nrm(ks[2], (DEPTH, D_MODEL, IN_COLS), D_MODEL ** -0.5) * col_scale
    ukv_scale = jnp.concatenate([jnp.ones((D_NOPE,), f32), jnp.full((D_V,), BETA, f32)])
    w_ukv = (nrm(ks[6], (DEPTH, KV_LORA, N_HEADS_B, D_NOPE + D_V), KV_LORA ** -0.5)
             * ukv_scale).reshape(DEPTH, KV_LORA, N_HEADS_B * (D_NOPE + D_V))
    return {
        "x_prompt": nrm(ks[0], (BATCH, SEQ, D_MODEL), 1.0),
        "x_sample": nrm(ks[1], (DEC_BATCH, DEC_SEQ, D_MODEL), 1.0),
        "rel_bias": nrm(ks[3], (N_BUCKETS, N_HEADS_A), 0.5),
        "w_in": w_in,
        "sink": nrm(ks[4], (DEPTH, N_HEADS_A), 0.5),
        "q_norm_g": 1.0 + nrm(ks[5], (DEPTH, Q_LORA), 0.02),
        "w_uq": nrm(ks[7], (DEPTH, Q_LORA, N_HEADS_B * (D_NOPE + D_ROPE)), Q_LORA ** -0.5),
        "kv_norm_g": 1.0 + nrm(ks[8], (DEPTH, KV_LORA), 0.02),
        "w_ukv": w_ukv,
        "w_o": nrm(ks[9], (DEPTH, MIX_WIDTH, D_MODEL), BETA * MIX_WIDTH ** -0.5),
        "ln1_g": 1.0 + nrm(ks[10], (DEPTH, D_MODEL), 0.02),
        "ln1_b": nrm(ks[11], (DEPTH, D_MODEL), 0.02),
        "w_up": nrm(ks[12], (DEPTH, D_MODEL, 2 * D_FF), BETA * D_MODEL ** -0.5),
        "conv_w": nrm(ks[13], (DEPTH, CONV_WIDTH, 2 * D_FF), CONV_WIDTH ** -0.5),
        "conv_b": nrm(ks[14], (DEPTH, 2 * D_FF), 0.01),
        "w_down": nrm(ks[15], (DEPTH, D_FF, D_MODEL), BETA * D_FF ** -0.5),
        "ln2_g": 1.0 + nrm(ks[16], (DEPTH, D_MODEL), 0.02),
        "ln2_b": nrm(ks[17], (DEPTH, D_MODEL), 0.02),
    }


def reference(x_prompt, x_sample, rel_bias, w_in, sink, q_norm_g, w_uq, kv_norm_g, w_ukv, w_o,
              ln1_g, ln1_b, w_up, conv_w, conv_b, w_down, ln2_g, ln2_b):
    y_prompt = run_trunk(x_prompt, rel_bias, w_in, sink, q_norm_g, w_uq, kv_norm_g, w_ukv, w_o,
                         ln1_g, ln1_b, w_up, conv_w, conv_b, w_down, ln2_g, ln2_b)
    y_sample = run_trunk(x_sample, rel_bias, w_in, sink, q_norm_g, w_uq, kv_norm_g, w_ukv, w_o,
                         ln1_g, ln1_b, w_up, conv_w, conv_b, w_down, ln2_g, ln2_b)
    return (y_prompt, y_sample)
```

```python
import math
from contextlib import ExitStack

import numpy as np
import concourse.bass as bass
import concourse.mybir as mybir
from concourse.bass_utils import run_bass_kernel_spmd

F32 = mybir.dt.float32
BF16 = mybir.dt.bfloat16
AF = mybir.ActivationFunctionType
ALU = mybir.AluOpType

D_MODEL = 2048
KC = 16
T = 1024
TT = 512
N_HEADS = 8
Q_LORA = 512
KV_LORA = 256
D_FF = 5632
NPAIR = D_FF // 128
IN_COLS = 2368
DEPTH = 4
ALPHA = (2 * DEPTH) ** 0.25
LN_EPS = 1e-5
RMS_EPS = 1e-6
SCALE_A = 128 ** -0.5
SCALE_B = 192 ** -0.5
NEG = -1e30
LROWS = 832
WTW = 1152


class Reg:
    __slots__ = ("writer", "readers")

    def __init__(self):
        self.writer = None
        self.readers = {}


class Eng:
    def __init__(self, fw, name, h, inorder=False):
        self.h = h
        self.name = name
        self.sem = fw.nc.alloc_semaphore(name="e_" + name)
        self.key = fw.addsem(self.sem)
        self.count = 0
        self.known = {}
        self.inorder = inorder
        self.ninst = 0


class FW:
    def __init__(self, nc, n_dma_sems=48):
        self.nc = nc
        self.sems = []
        self.pe = Eng(self, "pe", nc.tensor, inorder=True)
        self.act = Eng(self, "act", nc.scalar)
        self.dve = Eng(self, "dve", nc.vector)
        self.pool = Eng(self, "pool", nc.gpsimd)
        self.sp = Eng(self, "sp", nc.sync)
        self.engs = (self.pe, self.act, self.dve, self.pool, self.sp)
        self.dsems = []
        for i in range(n_dma_sems):
            s = nc.alloc_semaphore(name=f"d{i}")
            self.dsems.append([self.addsem(s), 0])
        self.ring_hw = list(range(0, n_dma_sems * 2 // 3))
        self.ring_sw = list(range(n_dma_sems * 2 // 3, n_dma_sems))
        self.next_hw = 0
        self.next_sw = 0

    def addsem(self, s):
        self.sems.append(s)
        return len(self.sems) - 1

    def fresh(self):
        r = Reg()
        for e in self.engs:
            if e.count > 0:
                r.readers[e.key] = e.count
        for k, v in self.dsems:
            if v > 0:
                r.readers[k] = v
        return r

    def _deps(self, eng, reads, writes):
        need = {}
        for t in reads:
            if t.writer is not None:
                k, v = t.writer
                if need.get(k, 0) < v:
                    need[k] = v
        for t in writes:
            if t.writer is not None:
                k, v = t.writer
                if need.get(k, 0) < v:
                    need[k] = v
            for k, v in t.readers.items():
                if need.get(k, 0) < v:
                    need[k] = v
        for k, v in need.items():
            if eng.known.get(k, 0) >= v:
                continue
            if k == eng.key and eng.inorder:
                continue
            eng.h.wait_ge(self.sems[k], v)
            eng.known[k] = v

    def _mark(self, ev, reads, writes):
        k, v = ev
        for t in reads:
            if t.readers.get(k, 0) < v:
                t.readers[k] = v
        for t in writes:
            t.writer = ev
            t.readers = {}

    def op(self, eng, fn, reads=(), writes=(), signal=True):
        self._deps(eng, reads, writes)
        inst = fn()
        eng.ninst += 1
        if signal:
            eng.count += 1
            inst.then_inc(eng.sem, 1)
            ev = (eng.key, eng.count)
        else:
            ev = (eng.key, eng.count + 1)
        self._mark(ev, reads, writes)

    def dma(self, eng, out_ap, in_ap, reads=(), writes=(), **kw):
        if eng is self.pool:
            slot = self.dsems[self.ring_sw[self.next_sw % len(self.ring_sw)]]
            self.next_sw += 1
        else:
            slot = self.dsems[self.ring_hw[self.next_hw % len(self.ring_hw)]]
            self.next_hw += 1
        k = slot[0]
        if slot[1] > 0 and eng.known.get(k, 0) < slot[1]:
            eng.h.wait_ge(self.sems[k], slot[1])
            eng.known[k] = slot[1]
        self._deps(eng, reads, writes)
        inst = eng.h.dma_start(out=out_ap, in_=in_ap, **kw)
        slot[1] += 16
        inst.then_inc(self.sems[k], 16)
        eng.ninst += 1
        self._mark((k, slot[1]), reads, writes)

    def finish(self, eng):
        for k, v in self.dsems:
            if v > 0 and eng.known.get(k, 0) < v:
                eng.h.wait_ge(self.sems[k], v)
                eng.known[k] = v
        for e in self.engs:
            if e is not eng and e.count > 0 and eng.known.get(e.key, 0) < e.count:
                eng.h.wait_ge(self.sems[e.key], e.count)
                eng.known[e.key] = e.count


class Buf:
    __slots__ = ("t", "r")

    def __init__(self, t, r):
        self.t = t
        self.r = r


class Builder:
    def __init__(self, seq_lens, n_layers, out_layers_debug=False):
        self.seq_lens = seq_lens
        self.n_layers = n_layers
        nc = self.nc = bass.Bass("TRN2", target_bir_lowering=False)
        self.fw = FW(nc)
        self.uid = 0
        self.units = []
        tok = 0
        for s, S in enumerate(seq_lens):
            nj = S // T
            base = len(self.units)
            for j in range(nj):
                self.units.append(dict(seq=s, j=j, nj=nj, tok0=tok, pos0=j * T, u=base + j,
                                       ctx=list(range(base, base + nj))))
                tok += T
        self.ntok = tok
        NU = self.NU = len(self.units)
        L = n_layers
        di = lambda name, shape, dt=F32: nc.dram_tensor(name, shape, dt, kind="ExternalInput").ap()
        self.xin = di("xin", [self.ntok, D_MODEL])
        self.w_in = di("w_in", [L, D_MODEL, IN_COLS])
        self.sink = di("sink", [L, 8])
        self.q_norm_g = di("q_norm_g", [L, Q_LORA])
        self.w_uq = di("w_uq", [L, Q_LORA, 1536])
        self.kv_norm_g = di("kv_norm_g", [L, KV_LORA])
        self.w_ukv = di("w_ukv", [L, KV_LORA, 2048])
        self.w_o = di("w_o", [L, D_MODEL, D_MODEL])
        self.ln1_g = di("ln1_g", [L, D_MODEL])
        self.ln1_b = di("ln1_b", [L, D_MODEL])
        self.w_up = di("w_up", [L, D_MODEL, 2 * D_FF])
        self.conv_w = di("conv_w", [L, 3, 2 * D_FF])
        self.conv_b = di("conv_b", [L, 2 * D_FF])
        self.w_down = di("w_down", [L, D_FF, D_MODEL])
        self.ln2_g = di("ln2_g", [L, D_MODEL])
        self.ln2_b = di("ln2_b", [L, D_MODEL])
        maxS = max(seq_lens)
        self.maxS = maxS
        self.cosT = di("cosT", [128, maxS])
        self.sinT = di("sinT", [128, maxS])
        self.wtab = di("wtab", [8, 128, WTW])
        self.identI = di("ident", [128, 128])
        self.yout = nc.dram_tensor("yout", [self.ntok, D_MODEL], F32, kind="ExternalOutput").ap()
        dsc = lambda name, shape, dt: nc.dram_tensor(name, shape, dt, kind="Internal").ap()
        self.XT = dsc("XT", [NU, KC, 128, T], F32)
        self.X1 = dsc("X1", [NU, KC, 128, T], F32)
        self.Lb = dsc("Lb", [NU, LROWS, T], BF16)
        self.QA = dsc("QA", [NU, 8, 128, T], BF16)
        self.QN = dsc("QN", [NU, 8, 128, T], BF16)
        self.QR = dsc("QR", [NU, 4, 128, T], BF16)
        self.EK = dsc("EK", [NU, 8, 128, T], BF16)
        self.EV = dsc("EV", [NU, T, 1024], BF16)
        self.WT = dsc("WT", [8, 128, WTW], BF16)
        R = lambda: Reg()
        self.rXT = [R() for _ in range(NU)]
        self.rX1 = [R() for _ in range(NU)]
        self.rL = [R() for _ in range(NU)]
        self.rQ = [R() for _ in range(NU)]
        self.rE = [R() for _ in range(NU)]
        self.rWT = R()
        self.rOut = R()
        self.banks = [Buf(nc.alloc_psum_tensor(f"bank{i}", [128, 512], F32), Reg()) for i in range(8)]
        self.bank_rr = 0

    def alloc(self, st, shape, dt):
        self.uid += 1
        t = st.enter_context(self.nc.sbuf_tensor(f"t{self.uid}", shape, dt))
        return Buf(t, self.fw.fresh())

    def palloc(self, shape, dt):
        self.uid += 1
        return Buf(self.nc.alloc_sbuf_tensor(f"p{self.uid}", shape, dt), Reg())

    def mm(self, bank, out_ap, lhsT, rhs, start, stop, reads, signal=None):
        nc = self.nc
        self.fw.op(self.fw.pe, lambda: nc.tensor.matmul(out_ap, lhsT=lhsT, rhs=rhs, start=start, stop=stop),
                   reads=reads, writes=[bank.r], signal=(stop if signal is None else signal))

    def act(self, fn, reads, writes):
        self.fw.op(self.fw.act, fn, reads=reads, writes=writes)

    def dve(self, fn, reads, writes):
        self.fw.op(self.fw.dve, fn, reads=reads, writes=writes)

    def setup_consts(self):
        nc, fw = self.nc, self.fw
        L = self.n_layers
        self.ident_f = self.palloc([128, 128], F32)
        self.ident_b = self.palloc([128, 128], BF16)
        self.ones_b = self.palloc([128, 128], BF16)
        self.ones_f = self.palloc([128, 128], F32)
        self.lnp = self.palloc([128, L, 4, KC], F32)
        self.cw = self.palloc([128, L, 3, 88], F32)
        self.cb = self.palloc([128, L, 88], F32)
        self.qg = self.palloc([128, L, 4], F32)
        self.kvg = self.palloc([128, L, 2], F32)
        self.es = self.palloc([128, L, 8], F32)
        self.eps_ln = self.palloc([128, 1], F32)
        self.eps_rms = self.palloc([128, 1], F32)
        fw.dma(fw.sp, self.ident_f.t[:, :], self.identI, writes=[self.ident_f.r])
        self.act(lambda: nc.scalar.copy(out=self.ident_b.t[:, :], in_=self.ident_f.t[:, :]),
                 [self.ident_f.r], [self.ident_b.r])
        self.dve(lambda: nc.vector.memset(self.ones_b.t[:, :], 1.0), [], [self.ones_b.r])
        self.dve(lambda: nc.vector.memset(self.ones_f.t[:, :], 1.0), [], [self.ones_f.r])
        self.dve(lambda: nc.vector.memset(self.eps_ln.t[:, :], LN_EPS), [], [self.eps_ln.r])
        self.dve(lambda: nc.vector.memset(self.eps_rms.t[:, :], RMS_EPS), [], [self.eps_rms.r])
        for l in range(L):
            for i, src in enumerate((self.ln1_g, self.ln1_b, self.ln2_g, self.ln2_b)):
                fw.dma(fw.sp, self.lnp.t[:, l, i, :], src[l].rearrange("(c p) -> p c", p=128),
                       writes=[self.lnp.r], allow_slow_non_contiguous=True)
            for k in range(3):
                fw.dma(fw.sp, self.cw.t[:, l, k, :], self.conv_w[l, k].rearrange("(c p) -> p c", p=128),
                       writes=[self.cw.r], allow_slow_non_contiguous=True)
            fw.dma(fw.sp, self.cb.t[:, l, :], self.conv_b[l].rearrange("(c p) -> p c", p=128),
                   writes=[self.cb.r], allow_slow_non_contiguous=True)
            fw.dma(fw.sp, self.qg.t[:, l, :], self.q_norm_g[l].rearrange("(c p) -> p c", p=128),
                   writes=[self.qg.r], allow_slow_non_contiguous=True)
            fw.dma(fw.sp, self.kvg.t[:, l, :], self.kv_norm_g[l].rearrange("(c p) -> p c", p=128),
                   writes=[self.kvg.r], allow_slow_non_contiguous=True)
            fw.dma(fw.sp, self.es.t[:, l, :], self.sink[l].partition_broadcast(128), writes=[self.es.r])
        self.act(lambda: nc.scalar.activation(out=self.es.t[:, :, :], in_=self.es.t[:, :, :], func=AF.Exp),
                 [self.es.r], [self.es.r])
        with ExitStack() as st:
            for h in range(8):
                tf = self.alloc(st, [128, WTW], F32)
                tb = self.alloc(st, [128, WTW], BF16)
                fw.dma(fw.sp, tf.t[:, :], self.wtab[h], writes=[tf.r])
                self.act(lambda: nc.scalar.mul(out=tb.t[:, :], in_=tf.t[:, :], mul=float(1.0 / SCALE_A)),
                         [tf.r], [tb.r])
                fw.dma(fw.act, self.WT[h], tb.t[:, :], reads=[tb.r], writes=[self.rWT])

    def phase0(self, un):
        nc, fw = self.nc, self.fw
        u = un["u"]
        with ExitStack() as st:
            stage = self.alloc(st, [128, KC, T], F32)
            xbs = [self.alloc(st, [128, D_MODEL], F32) for _ in range(2)]
            n = 0
            for tb in range(8):
                xb = xbs[tb % 2]
                fw.dma(fw.sp, xb.t[:, :], self.xin[un["tok0"] + tb * 128: un["tok0"] + (tb + 1) * 128, :],
                       writes=[xb.r])
                for g in range(4):
                    bank = self.banks[n % 8]
                    n += 1
                    for j in range(4):
                        c = g * 4 + j
                        fw.op(fw.pe, lambda: nc.tensor.transpose(bank.t[:, j * 128:(j + 1) * 128],
                                                                 xb.t[:, c * 128:(c + 1) * 128],
                                                                 self.ident_f.t[:, :]),
                              reads=[xb.r, self.ident_f.r], writes=[bank.r], signal=(j == 3))
                    src = bank.t[:, :].rearrange("p (j t) -> p j t", j=4)
                    dst = stage.t[:, g * 4:(g + 1) * 4, tb * 128:(tb + 1) * 128]
                    if n % 2:
                        self.act(lambda: nc.scalar.copy(out=dst, in_=src), [bank.r], [stage.r])
                    else:
                        self.dve(lambda: nc.vector.tensor_copy(out=dst, in_=src), [bank.r], [stage.r])
            for c0 in range(0, KC, 4):
                fw.dma(fw.sp, self.XT[u, c0:c0 + 4].rearrange("c p t -> p c t"), stage.t[:, c0:c0 + 4, :],
                       reads=[stage.r], writes=[self.rXT[u]])

    def fm_chunk(self, lhs_fn, rhs_fn, nk, reads, M=128, ntiles=2, n=TT):
        banks = []
        for tt in range(ntiles):
            bank = self.banks[self.bank_rr % 8]
            self.bank_rr += 1
            for kc in range(nk):
                self.mm(bank, bank.t[0:M, 0:n], lhs_fn(kc), rhs_fn(kc, tt), kc == 0, kc == nk - 1, reads)
            banks.append(bank)
        return banks

    def next_bank(self):
        bank = self.banks[self.bank_rr % 8]
        self.bank_rr += 1
        return bank

    def rms_stats(self, sq, nchunk, rs, eps_buf, inv_n):
        nc = self.nc
        for tt in range(2):
            bank = self.next_bank()
            for c in range(nchunk):
                self.mm(bank, bank.t[:, :], self.ones_b.t[:, :], sq.t[:, c, tt * TT:(tt + 1) * TT],
                        c == 0, c == nchunk - 1, [self.ones_b.r, sq.r])
            self.act(lambda: nc.scalar.activation(out=rs.t[:, tt * TT:(tt + 1) * TT], in_=bank.t[:, :],
                                                  func=AF.Sqrt, bias=eps_buf.t[:, 0:1], scale=inv_n),
                     [bank.r, eps_buf.r], [rs.r])
        self.dve(lambda: nc.vector.reciprocal(out=rs.t[:, :], in_=rs.t[:, :]), [rs.r], [rs.r])

    def phaseA(self, un, l):
        nc, fw = self.nc, self.fw
        u = un["u"]
        with ExitStack() as st:
            xb = self.alloc(st, [128, KC, T], BF16)
            for c0 in range(0, KC, 4):
                fw.dma(fw.pool, xb.t[:, c0:c0 + 4, :], self.XT[u, c0:c0 + 4].rearrange("c p t -> p c t"),
                       reads=[self.rXT[u]], writes=[xb.r])
            wgs = [self.alloc(st, [128, KC, 256], BF16) for _ in range(3)]
            stages = [self.alloc(st, [128, T], BF16) for _ in range(4)]
            cq = self.alloc(st, [128, 4, T], F32)
            sq = self.alloc(st, [128, 4, T], BF16)
            cqn = self.alloc(st, [128, 4, T], BF16)
            ckvn = self.alloc(st, [128, 2, T], BF16)
            rs = self.alloc(st, [128, T], F32)
            cs = self.alloc(st, [128, T], F32)
            sn = self.alloc(st, [128, T], F32)
            tmp1 = self.alloc(st, [128, TT], F32)
            tmp2 = self.alloc(st, [128, TT], F32)
            vst = self.alloc(st, [128, 8, 256], BF16)
            fw.dma(fw.sp, cs.t[:, :], self.cosT[:, un["pos0"]:un["pos0"] + T], writes=[cs.r])
            fw.dma(fw.sp, sn.t[:, :], self.sinT[:, un["pos0"]:un["pos0"] + T], writes=[sn.r])
            w_in = self.w_in[l]
            wi = [0]
            sti = [0]

            def load_wg(c0, ncols):
                wg = wgs[wi[0] % 3]
                wi[0] += 1
                fw.dma(fw.pool, wg.t[:, :, 0:ncols], w_in[:, c0:c0 + ncols].rearrange("(c p) n -> p c n", p=128),
                       writes=[wg.r])
                return wg

            def evac_store(banks, dst_ap, dst_reg, M=128):
                stg = stages[sti[0] % 4]
                sti[0] += 1
                for tt, bank in enumerate(banks):
                    self.act(lambda: nc.scalar.copy(out=stg.t[0:M, tt * TT:(tt + 1) * TT], in_=bank.t[0:M, :]),
                             [bank.r], [stg.r])
                fw.dma(fw.act, dst_ap, stg.t[0:M, :], reads=[stg.r], writes=[dst_reg])

            for g in range(5):
                wg = load_wg(g * 256, 256)
                for j in range(2):
                    banks = self.fm_chunk(lambda kc: wg.t[:, kc, j * 128:(j + 1) * 128],
                                          lambda kc, tt: xb.t[:, kc, tt * TT:(tt + 1) * TT], KC, [wg.r, xb.r])
                    if g < 4:
                        evac_store(banks, self.QA[u, 2 * g + j], self.rQ[u])
                    else:
                        evac_store(banks, self.Lb[u, j * 128:(j + 1) * 128, :], self.rL[u])
            wg = load_wg(1280, 256)
            for tb in range(8):
                bank = self.next_bank()
                for kc in range(KC):
                    self.mm(bank, bank.t[:, 0:256], xb.t[:, kc, tb * 128:(tb + 1) * 128], wg.t[:, kc, 0:256],
                            kc == 0, kc == KC - 1, [wg.r, xb.r])
                self.act(lambda: nc.scalar.copy(out=vst.t[:, tb, :], in_=bank.t[:, 0:256]), [bank.r], [vst.r])
            va_view = self.Lb[u, 576:832, :].rearrange("r (a d) -> (r a) d", d=256)
            fw.dma(fw.act, va_view.rearrange("(tb p) d -> p tb d", p=128), vst.t[:, :, :],
                   reads=[vst.r], writes=[self.rL[u]])
            for g in range(2):
                wg = load_wg(1536 + g * 256, 256)
                for j in range(2):
                    c = 2 * g + j
                    banks = self.fm_chunk(lambda kc: wg.t[:, kc, j * 128:(j + 1) * 128],
                                          lambda kc, tt: xb.t[:, kc, tt * TT:(tt + 1) * TT], KC, [wg.r, xb.r])
                    for tt, bank in enumerate(banks):
                        self.act(lambda: nc.scalar.copy(out=cq.t[:, c, tt * TT:(tt + 1) * TT], in_=bank.t[:, :]),
                                 [bank.r], [cq.r])
                        self.act(lambda: nc.scalar.activation(out=sq.t[:, c, tt * TT:(tt + 1) * TT],
                                                              in_=bank.t[:, :], func=AF.Square),
                                 [bank.r], [sq.r])
            self.rms_stats(sq, 4, rs, self.eps_rms, 1.0 / Q_LORA)
            for c in range(4):
                self.dve(lambda: nc.vector.scalar_tensor_tensor(out=cqn.t[:, c, :], in0=cq.t[:, c, :],
                                                                scalar=self.qg.t[:, l, c:c + 1], in1=rs.t[:, :],
                                                                op0=ALU.mult, op1=ALU.mult),
                         [cq.r, rs.r, self.qg.r], [cqn.r])
            wg = load_wg(2048, 256)
            for c in range(2):
                banks = self.fm_chunk(lambda kc: wg.t[:, kc, c * 128:(c + 1) * 128],
                                      lambda kc, tt: xb.t[:, kc, tt * TT:(tt + 1) * TT], KC, [wg.r, xb.r])
                for tt, bank in enumerate(banks):
                    self.act(lambda: nc.scalar.copy(out=cq.t[:, c, tt * TT:(tt + 1) * TT], in_=bank.t[:, :]),
                             [bank.r], [cq.r])
                    self.act(lambda: nc.scalar.activation(out=sq.t[:, c, tt * TT:(tt + 1) * TT],
                                                          in_=bank.t[:, :], func=AF.Square),
                             [bank.r], [sq.r])
            self.rms_stats(sq, 2, rs, self.eps_rms, 1.0 / KV_LORA)
            for c in range(2):
                self.dve(lambda: nc.vector.scalar_tensor_tensor(out=ckvn.t[:, c, :], in0=cq.t[:, c, :],
                                                                scalar=self.kvg.t[:, l, c:c + 1], in1=rs.t[:, :],
                                                                op0=ALU.mult, op1=ALU.mult),
                         [cq.r, rs.r, self.kvg.r], [ckvn.r])
                fw.dma(fw.sp, self.Lb[u, 256 + c * 128:256 + (c + 1) * 128, :], ckvn.t[:, c, :],
                       reads=[ckvn.r], writes=[self.rL[u]])
            wg = wgs[wi[0] % 3]
            wi[0] += 1
            src = lambda a, b: w_in[:, a:b].rearrange("(c p) n -> p c n", p=128)
            fw.dma(fw.pool, wg.t[:, :, 0:64], src(2304, 2368), writes=[wg.r])
            fw.dma(fw.pool, wg.t[:, :, 64:96], src(2336, 2368), writes=[wg.r])
            fw.dma(fw.pool, wg.t[:, :, 96:128], src(2304, 2336), writes=[wg.r])
            bA = self.fm_chunk(lambda kc: wg.t[:, kc, 0:64], lambda kc, tt: xb.t[:, kc, tt * TT:(tt + 1) * TT],
                               KC, [wg.r, xb.r], M=64)
            bB = self.fm_chunk(lambda kc: wg.t[:, kc, 64:128], lambda kc, tt: xb.t[:, kc, tt * TT:(tt + 1) * TT],
                               KC, [wg.r, xb.r], M=64)
            stg = stages[sti[0] % 4]
            sti[0] += 1
            for tt in range(2):
                self.rope(bA[tt], bB[tt], cs, sn, tmp1, tmp2, stg, tt, 64)
            fw.dma(fw.sp, self.Lb[u, 512:576, :], stg.t[0:64, :], reads=[stg.r], writes=[self.rL[u]])

            wqn = self.alloc(st, [128, 4, 8, 128], BF16)
            wqr = self.alloc(st, [128, 4, 8, 64], BF16)
            wqs = self.alloc(st, [128, 4, 8, 64], BF16)
            wq = self.w_uq[l].rearrange("(c p) (h d) -> c p h d", p=128, d=192)
            for kc in range(4):
                fw.dma(fw.pool, wqn.t[:, kc, :, :], wq[kc, :, :, 0:128], writes=[wqn.r])
                fw.dma(fw.pool, wqr.t[:, kc, :, :], wq[kc, :, :, 128:192], writes=[wqr.r])
                fw.dma(fw.pool, wqs.t[:, kc, :, 0:32], wq[kc, :, :, 160:192], writes=[wqs.r])
                fw.dma(fw.pool, wqs.t[:, kc, :, 32:64], wq[kc, :, :, 128:160], writes=[wqs.r])
            for h in range(8):
                banks = self.fm_chunk(lambda kc: wqn.t[:, kc, h, :],
                                      lambda kc, tt: cqn.t[:, kc, tt * TT:(tt + 1) * TT], 4, [wqn.r, cqn.r])
                evac_store(banks, self.QN[u, h], self.rQ[u])
            for j in range(4):
                bA = self.fm_chunk(lambda kc: wqr.t[:, kc, 2 * j:2 * j + 2, :].rearrange("p h d -> p (h d)"),
                                   lambda kc, tt: cqn.t[:, kc, tt * TT:(tt + 1) * TT], 4, [wqr.r, cqn.r])
                bB = self.fm_chunk(lambda kc: wqs.t[:, kc, 2 * j:2 * j + 2, :].rearrange("p h d -> p (h d)"),
                                   lambda kc, tt: cqn.t[:, kc, tt * TT:(tt + 1) * TT], 4, [wqs.r, cqn.r])
                stg = stages[sti[0] % 4]
                sti[0] += 1
                for tt in range(2):
                    self.rope(bA[tt], bB[tt], cs, sn, tmp1, tmp2, stg, tt, 128)
                fw.dma(fw.sp, self.QR[u, j], stg.t[:, :], reads=[stg.r], writes=[self.rQ[u]])

            wkn = self.alloc(st, [128, 2, 8, 128], BF16)
            wkv = self.alloc(st, [128, 2, 8, 128], BF16)
            wk = self.w_ukv[l].rearrange("(c p) (h d) -> c p h d", p=128, d=256)
            for kc in range(2):
                fw.dma(fw.pool, wkn.t[:, kc, :, :], wk[kc, :, :, 0:128], writes=[wkn.r])
                fw.dma(fw.pool, wkv.t[:, kc, :, :], wk[kc, :, :, 128:256], writes=[wkv.r])
            for h in range(8):
                banks = self.fm_chunk(lambda kc: wkn.t[:, kc, h, :],
                                      lambda kc, tt: ckvn.t[:, kc, tt * TT:(tt + 1) * TT], 2, [wkn.r, ckvn.r])
                evac_store(banks, self.EK[u, h], self.rE[u])
            for tb in range(8):
                stg = stages[sti[0] % 4]
                sti[0] += 1
                for half in range(2):
                    bank = self.next_bank()
                    for kc in range(2):
                        self.mm(bank, bank.t[:, :], ckvn.t[:, kc, tb * 128:(tb + 1) * 128],
                                wkv.t[:, kc, half * 4:(half + 1) * 4, :].rearrange("p h d -> p (h d)"),
                                kc == 0, kc == 1, [wkv.r, ckvn.r])
                    self.act(lambda: nc.scalar.copy(out=stg.t[:, half * TT:(half + 1) * TT], in_=bank.t[:, :]),
                             [bank.r], [stg.r])
                fw.dma(fw.act, self.EV[u, tb * 128:(tb + 1) * 128, :], stg.t[:, :], reads=[stg.r],
                       writes=[self.rE[u]])

    def rope(self, bankA, bankB, cs, sn, tmp1, tmp2, stg, tt, M):
        nc = self.nc
        sl = slice(tt * TT, (tt + 1) * TT)
        self.dve(lambda: nc.vector.tensor_tensor(out=tmp1.t[0:M, :], in0=bankA.t[0:M, :], in1=cs.t[0:M, sl],
                                                 op=ALU.mult), [bankA.r, cs.r], [tmp1.r])
        self.dve(lambda: nc.vector.tensor_tensor(out=tmp2.t[0:M, :], in0=bankB.t[0:M, :], in1=sn.t[0:M, sl],
                                                 op=ALU.mult), [bankB.r, sn.r], [tmp2.r])
        self.dve(lambda: nc.vector.tensor_tensor(out=stg.t[0:M, sl], in0=tmp1.t[0:M, :], in1=tmp2.t[0:M, :],
                                                 op=ALU.add), [tmp1.r, tmp2.r], [stg.r])

    def attn_begin(self, scale, pts, accO, accL):
        self.ap_scale = scale
        self.ap_pts = pts
        self.ap_accO = accO
        self.ap_accL = accL
        self.ap_q = []
        self.ap_i = 0

    def attn_push(self, tl):
        nc = self.nc
        bank = self.sb[self.sb_rr % len(self.sb)]
        self.sb_rr += 1
        nm = len(tl["s_mms"])
        for mi, (lhsT, rhs, reads) in enumerate(tl["s_mms"]):
            self.mm(bank, bank.t[:, :], lhsT, rhs, mi == 0, mi == nm - 1, reads)
        pt = self.ap_pts[self.ap_i % len(self.ap_pts)]
        self.ap_i += 1
        scale = self.ap_scale
        self.act(lambda: nc.scalar.activation(out=pt.t[:, :], in_=bank.t[:, :], func=AF.Exp, scale=scale),
                 [bank.r], [pt.r])
        self.ap_q.append((tl, pt))
        if len(self.ap_q) > 2:
            self._attn_pv(*self.ap_q.pop(0))

    def attn_flush(self):
        while self.ap_q:
            self._attn_pv(*self.ap_q.pop(0))

    def _attn_pv(self, tl, pt):
        qt = tl["qt"]
        accO, accL = self.ap_accO, self.ap_accL
        self.mm(accO[qt], accO[qt].t[:, :], tl["v_lhsT"], pt.t[:, :], tl["first"], tl["last"],
                tl["v_reads"] + [pt.r])
        self.mm(accL[qt], accL[qt].t[:, :], self.ones_b.t[:, :], pt.t[:, :], tl["first"], tl["last"],
                [self.ones_b.r, pt.r])

    def attn_epilogue(self, attn, idx, accO, accL, rl, sink_ap=None, sink_reg=None):
        nc = self.nc
        for qt in range(2):
            if sink_ap is not None:
                self.dve(lambda: nc.vector.tensor_scalar(out=rl.t[:, :], in0=accL[qt].t[:, :], scalar1=sink_ap,
                                                         scalar2=None, op0=ALU.add),
                         [accL[qt].r, sink_reg], [rl.r])
                self.dve(lambda: nc.vector.reciprocal(out=rl.t[:, :], in_=rl.t[:, :]), [rl.r], [rl.r])
            else:
                self.dve(lambda: nc.vector.reciprocal(out=rl.t[:, :], in_=accL[qt].t[:, :]), [accL[qt].r], [rl.r])
            self.dve(lambda: nc.vector.tensor_tensor(out=attn.t[:, idx, qt * TT:(qt + 1) * TT],
                                                     in0=accO[qt].t[:, :], in1=rl.t[:, :], op=ALU.mult),
                     [accO[qt].r, rl.r], [attn.r])

    def phaseBC(self, un, l):
        nc, fw = self.nc, self.fw
        u = un["u"]
        ctx = un["ctx"]
        j, nj = un["j"], un["nj"]
        prev_u = u - 1 if j > 0 else None
        next_u = u + 1 if j < nj - 1 else None
        with ExitStack() as st_outer:
            attn = self.alloc(st_outer, [128, 16, T], BF16)
            with ExitStack() as st:
                nctx = len(ctx)
                kr = self.alloc(st, [64, nctx, T], BF16)
                for ci, c in enumerate(ctx):
                    fw.dma(fw.sp, kr.t[:, ci, :], self.Lb[c, 512:576, :], reads=[self.rL[c]], writes=[kr.r])
                kax = self.alloc(st, [128, 2, 10, 128], BF16)
                vax = self.alloc(st, [128, 10, 256], BF16)

                def va_view(c):
                    return self.Lb[c, 576:832, :].rearrange("r (a d) -> (r a) d", d=256)
                for kv in range(2):
                    fw.dma(fw.sp, kax.t[:, kv, 1:9, :].rearrange("p b k -> p (b k)"),
                           self.Lb[u, kv * 128:(kv + 1) * 128, :], reads=[self.rL[u]], writes=[kax.r])
                    if prev_u is not None:
                        fw.dma(fw.sp, kax.t[:, kv, 0, :], self.Lb[prev_u, kv * 128:(kv + 1) * 128, 896:1024],
                               reads=[self.rL[prev_u]], writes=[kax.r])
                    if next_u is not None:
                        fw.dma(fw.sp, kax.t[:, kv, 9, :], self.Lb[next_u, kv * 128:(kv + 1) * 128, 0:128],
                               reads=[self.rL[next_u]], writes=[kax.r])
                fw.dma(fw.sp, vax.t[:, 1:9, :], va_view(u).rearrange("(tb p) d -> p tb d", p=128),
                       reads=[self.rL[u]], writes=[vax.r])
                if prev_u is not None:
                    fw.dma(fw.sp, vax.t[:, 0, :], va_view(prev_u)[896:1024, :], reads=[self.rL[prev_u]],
                           writes=[vax.r])
                if next_u is not None:
                    fw.dma(fw.sp, vax.t[:, 9, :], va_view(next_u)[0:128, :], reads=[self.rL[next_u]],
                           writes=[vax.r])
                pts = [self.alloc(st, [128, TT], BF16) for _ in range(4)]
                rl = self.alloc(st, [128, TT], F32)
                qas = [self.alloc(st, [128, T], BF16) for _ in range(2)]
                wts = [self.alloc(st, [128, WTW], BF16) for _ in range(2)]
                qns = [self.alloc(st, [128, T], BF16) for _ in range(2)]
                qrs = [self.alloc(st, [64, T], BF16) for _ in range(2)]
                eks = [self.alloc(st, [128, T], BF16) for _ in range(3)]
                evs = [self.alloc(st, [128, 8, 128], BF16) for _ in range(3)]
                accO = self.banks[0:2]
                accL = self.banks[2:4]
                self.sb = self.banks[4:8]
                self.sb_rr = 0
                for h in range(8):
                    kvh = h // 4
                    qa = qas[h % 2]
                    wt = wts[h % 2]
                    fw.dma(fw.sp, qa.t[:, :], self.QA[u, h], reads=[self.rQ[u]], writes=[qa.r])
                    fw.dma(fw.sp, wt.t[:, :], self.WT[h], reads=[self.rWT], writes=[wt.r])
                    self.attn_begin(SCALE_A, pts, accO, accL)
                    for qt in range(2):
                        blks = [qt * 4 + o for o in range(6)]
                        blks = [b for b in blks if not ((b == 0 and prev_u is None) or (b == 9 and next_u is None))]
                        for b in blks:
                            o = b - qt * 4
                            self.attn_push(dict(
                                qt=qt,
                                s_mms=[(kax.t[:, kvh, b, :], qa.t[:, qt * TT:(qt + 1) * TT], [kax.r, qa.r]),
                                       (self.ident_b.t[:, :], wt.t[:, 640 - 128 * o:1152 - 128 * o],
                                        [self.ident_b.r, wt.r])],
                                v_lhsT=vax.t[:, b, kvh * 128:(kvh + 1) * 128], v_reads=[vax.r],
                                first=(b == blks[0]), last=(b == blks[-1])))
                    self.attn_flush()
                    self.attn_epilogue(attn, h, accO, accL, rl, self.es.t[:, l, h:h + 1], self.es.r)
                li = 0
                for h in range(8):
                    qn = qns[h % 2]
                    qr = qrs[h % 2]
                    fw.dma(fw.sp, qn.t[:, :], self.QN[u, h], reads=[self.rQ[u]], writes=[qn.r])
                    fw.dma(fw.sp, qr.t[:, :], self.QR[u, h // 2, (h % 2) * 64:(h % 2) * 64 + 64, :],
                           reads=[self.rQ[u]], writes=[qr.r])
                    self.attn_begin(SCALE_B, pts, accO, accL)
                    for ci, c in enumerate(ctx):
                        ek = eks[li % 3]
                        ev = evs[li % 3]
                        li += 1
                        fw.dma(fw.sp, ek.t[:, :], self.EK[c, h], reads=[self.rE[c]], writes=[ek.r])
                        fw.dma(fw.sp, ev.t[:, :, :],
                               self.EV[c, :, h * 128:(h + 1) * 128].rearrange("(kb p) d -> p kb d", p=128),
                               reads=[self.rE[c]], writes=[ev.r])
                        for kb in range(8):
                            for qt in range(2):
                                self.attn_push(dict(
                                    qt=qt,
                                    s_mms=[(ek.t[:, kb * 128:(kb + 1) * 128], qn.t[:, qt * TT:(qt + 1) * TT],
                                            [ek.r, qn.r]),
                                           (kr.t[0:64, ci, kb * 128:(kb + 1) * 128],
                                            qr.t[0:64, qt * TT:(qt + 1) * TT], [kr.r, qr.r])],
                                    v_lhsT=ev.t[:, kb, :], v_reads=[ev.r],
                                    first=(ci == 0 and kb == 0), last=(ci == nctx - 1 and kb == 7)))
                    self.attn_flush()
                    self.attn_epilogue(attn, 8 + h, accO, accL, rl)
            with ExitStack() as st:
                y = self.alloc(st, [128, KC, T], F32)
                wos = [self.alloc(st, [128, KC, 256], BF16) for _ in range(2)]
                xrs = [self.alloc(st, [128, T], F32) for _ in range(2)]
                w_o = self.w_o[l]
                mains = self.banks[0:4]
                mrr = 0
                for g in range(8):
                    wo = wos[g % 2]
                    fw.dma(fw.pool, wo.t[:, :, :], w_o[:, g * 256:(g + 1) * 256].rearrange("(c p) n -> p c n", p=128),
                           writes=[wo.r])
                    for jj in range(2):
                        o = 2 * g + jj
                        xr = xrs[o % 2]
                        fw.dma(fw.sp, xr.t[:, :], self.XT[u, o], reads=[self.rXT[u]], writes=[xr.r])
                        for tt in range(2):
                            bank = mains[mrr % 4]
                            mrr += 1
                            for kc in range(KC):
                                self.mm(bank, bank.t[:, :], wo.t[:, kc, jj * 128:(jj + 1) * 128],
                                        attn.t[:, kc, tt * TT:(tt + 1) * TT], kc == 0, kc == KC - 1,
                                        [wo.r, attn.r])
                            self.dve(lambda: nc.vector.scalar_tensor_tensor(
                                out=y.t[:, o, tt * TT:(tt + 1) * TT], in0=xr.t[:, tt * TT:(tt + 1) * TT],
                                scalar=float(ALPHA), in1=bank.t[:, :], op0=ALU.mult, op1=ALU.add),
                                [xr.r, bank.r], [y.r])
                self.layer_norm(st, y, l, 0, self.banks[4:8])
                for c0 in range(0, KC, 4):
                    fw.dma(fw.sp, self.X1[u, c0:c0 + 4].rearrange("c p t -> p c t"), y.t[:, c0:c0 + 4, :],
                           reads=[y.r], writes=[self.rX1[u]])

    def layer_norm(self, st, y, l, which, banks4):
        nc = self.nc
        ybs = [self.alloc(st, [128, T], BF16) for _ in range(2)]
        yqs = [self.alloc(st, [128, T], BF16) for _ in range(2)]
        mean = self.alloc(st, [128, T], F32)
        rstd = self.alloc(st, [128, T], F32)
        psM = banks4[0:2]
        psQ = banks4[2:4]
        for c in range(KC):
            yb = ybs[c % 2]
            yq = yqs[c % 2]
            self.act(lambda: nc.scalar.copy(out=yb.t[:, :], in_=y.t[:, c, :]), [y.r], [yb.r])
            self.act(lambda: nc.scalar.activation(out=yq.t[:, :], in_=y.t[:, c, :], func=AF.Square), [y.r], [yq.r])
            for tt in range(2):
                self.mm(psM[tt], psM[tt].t[:, :], self.ones_b.t[:, :], yb.t[:, tt * TT:(tt + 1) * TT],
                        c == 0, c == KC - 1, [self.ones_b.r, yb.r])
                self.mm(psQ[tt], psQ[tt].t[:, :], self.ones_b.t[:, :], yq.t[:, tt * TT:(tt + 1) * TT],
                        c == 0, c == KC - 1, [self.ones_b.r, yq.r], signal=True)
        inv = 1.0 / D_MODEL
        for tt in range(2):
            sl = slice(tt * TT, (tt + 1) * TT)
            self.act(lambda: nc.scalar.mul(out=mean.t[:, sl], in_=psM[tt].t[:, :], mul=inv), [psM[tt].r], [mean.r])
            self.dve(lambda: nc.vector.tensor_tensor(out=rstd.t[:, sl], in0=mean.t[:, sl], in1=mean.t[:, sl],
                                                     op=ALU.mult), [mean.r], [rstd.r])
            self.dve(lambda: nc.vector.scalar_tensor_tensor(out=rstd.t[:, sl], in0=psQ[tt].t[:, :], scalar=inv,
                                                            in1=rstd.t[:, sl], op0=ALU.mult, op1=ALU.subtract),
                     [psQ[tt].r, rstd.r], [rstd.r])
        self.act(lambda: nc.scalar.activation(out=rstd.t[:, :], in_=rstd.t[:, :], func=AF.Sqrt,
                                              bias=self.eps_ln.t[:, 0:1], scale=1.0),
                 [rstd.r, self.eps_ln.r], [rstd.r])
        self.dve(lambda: nc.vector.reciprocal(out=rstd.t[:, :], in_=rstd.t[:, :]), [rstd.r], [rstd.r])
        gi, bi = 2 * which, 2 * which + 1
        for c in range(KC):
            self.dve(lambda: nc.vector.tensor_tensor(out=y.t[:, c, :], in0=y.t[:, c, :], in1=mean.t[:, :],
                                                     op=ALU.subtract), [y.r, mean.r], [y.r])
            self.dve(lambda: nc.vector.tensor_tensor(out=y.t[:, c, :], in0=y.t[:, c, :], in1=rstd.t[:, :],
                                                     op=ALU.mult), [y.r, rstd.r], [y.r])
            self.act(lambda: nc.scalar.activation(out=y.t[:, c, :], in_=y.t[:, c, :], func=AF.Identity,
                                                  bias=self.lnp.t[:, l, bi, c:c + 1],
                                                  scale=self.lnp.t[:, l, gi, c:c + 1]),
                     [y.r, self.lnp.r], [y.r])

    def phaseD(self, un, l, last):
        nc, fw = self.nc, self.fw
        u = un["u"]
        j, nj = un["j"], un["nj"]
        prev_u = u - 1 if j > 0 else None
        next_u = u + 1 if j < nj - 1 else None
        NW = 342
        with ExitStack() as st_outer:
            y2 = self.alloc(st_outer, [128, KC, T], F32)
            with ExitStack() as st:
                x1b = self.alloc(st, [128, KC, T + 2], BF16)
                for c0 in range(0, KC, 4):
                    fw.dma(fw.pool, x1b.t[:, c0:c0 + 4, 1:T + 1], self.X1[u, c0:c0 + 4].rearrange("c p t -> p c t"),
                           reads=[self.rX1[u]], writes=[x1b.r])
                    fw.dma(fw.sp, y2.t[:, c0:c0 + 4, :], self.X1[u, c0:c0 + 4].rearrange("c p t -> p c t"),
                           reads=[self.rX1[u]], writes=[y2.r])
                if prev_u is not None:
                    fw.dma(fw.pool, x1b.t[:, :, 0:1], self.X1[prev_u, :, :, T - 1:T].rearrange("c p t -> p c t"),
                           reads=[self.rX1[prev_u]], writes=[x1b.r], allow_slow_non_contiguous=True)
                else:
                    self.dve(lambda: nc.vector.memset(x1b.t[:, :, 0:1], 0.0), [], [x1b.r])
                if next_u is not None:
                    fw.dma(fw.pool, x1b.t[:, :, T + 1:T + 2], self.X1[next_u, :, :, 0:1].rearrange("c p t -> p c t"),
                           reads=[self.rX1[next_u]], writes=[x1b.r], allow_slow_non_contiguous=True)
                else:
                    self.dve(lambda: nc.vector.memset(x1b.t[:, :, T + 1:T + 2], 0.0), [], [x1b.r])
                for c in range(KC):
                    self.act(lambda: nc.scalar.mul(out=y2.t[:, c, :], in_=y2.t[:, c, :], mul=float(ALPHA)),
                             [y2.r], [y2.r])
                wus = [self.alloc(st, [128, KC, 2, 256], BF16) for _ in range(2)]
                wds = [self.alloc(st, [128, 2, D_MODEL], BF16) for _ in range(2)]
                us = [self.alloc(st, [128, T + 2], F32) for _ in range(2)]
                ag = self.alloc(st, [128, T], F32)
                av = self.alloc(st, [128, T], F32)
                sg = self.alloc(st, [128, T], F32)
                hgs = [self.alloc(st, [128, 2, T], BF16) for _ in range(2)]
                w_up = self.w_up[l]
                w_dn = self.w_down[l]
                usets = [self.banks[0:3], self.banks[3:6]]
                dbanks = self.banks[6:8]
                ci = 0
                drr = 0
                NG = NPAIR // 2

                def down(gi):
                    nonlocal drr
                    wd = wds[gi % 2]
                    hg = hgs[gi % 2]
                    for o in range(KC):
                        for tt in range(2):
                            bank = dbanks[drr % 2]
                            drr += 1
                            for k in range(2):
                                self.mm(bank, bank.t[:, :], wd.t[:, k, o * 128:(o + 1) * 128],
                                        hg.t[:, k, tt * TT:(tt + 1) * TT], k == 0, k == 1, [wd.r, hg.r])
                            self.dve(lambda: nc.vector.tensor_tensor(
                                out=y2.t[:, o, tt * TT:(tt + 1) * TT], in0=bank.t[:, :],
                                in1=y2.t[:, o, tt * TT:(tt + 1) * TT], op=ALU.add), [bank.r, y2.r], [y2.r])

                for gi in range(NG):
                    wu = wus[gi % 2]
                    wd = wds[gi % 2]
                    hg = hgs[gi % 2]
                    p0 = 2 * gi
                    for gv in range(2):
                        c0 = gv * D_FF + p0 * 128
                        fw.dma(fw.pool, wu.t[:, :, gv, :], w_up[:, c0:c0 + 256].rearrange("(c p) n -> p c n", p=128),
                               writes=[wu.r])
                    fw.dma(fw.pool, wd.t[:, :, :], w_dn[p0 * 128:(p0 + 2) * 128, :].rearrange("(k p) n -> p k n", p=128),
                           writes=[wd.r])
                    for k in range(2):
                        pr = p0 + k
                        for gv in range(2):
                            bset = usets[ci % 2]
                            ub = us[ci % 2]
                            ci += 1
                            for jn in range(3):
                                bank = bset[jn]
                                for kc in range(KC):
                                    self.mm(bank, bank.t[:, 0:NW], wu.t[:, kc, gv, k * 128:(k + 1) * 128],
                                            x1b.t[:, kc, jn * NW:(jn + 1) * NW], kc == 0, kc == KC - 1,
                                            [wu.r, x1b.r])
                                self.act(lambda: nc.scalar.copy(out=ub.t[:, jn * NW:(jn + 1) * NW],
                                                                in_=bank.t[:, 0:NW]), [bank.r], [ub.r])
                            a = ag if gv == 0 else av
                            col = gv * NPAIR + pr
                            self.act(lambda: nc.scalar.activation(out=a.t[:, :], in_=ub.t[:, 1:T + 1],
                                                                  func=AF.Identity,
                                                                  bias=self.cb.t[:, l, col:col + 1],
                                                                  scale=self.cw.t[:, l, 1, col:col + 1]),
                                     [ub.r, self.cw.r, self.cb.r], [a.r])
                            self.dve(lambda: nc.vector.scalar_tensor_tensor(
                                out=a.t[:, :], in0=ub.t[:, 0:T], scalar=self.cw.t[:, l, 0, col:col + 1],
                                in1=a.t[:, :], op0=ALU.mult, op1=ALU.add), [ub.r, a.r, self.cw.r], [a.r])
                            self.dve(lambda: nc.vector.scalar_tensor_tensor(
                                out=a.t[:, :], in0=ub.t[:, 2:T + 2], scalar=self.cw.t[:, l, 2, col:col + 1],
                                in1=a.t[:, :], op0=ALU.mult, op1=ALU.add), [ub.r, a.r, self.cw.r], [a.r])
                            if gv == 0:
                                self.act(lambda: nc.scalar.activation(out=sg.t[:, :], in_=ag.t[:, :], func=AF.Silu),
                                         [ag.r], [sg.r])
                        self.dve(lambda: nc.vector.tensor_tensor(out=hg.t[:, k, :], in0=sg.t[:, :], in1=av.t[:, :],
                                                                 op=ALU.mult), [sg.r, av.r], [hg.r])
                    if gi > 0:
                        down(gi - 1)
                down(NG - 1)
            with ExitStack() as st:
                self.layer_norm(st, y2, l, 1, self.banks[0:4])
                if not last:
                    for c0 in range(0, KC, 4):
                        fw.dma(fw.sp, self.XT[u, c0:c0 + 4].rearrange("c p t -> p c t"), y2.t[:, c0:c0 + 4, :],
                               reads=[y2.r], writes=[self.rXT[u]])
                else:
                    osts = [self.alloc(st, [128, D_MODEL], F32) for _ in range(2)]
                    n = 0
                    for tb in range(8):
                        ost = osts[tb % 2]
                        for g in range(4):
                            bank = self.banks[4 + n % 4]
                            n += 1
                            for jn in range(4):
                                c = g * 4 + jn
                                fw.op(fw.pe, lambda: nc.tensor.transpose(bank.t[:, jn * 128:(jn + 1) * 128],
                                                                         y2.t[:, c, tb * 128:(tb + 1) * 128],
                                                                         self.ident_f.t[:, :]),
                                      reads=[y2.r, self.ident_f.r], writes=[bank.r], signal=(jn == 3))
                            if n % 2:
                                self.act(lambda: nc.scalar.copy(out=ost.t[:, g * 512:(g + 1) * 512], in_=bank.t[:, :]),
                                         [bank.r], [ost.r])
                            else:
                                self.dve(lambda: nc.vector.tensor_copy(out=ost.t[:, g * 512:(g + 1) * 512],
                                                                       in_=bank.t[:, :]), [bank.r], [ost.r])
                        r0 = un["tok0"] + tb * 128
                        fw.dma(fw.sp, self.yout[r0:r0 + 128, :], ost.t[:, :], reads=[ost.r], writes=[self.rOut])

    def build(self):
        self.setup_consts()
        for un in self.units:
            self.phase0(un)
        for l in range(self.n_layers):
            for un in self.units:
                self.phaseA(un, l)
            for un in self.units:
                self.phaseBC(un, l)
            for un in self.units:
                self.phaseD(un, l, last=(l == self.n_layers - 1))
        self.fw.finish(self.fw.sp)
        return self.nc


def _const_tables(maxS, rel_bias):
    import jax
    import jax.numpy as jnp
    cpu = jax.devices("cpu")[0]
    with jax.default_device(cpu):
        inv = 1.0 / (10000.0 ** (jnp.arange(0, 64, 2, dtype=jnp.float32) / 64))
        ang = jnp.arange(maxS, dtype=jnp.float32)[:, None] * inv[None, :]
        cos = np.asarray(jnp.cos(ang).astype(jnp.float32)).T
        sin = np.asarray(jnp.sin(ang).astype(jnp.float32)).T
        rel = jnp.arange(-1200, 1201, dtype=jnp.int32)
        half, max_exact = 16, 8
        ret = (rel > 0).astype(jnp.int32) * half
        n = jnp.abs(rel)
        large = max_exact + (jnp.log(jnp.maximum(n, 1).astype(jnp.float32) / max_exact)
                             / math.log(128 / max_exact) * (half - max_exact)).astype(jnp.int32)
        large = jnp.minimum(large, half - 1)
        bucket = np.asarray(ret + jnp.where(n < max_exact, n, large))
    idx = np.arange(128) % 32
    cosT = np.ascontiguousarray(cos[idx, :])
    sgn = np.where((np.arange(128) % 64) < 32, -1.0, 1.0).astype(np.float32)[:, None]
    sinT = np.ascontiguousarray(sin[idx, :] * sgn)
    kl = np.arange(128)[:, None]
    jj = np.arange(WTW)[None, :]
    relm = kl - jj + 512
    bidx = bucket[relm + 1200]
    valid = np.abs(relm) <= 128
    rb_ext = np.concatenate([np.asarray(rel_bias, np.float32), np.full((1, 8), NEG, np.float32)], axis=0)
    bidx = np.where(valid, bidx, 32)
    wtab = np.ascontiguousarray(rb_ext[bidx].transpose(2, 0, 1))
    return cosT.astype(np.float32), sinT.astype(np.float32), wtab.astype(np.float32)


_WNAMES = ["w_in", "sink", "q_norm_g", "w_uq", "kv_norm_g", "w_ukv", "w_o", "ln1_g", "ln1_b", "w_up",
           "conv_w", "conv_b", "w_down", "ln2_g", "ln2_b"]


def run(x_seqs_per_core, weights, rel_bias, n_layers, n_cores):
    seq_lens = [int(a.shape[0]) for a in x_seqs_per_core[0]]
    b = Builder(seq_lens, n_layers)
    nc = b.build()
    cosT, sinT, wtab = _const_tables(max(seq_lens), rel_bias)
    ident = np.eye(128, dtype=np.float32)
    in_maps = []
    for c in range(n_cores):
        m = {"xin": np.ascontiguousarray(np.concatenate(x_seqs_per_core[c], axis=0), dtype=np.float32),
             "cosT": cosT, "sinT": sinT, "wtab": wtab, "ident": ident}
        for k in _WNAMES:
            m[k] = np.ascontiguousarray(np.asarray(weights[k], np.float32)[:n_layers])
        in_maps.append(m)
    res = run_bass_kernel_spmd(nc, in_maps, core_ids=list(range(n_cores)))
    outs = []
    for c in range(n_cores):
        y = res.results[c]["yout"]
        o, t0 = [], 0
        for S in seq_lens:
            o.append(y[t0:t0 + S])
            t0 += S
        outs.append(o)
    return outs


def kernel(x_prompt, x_sample, rel_bias, w_in, sink, q_norm_g, w_uq, kv_norm_g, w_ukv, w_o,
           ln1_g, ln1_b, w_up, conv_w, conv_b, w_down, ln2_g, ln2_b):
    weights = dict(w_in=w_in, sink=sink, q_norm_g=q_norm_g, w_uq=w_uq, kv_norm_g=kv_norm_g, w_ukv=w_ukv,
                   w_o=w_o, ln1_g=ln1_g, ln1_b=ln1_b, w_up=w_up, conv_w=conv_w, conv_b=conv_b,
                   w_down=w_down, ln2_g=ln2_g, ln2_b=ln2_b)
    x_prompt = np.asarray(x_prompt, np.float32)
    x_sample = np.asarray(x_sample, np.float32)
    n = 8
    per_core = [[x_prompt[2 * c], x_prompt[2 * c + 1], x_sample[0]] for c in range(n)]
    outs = run(per_core, weights, rel_bias, DEPTH, n)
    y_prompt = np.stack([outs[c][i] for c in range(n) for i in range(2)], axis=0)
    y_sample = outs[0][2][None]
    return (y_prompt.astype(np.float32), y_sample.astype(np.float32))
```

```python
import math
from contextlib import ExitStack

import numpy as np
import concourse.bass as bass
import concourse.mybir as mybir
from concourse.bass_utils import run_bass_kernel_spmd

F32 = mybir.dt.float32
BF16 = mybir.dt.bfloat16
AF = mybir.ActivationFunctionType
ALU = mybir.AluOpType

D_MODEL = 2048
KC = 16
T = 1024
TT = 512
N_HEADS = 8
Q_LORA = 512
KV_LORA = 256
D_FF = 5632
NPAIR = D_FF // 128
IN_COLS = 2368
DEPTH = 4
ALPHA = (2 * DEPTH) ** 0.25
LN_EPS = 1e-5
RMS_EPS = 1e-6
SCALE_A = 128 ** -0.5
SCALE_B = 192 ** -0.5
NEG = -1e30
LROWS = 832
WTW = 1152


class Reg:
    __slots__ = ("writer", "readers")

    def __init__(self):
        self.writer = None
        self.readers = {}


class Eng:
    def __init__(self, fw, name, h, inorder=False):
        self.h = h
        self.name = name
        self.sem = fw.nc.alloc_semaphore(name="e_" + name)
        self.key = fw.addsem(self.sem)
        self.count = 0
        self.known = {}
        self.inorder = inorder
        self.ninst = 0


class FW:
    def __init__(self, nc, n_dma_sems=48):
        self.nc = nc
        self.sems = []
        self.pe = Eng(self, "pe", nc.tensor, inorder=True)
        self.act = Eng(self, "act", nc.scalar)
        self.dve = Eng(self, "dve", nc.vector)
        self.pool = Eng(self, "pool", nc.gpsimd)
        self.sp = Eng(self, "sp", nc.sync)
        self.engs = (self.pe, self.act, self.dve, self.pool, self.sp)
        self.dsems = []
        for i in range(n_dma_sems):
            s = nc.alloc_semaphore(name=f"d{i}")
            self.dsems.append([self.addsem(s), 0])
        self.ring_hw = list(range(0, n_dma_sems * 2 // 3))
        self.ring_sw = list(range(n_dma_sems * 2 // 3, n_dma_sems))
        self.next_hw = 0
        self.next_sw = 0
        self.ccsem = [self.addsem(nc.alloc_semaphore(name="cc")), 0]

    def addsem(self, s):
        self.sems.append(s)
        return len(self.sems) - 1

    def fresh(self):
        r = Reg()
        for e in self.engs:
            if e.count > 0:
                r.readers[e.key] = e.count
        for k, v in self.dsems + [self.ccsem]:
            if v > 0:
                r.readers[k] = v
        return r

    def _deps(self, eng, reads, writes):
        need = {}
        for t in reads:
            if t.writer is not None:
                k, v = t.writer
                if need.get(k, 0) < v:
                    need[k] = v
        for t in writes:
            if t.writer is not None:
                k, v = t.writer
                if need.get(k, 0) < v:
                    need[k] = v
            for k, v in t.readers.items():
                if need.get(k, 0) < v:
                    need[k] = v
        for k, v in need.items():
            if eng.known.get(k, 0) >= v:
                continue
            if k == eng.key and eng.inorder:
                continue
            eng.h.wait_ge(self.sems[k], v)
            eng.known[k] = v

    def _mark(self, ev, reads, writes):
        k, v = ev
        for t in reads:
            if t.readers.get(k, 0) < v:
                t.readers[k] = v
        for t in writes:
            t.writer = ev
            t.readers = {}

    def op(self, eng, fn, reads=(), writes=(), signal=True):
        self._deps(eng, reads, writes)
        inst = fn()
        eng.ninst += 1
        if signal:
            eng.count += 1
            inst.then_inc(eng.sem, 1)
            ev = (eng.key, eng.count)
        else:
            ev = (eng.key, eng.count + 1)
        self._mark(ev, reads, writes)

    def dma(self, eng, out_ap, in_ap, reads=(), writes=(), **kw):
        if eng is self.pool:
            slot = self.dsems[self.ring_sw[self.next_sw % len(self.ring_sw)]]
            self.next_sw += 1
        else:
            slot = self.dsems[self.ring_hw[self.next_hw % len(self.ring_hw)]]
            self.next_hw += 1
        k = slot[0]
        if slot[1] > 0 and eng.known.get(k, 0) < slot[1]:
            eng.h.wait_ge(self.sems[k], slot[1])
            eng.known[k] = slot[1]
        self._deps(eng, reads, writes)
        inst = eng.h.dma_start(out=out_ap, in_=in_ap, **kw)
        slot[1] += 16
        inst.then_inc(self.sems[k], 16)
        eng.ninst += 1
        self._mark((k, slot[1]), reads, writes)

    def allreduce(self, out_ap, in_ap, reads=(), writes=(), n_cores=8):
        eng = self.pool
        k = self.ccsem[0]
        if self.ccsem[1] > 0 and eng.known.get(k, 0) < self.ccsem[1]:
            eng.h.wait_ge(self.sems[k], self.ccsem[1])
            eng.known[k] = self.ccsem[1]
        self._deps(eng, reads, writes)
        inst = eng.h.collective_compute("AllReduce", ALU.add, replica_groups=[list(range(n_cores))],
                                        ins=[in_ap], outs=[out_ap])
        self.ccsem[1] += 1
        inst.then_inc(self.sems[k], 1)
        eng.ninst += 1
        self._mark((k, self.ccsem[1]), reads, writes)

    def finish(self, eng):
        for k, v in self.dsems + [self.ccsem]:
            if v > 0 and eng.known.get(k, 0) < v:
                eng.h.wait_ge(self.sems[k], v)
                eng.known[k] = v
        for e in self.engs:
            if e is not eng and e.count > 0 and eng.known.get(e.key, 0) < e.count:
                eng.h.wait_ge(self.sems[e.key], e.count)
                eng.known[e.key] = e.count


class Buf:
    __slots__ = ("t", "r")

    def __init__(self, t, r):
        self.t = t
        self.r = r


class Builder:
    def __init__(self, seq_lens, n_layers, shard=False):
        self.seq_lens = seq_lens
        self.n_layers = n_layers
        self.shard = shard
        nc = self.nc = bass.Bass("TRN2", target_bir_lowering=False)
        self.fw = FW(nc)
        self.uid = 0
        self.units = []
        tok = 0
        for s, S in enumerate(seq_lens):
            nj = S // T
            base = len(self.units)
            for j in range(nj):
                self.units.append(dict(seq=s, j=j, nj=nj, tok0=tok, pos0=j * T, u=base + j,
                                       ctx=list(range(base, base + nj))))
                tok += T
        if shard:
            self.units.append(dict(seq=len(seq_lens), j=0, nj=1, tok0=tok, pos0=0, u=len(self.units),
                                   ctx=list(range(8)), shard=True))
            tok += T
        self.ntok = tok
        NU = self.NU = len(self.units)
        L = n_layers
        di = lambda name, shape, dt=F32: nc.dram_tensor(name, shape, dt, kind="ExternalInput").ap()
        self.xin = di("xin", [self.ntok, D_MODEL])
        self.w_in = di("w_in", [L, D_MODEL, IN_COLS])
        self.sink = di("sink", [L, 8])
        self.q_norm_g = di("q_norm_g", [L, Q_LORA])
        self.w_uq = di("w_uq", [L, Q_LORA, 1536])
        self.kv_norm_g = di("kv_norm_g", [L, KV_LORA])
        self.w_ukv = di("w_ukv", [L, KV_LORA, 2048])
        self.w_o = di("w_o", [L, D_MODEL, D_MODEL])
        self.ln1_g = di("ln1_g", [L, D_MODEL])
        self.ln1_b = di("ln1_b", [L, D_MODEL])
        self.w_up = di("w_up", [L, D_MODEL, 2 * D_FF])
        self.conv_w = di("conv_w", [L, 3, 2 * D_FF])
        self.conv_b = di("conv_b", [L, 2 * D_FF])
        self.w_down = di("w_down", [L, D_FF, D_MODEL])
        self.ln2_g = di("ln2_g", [L, D_MODEL])
        self.ln2_b = di("ln2_b", [L, D_MODEL])
        maxS = max(seq_lens) if seq_lens else T
        self.maxS = maxS
        if shard:
            self.cosS = di("cosS", [128, T])
            self.sinS = di("sinS", [128, T])
            self.masksI = di("masks", [128, 26])
            dsc0 = lambda name, shape, dt: nc.dram_tensor(name, shape, dt, kind="Internal").ap()
            self.GIN = dsc0("GIN", [8 * LROWS, T], BF16)
            self.GOUT = dsc0("GOUT", [8 * LROWS, T], BF16)
            self.HBIN = dsc0("HBIN", [8 * 128, 32], F32)
            self.HBOUT = dsc0("HBOUT", [8 * 128, 32], F32)
            self.EKs = dsc0("EKs", [8, 8, 128, T], BF16)
            self.EVs = dsc0("EVs", [8, T, 1024], BF16)
            self.rGIN, self.rGOUT, self.rHBIN, self.rHBOUT, self.rEs = Reg(), Reg(), Reg(), Reg(), Reg()
        self.cosT = di("cosT", [128, maxS])
        self.sinT = di("sinT", [128, maxS])
        self.wtab = di("wtab", [8, 128, WTW])
        self.identI = di("ident", [128, 128])
        self.yout = nc.dram_tensor("yout", [self.ntok, D_MODEL], F32, kind="ExternalOutput").ap()
        dsc = lambda name, shape, dt: nc.dram_tensor(name, shape, dt, kind="Internal").ap()
        self.XT = dsc("XT", [NU, KC, 128, T], F32)
        self.X1 = dsc("X1", [NU, KC, 128, T], F32)
        self.Lb = dsc("Lb", [NU, LROWS, T], BF16)
        self.QA = dsc("QA", [NU, 8, 128, T], BF16)
        self.QN = dsc("QN", [NU, 8, 128, T], BF16)
        self.QR = dsc("QR", [NU, 4, 128, T], BF16)
        self.EK = dsc("EK", [NU, 8, 128, T], BF16)
        self.EV = dsc("EV", [NU, T, 1024], BF16)
        self.WT = dsc("WT", [8, 128, WTW], BF16)
        R = lambda: Reg()
        self.rXT = [R() for _ in range(NU)]
        self.rX1 = [R() for _ in range(NU)]
        self.rL = [R() for _ in range(NU)]
        self.rQ = [R() for _ in range(NU)]
        self.rE = [R() for _ in range(NU)]
        self.rWT = R()
        self.rOut = R()
        self.banks = [Buf(nc.alloc_psum_tensor(f"bank{i}", [128, 512], F32), Reg()) for i in range(8)]
        self.bank_rr = 0

    def _arena_init(self):
        nc = self.nc
        free = nc.sbuf_bytes_remaining
        free = free() if callable(free) else free
        size = (int(free) - 64) // 32 * 32
        beg, end = nc.bump_sbuf(size)
        self.ar_beg, self.ar_end = int(beg), int(end)
        self.ar_ptr = self.ar_beg
        self.ar_occ = []

    def scope(self):
        b = self

        class _Scope:
            def __enter__(self_):
                if not hasattr(b, "ar_ptr"):
                    b._arena_init()
                self_.mark = b.ar_ptr
                return self_

            def __exit__(self_, *exc):
                b.ar_ptr = self_.mark
                return False
        return _Scope()

    def alloc(self, st, shape, dt):
        self.uid += 1
        esz = 4 if dt == F32 else 2
        nbytes = esz
        for d in shape[1:]:
            nbytes *= d
        start = (self.ar_ptr + 31) // 32 * 32
        stop = start + nbytes
        assert stop <= self.ar_end, f"arena overflow: need {stop - self.ar_beg} have {self.ar_end - self.ar_beg}"
        self.ar_ptr = stop
        r = Reg()
        keep = []
        for (s0, e0, reg) in self.ar_occ:
            if e0 <= start or s0 >= stop:
                keep.append((s0, e0, reg))
                continue
            evs = list(reg.readers.items())
            if reg.writer is not None:
                evs.append(reg.writer)
            for k, v in evs:
                if r.readers.get(k, 0) < v:
                    r.readers[k] = v
            if not (s0 >= start and e0 <= stop):
                keep.append((s0, e0, reg))
        keep.append((start, stop, r))
        self.ar_occ = keep
        t = self.nc.alloc_sbuf_tensor_at(f"t{self.uid}", shape, dt, offset=start)
        return Buf(t, r)

    def palloc(self, shape, dt):
        self.uid += 1
        return Buf(self.nc.alloc_sbuf_tensor(f"p{self.uid}", shape, dt), Reg())

    def mm(self, bank, out_ap, lhsT, rhs, start, stop, reads, signal=None):
        nc = self.nc
        self.fw.op(self.fw.pe, lambda: nc.tensor.matmul(out_ap, lhsT=lhsT, rhs=rhs, start=start, stop=stop),
                   reads=reads, writes=[bank.r], signal=(stop if signal is None else signal))

    def act(self, fn, reads, writes):
        self.fw.op(self.fw.act, fn, reads=reads, writes=writes)

    def dve(self, fn, reads, writes):
        self.fw.op(self.fw.dve, fn, reads=reads, writes=writes)

    def setup_consts(self):
        nc, fw = self.nc, self.fw
        L = self.n_layers
        self.ident_f = self.palloc([128, 128], F32)
        self.ident_b = self.palloc([128, 128], BF16)
        self.ones_b = self.palloc([128, 128], BF16)
        self.ones_f = self.palloc([128, 128], F32)
        self.lnp = self.palloc([128, L, 4, KC], F32)
        self.cw = self.palloc([128, L, 3, 88], F32)
        self.cb = self.palloc([128, L, 88], F32)
        self.qg = self.palloc([128, L, 4], F32)
        self.kvg = self.palloc([128, L, 2], F32)
        self.es = self.palloc([128, L, 8], F32)
        self.eps_ln = self.palloc([128, 1], F32)
        self.eps_rms = self.palloc([128, 1], F32)
        fw.dma(fw.sp, self.ident_f.t[:, :], self.identI, writes=[self.ident_f.r])
        if self.shard:
            self.masks = self.palloc([128, 26], F32)
            fw.dma(fw.sp, self.masks.t[:, :], self.masksI, writes=[self.masks.r])
        self.act(lambda: nc.scalar.copy(out=self.ident_b.t[:, :], in_=self.ident_f.t[:, :]),
                 [self.ident_f.r], [self.ident_b.r])
        self.dve(lambda: nc.vector.memset(self.ones_b.t[:, :], 1.0), [], [self.ones_b.r])
        self.dve(lambda: nc.vector.memset(self.ones_f.t[:, :], 1.0), [], [self.ones_f.r])
        self.dve(lambda: nc.vector.memset(self.eps_ln.t[:, :], LN_EPS), [], [self.eps_ln.r])
        self.dve(lambda: nc.vector.memset(self.eps_rms.t[:, :], RMS_EPS), [], [self.eps_rms.r])
        for l in range(L):
            for i, src in enumerate((self.ln1_g, self.ln1_b, self.ln2_g, self.ln2_b)):
                fw.dma(fw.sp, self.lnp.t[:, l, i, :], src[l].rearrange("(c p) -> p c", p=128),
                       writes=[self.lnp.r], allow_slow_non_contiguous=True)
            for k in range(3):
                fw.dma(fw.sp, self.cw.t[:, l, k, :], self.conv_w[l, k].rearrange("(c p) -> p c", p=128),
                       writes=[self.cw.r], allow_slow_non_contiguous=True)
            fw.dma(fw.sp, self.cb.t[:, l, :], self.conv_b[l].rearrange("(c p) -> p c", p=128),
                   writes=[self.cb.r], allow_slow_non_contiguous=True)
            fw.dma(fw.sp, self.qg.t[:, l, :], self.q_norm_g[l].rearrange("(c p) -> p c", p=128),
                   writes=[self.qg.r], allow_slow_non_contiguous=True)
            fw.dma(fw.sp, self.kvg.t[:, l, :], self.kv_norm_g[l].rearrange("(c p) -> p c", p=128),
                   writes=[self.kvg.r], allow_slow_non_contiguous=True)
            fw.dma(fw.sp, self.es.t[:, l, :], self.sink[l].partition_broadcast(128), writes=[self.es.r])
        self.act(lambda: nc.scalar.activation(out=self.es.t[:, :, :], in_=self.es.t[:, :, :], func=AF.Exp),
                 [self.es.r], [self.es.r])
        with self.scope() as st:
            for h in range(8):
                tf = self.alloc(st, [128, WTW], F32)
                tb = self.alloc(st, [128, WTW], BF16)
                fw.dma(fw.sp, tf.t[:, :], self.wtab[h], writes=[tf.r])
                self.act(lambda: nc.scalar.mul(out=tb.t[:, :], in_=tf.t[:, :], mul=float(1.0 / SCALE_A)),
                         [tf.r], [tb.r])
                fw.dma(fw.act, self.WT[h], tb.t[:, :], reads=[tb.r], writes=[self.rWT])

    def phase0(self, un):
        nc, fw = self.nc, self.fw
        u = un["u"]
        with self.scope() as st:
            stage = self.alloc(st, [128, KC, T], F32)
            xbs = [self.alloc(st, [128, D_MODEL], F32) for _ in range(2)]
            n = 0
            for tb in range(8):
                xb = xbs[tb % 2]
                fw.dma(fw.sp, xb.t[:, :], self.xin[un["tok0"] + tb * 128: un["tok0"] + (tb + 1) * 128, :],
                       writes=[xb.r])
                for g in range(4):
                    bank = self.banks[n % 8]
                    n += 1
                    for j in range(4):
                        c = g * 4 + j
                        fw.op(fw.pe, lambda: nc.tensor.transpose(bank.t[:, j * 128:(j + 1) * 128],
                                                                 xb.t[:, c * 128:(c + 1) * 128],
                                                                 self.ident_f.t[:, :]),
                              reads=[xb.r, self.ident_f.r], writes=[bank.r], signal=(j == 3))
                    src = bank.t[:, :].rearrange("p (j t) -> p j t", j=4)
                    dst = stage.t[:, g * 4:(g + 1) * 4, tb * 128:(tb + 1) * 128]
                    if n % 2:
                        self.act(lambda: nc.scalar.copy(out=dst, in_=src), [bank.r], [stage.r])
                    else:
                        self.dve(lambda: nc.vector.tensor_copy(out=dst, in_=src), [bank.r], [stage.r])
            for c0 in range(0, KC, 4):
                fw.dma(fw.sp, self.XT[u, c0:c0 + 4].rearrange("c p t -> p c t"), stage.t[:, c0:c0 + 4, :],
                       reads=[stage.r], writes=[self.rXT[u]])

    def fm_chunk(self, lhs_fn, rhs_fn, nk, reads, M=128, ntiles=2, n=TT):
        banks = []
        for tt in range(ntiles):
            bank = self.banks[self.bank_rr % 8]
            self.bank_rr += 1
            for kc in range(nk):
                self.mm(bank, bank.t[0:M, 0:n], lhs_fn(kc), rhs_fn(kc, tt), kc == 0, kc == nk - 1,
                        reads(kc) if callable(reads) else reads)
            banks.append(bank)
        return banks

    def next_bank(self):
        bank = self.banks[self.bank_rr % 8]
        self.bank_rr += 1
        return bank

    def rms_stats(self, sq, nchunk, rs, eps_buf, inv_n):
        nc = self.nc
        for tt in range(2):
            bank = self.next_bank()
            for c in range(nchunk):
                self.mm(bank, bank.t[:, :], self.ones_b.t[:, :], sq.t[:, c, tt * TT:(tt + 1) * TT],
                        c == 0, c == nchunk - 1, [self.ones_b.r, sq.r])
            self.act(lambda: nc.scalar.activation(out=rs.t[:, tt * TT:(tt + 1) * TT], in_=bank.t[:, :],
                                                  func=AF.Sqrt, bias=eps_buf.t[:, 0:1], scale=inv_n),
                     [bank.r, eps_buf.r], [rs.r])
        self.dve(lambda: nc.vector.reciprocal(out=rs.t[:, :], in_=rs.t[:, :]), [rs.r], [rs.r])

    def phaseA(self, un, l):
        nc, fw = self.nc, self.fw
        u = un["u"]
        with self.scope() as st:
            xbq = [self.alloc(st, [128, 4, T], BF16) for _ in range(4)]
            for q in range(4):
                fw.dma(fw.pool, xbq[q].t[:, :, :], self.XT[u, 4 * q:4 * q + 4].rearrange("c p t -> p c t"),
                       reads=[self.rXT[u]], writes=[xbq[q].r])

            xsl = lambda kc, sl: xbq[kc // 4].t[:, kc % 4, sl]
            wgs = [self.alloc(st, [128, KC, 256], BF16) for _ in range(3)]
            stages = [self.alloc(st, [128, T], BF16) for _ in range(4)]
            cq = self.alloc(st, [128, 4, T], F32)
            sq = self.alloc(st, [128, 4, T], BF16)
            cqn = self.alloc(st, [128, 4, T], BF16)
            ckvn = self.alloc(st, [128, 2, T], BF16)
            rs = self.alloc(st, [128, T], F32)
            cs = self.alloc(st, [128, T], F32)
            sn = self.alloc(st, [128, T], F32)
            tmp1 = self.alloc(st, [128, TT], F32)
            tmp2 = self.alloc(st, [128, TT], F32)
            vst = self.alloc(st, [128, 8, 256], BF16)
            if un.get("shard"):
                fw.dma(fw.sp, cs.t[:, :], self.cosS, writes=[cs.r])
                fw.dma(fw.sp, sn.t[:, :], self.sinS, writes=[sn.r])
            else:
                fw.dma(fw.sp, cs.t[:, :], self.cosT[:, un["pos0"]:un["pos0"] + T], writes=[cs.r])
                fw.dma(fw.sp, sn.t[:, :], self.sinT[:, un["pos0"]:un["pos0"] + T], writes=[sn.r])
            w_in = self.w_in[l]
            wi = [0]
            sti = [0]

            def load_wg(c0, ncols):
                wg = wgs[wi[0] % 3]
                wi[0] += 1
                fw.dma(fw.pool, wg.t[:, :, 0:ncols], w_in[:, c0:c0 + ncols].rearrange("(c p) n -> p c n", p=128),
                       writes=[wg.r])
                return wg

            def evac_store(banks, dst_ap, dst_reg, M=128):
                stg = stages[sti[0] % 4]
                sti[0] += 1
                for tt, bank in enumerate(banks):
                    self.act(lambda: nc.scalar.copy(out=stg.t[0:M, tt * TT:(tt + 1) * TT], in_=bank.t[0:M, :]),
                             [bank.r], [stg.r])
                fw.dma(fw.act, dst_ap, stg.t[0:M, :], reads=[stg.r], writes=[dst_reg])

            for g in range(5):
                wg = load_wg(g * 256, 256)
                for j in range(2):
                    banks = self.fm_chunk(lambda kc: wg.t[:, kc, j * 128:(j + 1) * 128],
                                          lambda kc, tt: xsl(kc, slice(tt * TT, (tt + 1) * TT)), KC,
                                          lambda kc: [wg.r, xbq[kc // 4].r])
                    if g < 4:
                        evac_store(banks, self.QA[u, 2 * g + j], self.rQ[u])
                    else:
                        evac_store(banks, self.Lb[u, j * 128:(j + 1) * 128, :], self.rL[u])
            wg = load_wg(1280, 256)
            for tb in range(8):
                bank = self.next_bank()
                for kc in range(KC):
                    self.mm(bank, bank.t[:, 0:256], xsl(kc, slice(tb * 128, (tb + 1) * 128)), wg.t[:, kc, 0:256],
                            kc == 0, kc == KC - 1, [wg.r, xbq[kc // 4].r])
                self.act(lambda: nc.scalar.copy(out=vst.t[:, tb, :], in_=bank.t[:, 0:256]), [bank.r], [vst.r])
            va_view = self.Lb[u, 576:832, :].rearrange("r (a d) -> (r a) d", d=256)
            fw.dma(fw.act, va_view.rearrange("(tb p) d -> p tb d", p=128), vst.t[:, :, :],
                   reads=[vst.r], writes=[self.rL[u]])
            for g in range(2):
                wg = load_wg(1536 + g * 256, 256)
                for j in range(2):
                    c = 2 * g + j
                    banks = self.fm_chunk(lambda kc: wg.t[:, kc, j * 128:(j + 1) * 128],
                                          lambda kc, tt: xsl(kc, slice(tt * TT, (tt + 1) * TT)), KC,
                                          lambda kc: [wg.r, xbq[kc // 4].r])
                    for tt, bank in enumerate(banks):
                        self.act(lambda: nc.scalar.copy(out=cq.t[:, c, tt * TT:(tt + 1) * TT], in_=bank.t[:, :]),
                                 [bank.r], [cq.r])
                        self.act(lambda: nc.scalar.activation(out=sq.t[:, c, tt * TT:(tt + 1) * TT],
                                                              in_=bank.t[:, :], func=AF.Square),
                                 [bank.r], [sq.r])
            self.rms_stats(sq, 4, rs, self.eps_rms, 1.0 / Q_LORA)
            for c in range(4):
                self.dve(lambda: nc.vector.scalar_tensor_tensor(out=cqn.t[:, c, :], in0=cq.t[:, c, :],
                                                                scalar=self.qg.t[:, l, c:c + 1], in1=rs.t[:, :],
                                                                op0=ALU.mult, op1=ALU.mult),
                         [cq.r, rs.r, self.qg.r], [cqn.r])
            wg = load_wg(2048, 256)
            for c in range(2):
                banks = self.fm_chunk(lambda kc: wg.t[:, kc, c * 128:(c + 1) * 128],
                                      lambda kc, tt: xsl(kc, slice(tt * TT, (tt + 1) * TT)), KC,
                                          lambda kc: [wg.r, xbq[kc // 4].r])
                for tt, bank in enumerate(banks):
                    self.act(lambda: nc.scalar.copy(out=cq.t[:, c, tt * TT:(tt + 1) * TT], in_=bank.t[:, :]),
                             [bank.r], [cq.r])
                    self.act(lambda: nc.scalar.activation(out=sq.t[:, c, tt * TT:(tt + 1) * TT],
                                                          in_=bank.t[:, :], func=AF.Square),
                             [bank.r], [sq.r])
            self.rms_stats(sq, 2, rs, self.eps_rms, 1.0 / KV_LORA)
            for c in range(2):
                self.dve(lambda: nc.vector.scalar_tensor_tensor(out=ckvn.t[:, c, :], in0=cq.t[:, c, :],
                                                                scalar=self.kvg.t[:, l, c:c + 1], in1=rs.t[:, :],
                                                                op0=ALU.mult, op1=ALU.mult),
                         [cq.r, rs.r, self.kvg.r], [ckvn.r])
                fw.dma(fw.sp, self.Lb[u, 256 + c * 128:256 + (c + 1) * 128, :], ckvn.t[:, c, :],
                       reads=[ckvn.r], writes=[self.rL[u]])
            wg = wgs[wi[0] % 3]
            wi[0] += 1
            src = lambda a, b: w_in[:, a:b].rearrange("(c p) n -> p c n", p=128)
            fw.dma(fw.pool, wg.t[:, :, 0:64], src(2304, 2368), writes=[wg.r])
            fw.dma(fw.pool, wg.t[:, :, 64:96], src(2336, 2368), writes=[wg.r])
            fw.dma(fw.pool, wg.t[:, :, 96:128], src(2304, 2336), writes=[wg.r])
            bA = self.fm_chunk(lambda kc: wg.t[:, kc, 0:64], lambda kc, tt: xsl(kc, slice(tt * TT, (tt + 1) * TT)),
                               KC, lambda kc: [wg.r, xbq[kc // 4].r], M=64)
            bB = self.fm_chunk(lambda kc: wg.t[:, kc, 64:128], lambda kc, tt: xsl(kc, slice(tt * TT, (tt + 1) * TT)),
                               KC, lambda kc: [wg.r, xbq[kc // 4].r], M=64)
            stg = stages[sti[0] % 4]
            sti[0] += 1
            for tt in range(2):
                self.rope(bA[tt], bB[tt], cs, sn, tmp1, tmp2, stg, tt, 64)
            fw.dma(fw.sp, self.Lb[u, 512:576, :], stg.t[0:64, :], reads=[stg.r], writes=[self.rL[u]])

            wqn = self.alloc(st, [128, 4, 8, 128], BF16)
            wqr = self.alloc(st, [128, 4, 8, 64], BF16)
            wqs = self.alloc(st, [128, 4, 8, 64], BF16)
            wq = self.w_uq[l].rearrange("(c p) (h d) -> c p h d", p=128, d=192)
            for kc in range(4):
                fw.dma(fw.pool, wqn.t[:, kc, :, :], wq[kc, :, :, 0:128], writes=[wqn.r])
                fw.dma(fw.pool, wqr.t[:, kc, :, :], wq[kc, :, :, 128:192], writes=[wqr.r])
                fw.dma(fw.pool, wqs.t[:, kc, :, 0:32], wq[kc, :, :, 160:192], writes=[wqs.r])
                fw.dma(fw.pool, wqs.t[:, kc, :, 32:64], wq[kc, :, :, 128:160], writes=[wqs.r])
            for h in range(8):
                banks = self.fm_chunk(lambda kc: wqn.t[:, kc, h, :],
                                      lambda kc, tt: cqn.t[:, kc, tt * TT:(tt + 1) * TT], 4, [wqn.r, cqn.r])
                evac_store(banks, self.QN[u, h], self.rQ[u])
            for j in range(4):
                bA = self.fm_chunk(lambda kc: wqr.t[:, kc, 2 * j:2 * j + 2, :].rearrange("p h d -> p (h d)"),
                                   lambda kc, tt: cqn.t[:, kc, tt * TT:(tt + 1) * TT], 4, [wqr.r, cqn.r])
                bB = self.fm_chunk(lambda kc: wqs.t[:, kc, 2 * j:2 * j + 2, :].rearrange("p h d -> p (h d)"),
                                   lambda kc, tt: cqn.t[:, kc, tt * TT:(tt + 1) * TT], 4, [wqs.r, cqn.r])
                stg = stages[sti[0] % 4]
                sti[0] += 1
                for tt in range(2):
                    self.rope(bA[tt], bB[tt], cs, sn, tmp1, tmp2, stg, tt, 128)
                fw.dma(fw.sp, self.QR[u, j], stg.t[:, :], reads=[stg.r], writes=[self.rQ[u]])

            if un.get("shard"):
                return
            wkn = self.alloc(st, [128, 2, 8, 128], BF16)
            wkv = self.alloc(st, [128, 2, 8, 128], BF16)
            wk = self.w_ukv[l].rearrange("(c p) (h d) -> c p h d", p=128, d=256)
            for kc in range(2):
                fw.dma(fw.pool, wkn.t[:, kc, :, :], wk[kc, :, :, 0:128], writes=[wkn.r])
                fw.dma(fw.pool, wkv.t[:, kc, :, :], wk[kc, :, :, 128:256], writes=[wkv.r])
            for h in range(8):
                banks = self.fm_chunk(lambda kc: wkn.t[:, kc, h, :],
                                      lambda kc, tt: ckvn.t[:, kc, tt * TT:(tt + 1) * TT], 2, [wkn.r, ckvn.r])
                evac_store(banks, self.EK[u, h], self.rE[u])
            for tb in range(8):
                stg = stages[sti[0] % 4]
                sti[0] += 1
                for half in range(2):
                    bank = self.next_bank()
                    for kc in range(2):
                        self.mm(bank, bank.t[:, :], ckvn.t[:, kc, tb * 128:(tb + 1) * 128],
                                wkv.t[:, kc, half * 4:(half + 1) * 4, :].rearrange("p h d -> p (h d)"),
                                kc == 0, kc == 1, [wkv.r, ckvn.r])
                    self.act(lambda: nc.scalar.copy(out=stg.t[:, half * TT:(half + 1) * TT], in_=bank.t[:, :]),
                             [bank.r], [stg.r])
                fw.dma(fw.act, self.EV[u, tb * 128:(tb + 1) * 128, :], stg.t[:, :], reads=[stg.r],
                       writes=[self.rE[u]])

    def publish_L(self, un):
        nc, fw = self.nc, self.fw
        u = un["u"]
        with self.scope() as st:
            pubs = [self.alloc(st, [128, T], BF16) for _ in range(2)]
            pms = [self.alloc(st, [128, T], BF16) for _ in range(3)]
            n = 0
            for bi, r0 in enumerate(range(0, LROWS, 128)):
                nr = min(128, LROWS - r0)
                pub = pubs[bi % 2]
                fw.dma(fw.sp, pub.t[0:nr, :], self.Lb[u, r0:r0 + nr, :], reads=[self.rL[u]], writes=[pub.r])
                for sl in range(8):
                    pm = pms[n % 3]
                    n += 1
                    self.act(lambda: nc.scalar.activation(out=pm.t[0:nr, :], in_=pub.t[0:nr, :], func=AF.Identity,
                                                          scale=self.masks.t[0:nr, sl:sl + 1]),
                             [pub.r, self.masks.r], [pm.r])
                    fw.dma(fw.act, self.GIN[sl * LROWS + r0: sl * LROWS + r0 + nr, :], pm.t[0:nr, :],
                           reads=[pm.r], writes=[self.rGIN])
        fw.allreduce(self.GOUT, self.GIN, reads=[self.rGIN], writes=[self.rGOUT])

    def publish_halo(self, st, y):
        nc, fw = self.nc, self.fw
        hb = self.alloc(st, [128, KC, 2], F32)
        self.act(lambda: nc.scalar.copy(out=hb.t[:, :, 0:1], in_=y.t[:, :, 0:1]), [y.r], [hb.r])
        self.act(lambda: nc.scalar.copy(out=hb.t[:, :, 1:2], in_=y.t[:, :, T - 1:T]), [y.r], [hb.r])
        hms = [self.alloc(st, [128, 2 * KC], F32) for _ in range(2)]
        for sl in range(8):
            hm = hms[sl % 2]
            self.act(lambda: nc.scalar.activation(out=hm.t[:, :], in_=hb.t[:, :, :].rearrange("p c e -> p (c e)"),
                                                  func=AF.Identity, scale=self.masks.t[:, sl:sl + 1]),
                     [hb.r, self.masks.r], [hm.r])
            fw.dma(fw.act, self.HBIN[sl * 128:(sl + 1) * 128, :], hm.t[:, :], reads=[hm.r], writes=[self.rHBIN])
        fw.allreduce(self.HBOUT, self.HBIN, reads=[self.rHBIN], writes=[self.rHBOUT])

    def masked_select(self, acc, width, sel, mcol0):
        nc = self.nc
        for sl in range(8):
            if sl == 0:
                self.dve(lambda: nc.vector.tensor_scalar(out=acc.t[:, 0:width], in0=sel.t[:, 0, 0:width],
                                                         scalar1=self.masks.t[:, mcol0:mcol0 + 1], scalar2=None,
                                                         op0=ALU.mult), [sel.r, self.masks.r], [acc.r])
            else:
                self.dve(lambda: nc.vector.scalar_tensor_tensor(
                    out=acc.t[:, 0:width], in0=sel.t[:, sl, 0:width],
                    scalar=self.masks.t[:, mcol0 + sl:mcol0 + sl + 1], in1=acc.t[:, 0:width],
                    op0=ALU.mult, op1=ALU.add), [sel.r, self.masks.r, acc.r], [acc.r])

    def expand_slots(self, l):
        nc, fw = self.nc, self.fw
        with self.scope() as st:
            wkn = self.alloc(st, [128, 2, 8, 128], BF16)
            wkv = self.alloc(st, [128, 2, 8, 128], BF16)
            wk = self.w_ukv[l].rearrange("(c p) (h d) -> c p h d", p=128, d=256)
            for kc in range(2):
                fw.dma(fw.pool, wkn.t[:, kc, :, :], wk[kc, :, :, 0:128], writes=[wkn.r])
                fw.dma(fw.pool, wkv.t[:, kc, :, :], wk[kc, :, :, 128:256], writes=[wkv.r])
            cks = [self.alloc(st, [128, 2, T], BF16) for _ in range(2)]
            stages = [self.alloc(st, [128, T], BF16) for _ in range(4)]
            sti = 0
            for sl in range(8):
                ckvn = cks[sl % 2]
                r0 = sl * LROWS + 256
                fw.dma(fw.sp, ckvn.t[:, :, :], self.GOUT[r0:r0 + 256, :].rearrange("(c p) t -> p c t", p=128),
                       reads=[self.rGOUT], writes=[ckvn.r])
                for h in range(8):
                    banks = self.fm_chunk(lambda kc: wkn.t[:, kc, h, :],
                                          lambda kc, tt: ckvn.t[:, kc, tt * TT:(tt + 1) * TT], 2, [wkn.r, ckvn.r])
                    stg = stages[sti % 4]
                    sti += 1
                    for tt, bank in enumerate(banks):
                        self.act(lambda: nc.scalar.copy(out=stg.t[:, tt * TT:(tt + 1) * TT], in_=bank.t[:, :]),
                                 [bank.r], [stg.r])
                    fw.dma(fw.act, self.EKs[sl, h], stg.t[:, :], reads=[stg.r], writes=[self.rEs])
                for tb in range(8):
                    stg = stages[sti % 4]
                    sti += 1
                    for half in range(2):
                        bank = self.next_bank()
                        for kc in range(2):
                            self.mm(bank, bank.t[:, :], ckvn.t[:, kc, tb * 128:(tb + 1) * 128],
                                    wkv.t[:, kc, half * 4:(half + 1) * 4, :].rearrange("p h d -> p (h d)"),
                                    kc == 0, kc == 1, [wkv.r, ckvn.r])
                        self.act(lambda: nc.scalar.copy(out=stg.t[:, half * TT:(half + 1) * TT], in_=bank.t[:, :]),
                                 [bank.r], [stg.r])
                    fw.dma(fw.act, self.EVs[sl, tb * 128:(tb + 1) * 128, :], stg.t[:, :], reads=[stg.r],
                           writes=[self.rEs])

    def rope(self, bankA, bankB, cs, sn, tmp1, tmp2, stg, tt, M):
        nc = self.nc
        sl = slice(tt * TT, (tt + 1) * TT)
        self.dve(lambda: nc.vector.tensor_tensor(out=tmp1.t[0:M, :], in0=bankA.t[0:M, :], in1=cs.t[0:M, sl],
                                                 op=ALU.mult), [bankA.r, cs.r], [tmp1.r])
        self.dve(lambda: nc.vector.tensor_tensor(out=tmp2.t[0:M, :], in0=bankB.t[0:M, :], in1=sn.t[0:M, sl],
                                                 op=ALU.mult), [bankB.r, sn.r], [tmp2.r])
        self.dve(lambda: nc.vector.tensor_tensor(out=stg.t[0:M, sl], in0=tmp1.t[0:M, :], in1=tmp2.t[0:M, :],
                                                 op=ALU.add), [tmp1.r, tmp2.r], [stg.r])

    def attn_begin(self, scale, pts, accO, accL):
        self.ap_scale = scale
        self.ap_pts = pts
        self.ap_accO = accO
        self.ap_accL = accL
        self.ap_q = []
        self.ap_i = 0

    def attn_push(self, tl):
        nc = self.nc
        bank = self.sb[self.sb_rr % len(self.sb)]
        self.sb_rr += 1
        nm = len(tl["s_mms"])
        for mi, (lhsT, rhs, reads) in enumerate(tl["s_mms"]):
            self.mm(bank, bank.t[:, :], lhsT, rhs, mi == 0, mi == nm - 1, reads)
        pt = self.ap_pts[self.ap_i % len(self.ap_pts)]
        self.ap_i += 1
        scale = self.ap_scale
        if tl.get("bias") is not None:
            self.act(lambda: nc.scalar.activation(out=pt.t[:, :], in_=bank.t[:, :], func=AF.Exp, scale=scale,
                                                  bias=tl["bias"]), [bank.r, self.masks.r], [pt.r])
        else:
            self.act(lambda: nc.scalar.activation(out=pt.t[:, :], in_=bank.t[:, :], func=AF.Exp, scale=scale),
                     [bank.r], [pt.r])
        self.ap_q.append((tl, pt))
        if len(self.ap_q) > 2:
            self._attn_pv(*self.ap_q.pop(0))

    def attn_flush(self):
        while self.ap_q:
            self._attn_pv(*self.ap_q.pop(0))

    def _attn_pv(self, tl, pt):
        qt = tl["qt"]
        accO, accL = self.ap_accO, self.ap_accL
        self.mm(accO[qt], accO[qt].t[:, :], tl["v_lhsT"], pt.t[:, :], tl["first"], tl["last"],
                tl["v_reads"] + [pt.r])
        self.mm(accL[qt], accL[qt].t[:, :], self.ones_b.t[:, :], pt.t[:, :], tl["first"], tl["last"],
                [self.ones_b.r, pt.r])

    def attn_epilogue(self, attn, idx, accO, accL, eset, sink_ap=None, sink_reg=None):
        nc = self.nc
        for qt in range(2):
            so, sl = eset[qt]
            self.act(lambda: nc.scalar.copy(out=so.t[:, :], in_=accO[qt].t[:, :]), [accO[qt].r], [so.r])
            if sink_ap is not None:
                self.act(lambda: nc.scalar.activation(out=sl.t[:, :], in_=accL[qt].t[:, :], func=AF.Identity,
                                                      bias=sink_ap, scale=1.0),
                         [accL[qt].r, sink_reg], [sl.r])
            else:
                self.act(lambda: nc.scalar.copy(out=sl.t[:, :], in_=accL[qt].t[:, :]), [accL[qt].r], [sl.r])
        for qt in range(2):
            so, sl = eset[qt]
            self.dve(lambda: nc.vector.reciprocal(out=sl.t[:, :], in_=sl.t[:, :]), [sl.r], [sl.r])
            self.dve(lambda: nc.vector.tensor_tensor(out=attn.t[:, idx, qt * TT:(qt + 1) * TT],
                                                     in0=so.t[:, :], in1=sl.t[:, :], op=ALU.mult),
                     [so.r, sl.r], [attn.r])

    def phaseBC(self, un, l):
        nc, fw = self.nc, self.fw
        u = un["u"]
        ctx = un["ctx"]
        j, nj = un["j"], un["nj"]
        prev_u = u - 1 if j > 0 else None
        next_u = u + 1 if j < nj - 1 else None
        shard = bool(un.get("shard"))
        if shard:
            G3 = self.GOUT.rearrange("(s r) t -> s r t", s=8)
            kr_src = lambda c: (G3[c, 512:576, :], self.rGOUT)
            ek_src = lambda c, h: (self.EKs[c, h], self.rEs)
            ev_src = lambda c, h: (self.EVs[c, :, h * 128:(h + 1) * 128], self.rEs)
        else:
            kr_src = lambda c: (self.Lb[c, 512:576, :], self.rL[c])
            ek_src = lambda c, h: (self.EK[c, h], self.rE[c])
            ev_src = lambda c, h: (self.EV[c, :, h * 128:(h + 1) * 128], self.rE[c])
        with self.scope() as st_outer:
            attn = self.alloc(st_outer, [128, 16, T], BF16)
            with self.scope() as st:
                nctx = len(ctx)
                kr = self.alloc(st, [64, nctx, T], BF16)
                for ci, c in enumerate(ctx):
                    kap, kreg = kr_src(c)
                    fw.dma(fw.sp, kr.t[:, ci, :], kap, reads=[kreg], writes=[kr.r])
                kax = self.alloc(st, [128, 2, 10, 128], BF16)
                vax = self.alloc(st, [128, 10, 256], BF16)

                def va_view(c):
                    return self.Lb[c, 576:832, :].rearrange("r (a d) -> (r a) d", d=256)
                for kv in range(2):
                    fw.dma(fw.sp, kax.t[:, kv, 1:9, :].rearrange("p b k -> p (b k)"),
                           self.Lb[u, kv * 128:(kv + 1) * 128, :], reads=[self.rL[u]], writes=[kax.r])
                    if prev_u is not None:
                        fw.dma(fw.sp, kax.t[:, kv, 0, :], self.Lb[prev_u, kv * 128:(kv + 1) * 128, 896:1024],
                               reads=[self.rL[prev_u]], writes=[kax.r])
                    if next_u is not None:
                        fw.dma(fw.sp, kax.t[:, kv, 9, :], self.Lb[next_u, kv * 128:(kv + 1) * 128, 0:128],
                               reads=[self.rL[next_u]], writes=[kax.r])
                fw.dma(fw.sp, vax.t[:, 1:9, :], va_view(u).rearrange("(tb p) d -> p tb d", p=128),
                       reads=[self.rL[u]], writes=[vax.r])
                if prev_u is not None:
                    fw.dma(fw.sp, vax.t[:, 0, :], va_view(prev_u)[896:1024, :], reads=[self.rL[prev_u]],
                           writes=[vax.r])
                if next_u is not None:
                    fw.dma(fw.sp, vax.t[:, 9, :], va_view(next_u)[0:128, :], reads=[self.rL[next_u]],
                           writes=[vax.r])
                if shard:
                    selk = self.alloc(st, [128, 8, 128], BF16)
                    selv = self.alloc(st, [128, 8, 256], BF16)
                    accf = self.alloc(st, [128, 256], F32)
                    VAv = G3[:, 576:832, :].rearrange("s r (a d) -> s (r a) d", d=256)
                    for blk, lo, mcol0 in ((0, 896, 8), (9, 0, 16)):
                        for kv in range(2):
                            fw.dma(fw.sp, selk.t[:, :, :],
                                   G3[:, kv * 128:(kv + 1) * 128, lo:lo + 128].rearrange("s p k -> p s k"),
                                   reads=[self.rGOUT], writes=[selk.r])
                            self.masked_select(accf, 128, selk, mcol0)
                            self.act(lambda: nc.scalar.copy(out=kax.t[:, kv, blk, :], in_=accf.t[:, 0:128]),
                                     [accf.r], [kax.r])
                        fw.dma(fw.sp, selv.t[:, :, :], VAv[:, lo:lo + 128, :].rearrange("s p d -> p s d"),
                               reads=[self.rGOUT], writes=[selv.r])
                        self.masked_select(accf, 256, selv, mcol0)
                        self.act(lambda: nc.scalar.copy(out=vax.t[:, blk, :], in_=accf.t[:, 0:256]),
                                 [accf.r], [vax.r])
                pts = [self.alloc(st, [128, TT], BF16) for _ in range(4)]
                esets = [[(self.alloc(st, [128, TT], F32), self.alloc(st, [128, TT], F32)) for _ in range(2)]
                         for _ in range(2)]
                qas = [self.alloc(st, [128, T], BF16) for _ in range(2)]
                wts = [self.alloc(st, [128, WTW], BF16) for _ in range(2)]
                qns = [self.alloc(st, [128, T], BF16) for _ in range(2)]
                qrs = [self.alloc(st, [64, T], BF16) for _ in range(2)]
                eks = [self.alloc(st, [128, T], BF16) for _ in range(3)]
                evs = [self.alloc(st, [128, 8, 128], BF16) for _ in range(3)]
                accO = self.banks[0:2]
                accL = self.banks[2:4]
                self.sb = self.banks[4:8]
                self.sb_rr = 0
                for h in range(8):
                    kvh = h // 4
                    qa = qas[h % 2]
                    wt = wts[h % 2]
                    fw.dma(fw.sp, qa.t[:, :], self.QA[u, h], reads=[self.rQ[u]], writes=[qa.r])
                    fw.dma(fw.sp, wt.t[:, :], self.WT[h], reads=[self.rWT], writes=[wt.r])
                    self.attn_begin(SCALE_A, pts, accO, accL)
                    for qt in range(2):
                        blks = [qt * 4 + o for o in range(6)]
                        if not shard:
                            blks = [b for b in blks if not ((b == 0 and prev_u is None) or (b == 9 and next_u is None))]
                        for b in blks:
                            o = b - qt * 4
                            self.attn_push(dict(
                                qt=qt,
                                s_mms=[(kax.t[:, kvh, b, :], qa.t[:, qt * TT:(qt + 1) * TT], [kax.r, qa.r]),
                                       (self.ident_b.t[:, :], wt.t[:, 640 - 128 * o:1152 - 128 * o],
                                        [self.ident_b.r, wt.r])],
                                v_lhsT=vax.t[:, b, kvh * 128:(kvh + 1) * 128], v_reads=[vax.r],
                                bias=(self.masks.t[:, 24:25] if (shard and b == 0) else
                                      self.masks.t[:, 25:26] if (shard and b == 9) else None),
                                first=(b == blks[0]), last=(b == blks[-1])))
                    self.attn_flush()
                    self.attn_epilogue(attn, h, accO, accL, esets[h % 2], self.es.t[:, l, h:h + 1], self.es.r)
                li = 0
                for h in range(8):
                    qn = qns[h % 2]
                    qr = qrs[h % 2]
                    fw.dma(fw.sp, qn.t[:, :], self.QN[u, h], reads=[self.rQ[u]], writes=[qn.r])
                    fw.dma(fw.sp, qr.t[:, :], self.QR[u, h // 2, (h % 2) * 64:(h % 2) * 64 + 64, :],
                           reads=[self.rQ[u]], writes=[qr.r])
                    self.attn_begin(SCALE_B, pts, accO, accL)
                    for ci, c in enumerate(ctx):
                        ek = eks[li % 3]
                        ev = evs[li % 3]
                        li += 1
                        eap, ereg = ek_src(c, h)
                        fw.dma(fw.sp, ek.t[:, :], eap, reads=[ereg], writes=[ek.r])
                        vap, vreg = ev_src(c, h)
                        fw.dma(fw.sp, ev.t[:, :, :], vap.rearrange("(kb p) d -> p kb d", p=128),
                               reads=[vreg], writes=[ev.r])
                        for kb in range(8):
                            for qt in range(2):
                                self.attn_push(dict(
                                    qt=qt,
                                    s_mms=[(ek.t[:, kb * 128:(kb + 1) * 128], qn.t[:, qt * TT:(qt + 1) * TT],
                                            [ek.r, qn.r]),
                                           (kr.t[0:64, ci, kb * 128:(kb + 1) * 128],
                                            qr.t[0:64, qt * TT:(qt + 1) * TT], [kr.r, qr.r])],
                                    v_lhsT=ev.t[:, kb, :], v_reads=[ev.r],
                                    first=(ci == 0 and kb == 0), last=(ci == nctx - 1 and kb == 7)))
                    self.attn_flush()
                    self.attn_epilogue(attn, 8 + h, accO, accL, esets[h % 2])
            with self.scope() as st:
                y = self.alloc(st, [128, KC, T], F32)
                wos = [self.alloc(st, [128, KC, 256], BF16) for _ in range(2)]
                xrs = [self.alloc(st, [128, T], F32) for _ in range(2)]
                w_o = self.w_o[l]
                mains = self.banks[0:4]
                mrr = 0
                for g in range(8):
                    wo = wos[g % 2]
                    fw.dma(fw.pool, wo.t[:, :, :], w_o[:, g * 256:(g + 1) * 256].rearrange("(c p) n -> p c n", p=128),
                           writes=[wo.r])
                    for jj in range(2):
                        o = 2 * g + jj
                        xr = xrs[o % 2]
                        fw.dma(fw.sp, xr.t[:, :], self.XT[u, o], reads=[self.rXT[u]], writes=[xr.r])
                        for tt in range(2):
                            bank = mains[mrr % 4]
                            mrr += 1
                            for kc in range(KC):
                                self.mm(bank, bank.t[:, :], wo.t[:, kc, jj * 128:(jj + 1) * 128],
                                        attn.t[:, kc, tt * TT:(tt + 1) * TT], kc == 0, kc == KC - 1,
                                        [wo.r, attn.r])
                            self.dve(lambda: nc.vector.scalar_tensor_tensor(
                                out=y.t[:, o, tt * TT:(tt + 1) * TT], in0=xr.t[:, tt * TT:(tt + 1) * TT],
                                scalar=float(ALPHA), in1=bank.t[:, :], op0=ALU.mult, op1=ALU.add),
                                [xr.r, bank.r], [y.r])
                self.layer_norm(st, y, l, 0, self.banks[4:8])
                if shard:
                    self.publish_halo(st, y)
                for c0 in range(0, KC, 4):
                    fw.dma(fw.sp, self.X1[u, c0:c0 + 4].rearrange("c p t -> p c t"), y.t[:, c0:c0 + 4, :],
                           reads=[y.r], writes=[self.rX1[u]])

    def layer_norm(self, st, y, l, which, banks4):
        nc = self.nc
        ybs = [self.alloc(st, [128, T], BF16) for _ in range(2)]
        yqs = [self.alloc(st, [128, T], BF16) for _ in range(2)]
        mean = self.alloc(st, [128, T], F32)
        rstd = self.alloc(st, [128, T], F32)
        psM = banks4[0:2]
        psQ = banks4[2:4]
        for c in range(KC):
            yb = ybs[c % 2]
            yq = yqs[c % 2]
            self.act(lambda: nc.scalar.copy(out=yb.t[:, :], in_=y.t[:, c, :]), [y.r], [yb.r])
            self.act(lambda: nc.scalar.activation(out=yq.t[:, :], in_=y.t[:, c, :], func=AF.Square), [y.r], [yq.r])
            for tt in range(2):
                self.mm(psM[tt], psM[tt].t[:, :], self.ones_b.t[:, :], yb.t[:, tt * TT:(tt + 1) * TT],
                        c == 0, c == KC - 1, [self.ones_b.r, yb.r])
                self.mm(psQ[tt], psQ[tt].t[:, :], self.ones_b.t[:, :], yq.t[:, tt * TT:(tt + 1) * TT],
                        c == 0, c == KC - 1, [self.ones_b.r, yq.r], signal=True)
        inv = 1.0 / D_MODEL
        for tt in range(2):
            sl = slice(tt * TT, (tt + 1) * TT)
            self.act(lambda: nc.scalar.mul(out=mean.t[:, sl], in_=psM[tt].t[:, :], mul=inv), [psM[tt].r], [mean.r])
            self.dve(lambda: nc.vector.tensor_tensor(out=rstd.t[:, sl], in0=mean.t[:, sl], in1=mean.t[:, sl],
                                                     op=ALU.mult), [mean.r], [rstd.r])
            self.dve(lambda: nc.vector.scalar_tensor_tensor(out=rstd.t[:, sl], in0=psQ[tt].t[:, :], scalar=inv,
                                                            in1=rstd.t[:, sl], op0=ALU.mult, op1=ALU.subtract),
                     [psQ[tt].r, rstd.r], [rstd.r])
        self.act(lambda: nc.scalar.activation(out=rstd.t[:, :], in_=rstd.t[:, :], func=AF.Sqrt,
                                              bias=self.eps_ln.t[:, 0:1], scale=1.0),
                 [rstd.r, self.eps_ln.r], [rstd.r])
        self.dve(lambda: nc.vector.reciprocal(out=rstd.t[:, :], in_=rstd.t[:, :]), [rstd.r], [rstd.r])
        gi, bi = 2 * which, 2 * which + 1
        for c in range(KC):
            self.dve(lambda: nc.vector.tensor_tensor(out=y.t[:, c, :], in0=y.t[:, c, :], in1=mean.t[:, :],
                                                     op=ALU.subtract), [y.r, mean.r], [y.r])
            self.dve(lambda: nc.vector.tensor_tensor(out=y.t[:, c, :], in0=y.t[:, c, :], in1=rstd.t[:, :],
                                                     op=ALU.mult), [y.r, rstd.r], [y.r])
            self.act(lambda: nc.scalar.activation(out=y.t[:, c, :], in_=y.t[:, c, :], func=AF.Identity,
                                                  bias=self.lnp.t[:, l, bi, c:c + 1],
                                                  scale=self.lnp.t[:, l, gi, c:c + 1]),
                     [y.r, self.lnp.r], [y.r])

    def phaseD(self, un, l, last):
        nc, fw = self.nc, self.fw
        u = un["u"]
        j, nj = un["j"], un["nj"]
        prev_u = u - 1 if j > 0 else None
        next_u = u + 1 if j < nj - 1 else None
        NW = 342
        with self.scope() as st_outer:
            y2 = self.alloc(st_outer, [128, KC, T], F32)
            with self.scope() as st:
                x1q = [self.alloc(st, [128, 4, T + 2], BF16) for _ in range(4)]
                for q in range(4):
                    c0 = 4 * q
                    fw.dma(fw.pool, x1q[q].t[:, :, 1:T + 1], self.X1[u, c0:c0 + 4].rearrange("c p t -> p c t"),
                           reads=[self.rX1[u]], writes=[x1q[q].r])
                for c0 in range(0, KC, 4):
                    fw.dma(fw.sp, y2.t[:, c0:c0 + 4, :], self.X1[u, c0:c0 + 4].rearrange("c p t -> p c t"),
                           reads=[self.rX1[u]], writes=[y2.r])
                if un.get("shard"):
                    hbo = self.alloc(st, [128, 8, 2 * KC], F32)
                    hacc = self.alloc(st, [128, 2 * KC], F32)
                    fw.dma(fw.sp, hbo.t[:, :, :], self.HBOUT.rearrange("(s p) e -> p s e", p=128),
                           reads=[self.rHBOUT], writes=[hbo.r])
                    for mcol0, e, col in ((8, 1, 0), (16, 0, T + 1)):
                        self.masked_select(hacc, 2 * KC, hbo, mcol0)
                        for q in range(4):
                            self.act(lambda: nc.scalar.copy(
                                out=x1q[q].t[:, :, col:col + 1],
                                in_=hacc.t[:, :].rearrange("p (c e) -> p c e", e=2)[:, 4 * q:4 * q + 4, e:e + 1]),
                                [hacc.r], [x1q[q].r])
                elif prev_u is not None:
                    for q in range(4):
                        fw.dma(fw.pool, x1q[q].t[:, :, 0:1],
                               self.X1[prev_u, 4 * q:4 * q + 4, :, T - 1:T].rearrange("c p t -> p c t"),
                               reads=[self.rX1[prev_u]], writes=[x1q[q].r], allow_slow_non_contiguous=True)
                else:
                    for q in range(4):
                        self.dve(lambda: nc.vector.memset(x1q[q].t[:, :, 0:1], 0.0), [], [x1q[q].r])
                if un.get("shard"):
                    pass
                elif next_u is not None:
                    for q in range(4):
                        fw.dma(fw.pool, x1q[q].t[:, :, T + 1:T + 2],
                               self.X1[next_u, 4 * q:4 * q + 4, :, 0:1].rearrange("c p t -> p c t"),
                               reads=[self.rX1[next_u]], writes=[x1q[q].r], allow_slow_non_contiguous=True)
                else:
                    for q in range(4):
                        self.dve(lambda: nc.vector.memset(x1q[q].t[:, :, T + 1:T + 2], 0.0), [], [x1q[q].r])
                for c in range(KC):
                    self.act(lambda: nc.scalar.mul(out=y2.t[:, c, :], in_=y2.t[:, c, :], mul=float(ALPHA)),
                             [y2.r], [y2.r])
                wus = [self.alloc(st, [128, KC, 2, 256], BF16) for _ in range(2)]
                wds = [self.alloc(st, [128, 4, D_MODEL], BF16) for _ in range(2)]
                us = [self.alloc(st, [128, T + 2], F32) for _ in range(2)]
                ag = self.alloc(st, [128, T], F32)
                av = self.alloc(st, [128, T], F32)
                sg = self.alloc(st, [128, T], F32)
                hgs = [self.alloc(st, [128, 4, T], BF16) for _ in range(2)]
                w_up = self.w_up[l]
                w_dn = self.w_down[l]
                usets = [self.banks[0:3], self.banks[3:6]]
                dbanks = self.banks[6:8]
                ci = 0
                drr = 0
                wui = 0
                GP = 4
                NG = NPAIR // GP

                def down(gi, o_lo, o_hi):
                    nonlocal drr
                    wd = wds[gi % 2]
                    hg = hgs[gi % 2]
                    for o in range(o_lo, o_hi):
                        for tt in range(2):
                            bank = dbanks[drr % 2]
                            drr += 1
                            for k in range(GP):
                                self.mm(bank, bank.t[:, :], wd.t[:, k, o * 128:(o + 1) * 128],
                                        hg.t[:, k, tt * TT:(tt + 1) * TT], k == 0, k == GP - 1, [wd.r, hg.r])
                            self.dve(lambda: nc.vector.tensor_tensor(
                                out=y2.t[:, o, tt * TT:(tt + 1) * TT], in0=bank.t[:, :],
                                in1=y2.t[:, o, tt * TT:(tt + 1) * TT], op=ALU.add), [bank.r, y2.r], [y2.r])

                for gi in range(NG):
                    wd = wds[gi % 2]
                    hg = hgs[gi % 2]
                    p0 = GP * gi
                    for kk in range(GP):
                        pr = p0 + kk
                        if kk % 2 == 0:
                            wu = wus[wui % 2]
                            wui += 1
                            for gv in range(2):
                                c0 = gv * D_FF + pr * 128
                                fw.dma(fw.pool, wu.t[:, :, gv, :],
                                       w_up[:, c0:c0 + 256].rearrange("(c p) n -> p c n", p=128), writes=[wu.r])
                        if kk == 2:
                            fw.dma(fw.pool, wd.t[:, :, :],
                                   w_dn[p0 * 128:(p0 + GP) * 128, :].rearrange("(k p) n -> p k n", p=128),
                                   writes=[wd.r])
                        k = kk % 2
                        for gv in range(2):
                            bset = usets[ci % 2]
                            ub = us[ci % 2]
                            ci += 1
                            for jn in range(3):
                                bank = bset[jn]
                                for kc in range(KC):
                                    self.mm(bank, bank.t[:, 0:NW], wu.t[:, kc, gv, k * 128:(k + 1) * 128],
                                            x1q[kc // 4].t[:, kc % 4, jn * NW:(jn + 1) * NW], kc == 0, kc == KC - 1,
                                            [wu.r, x1q[kc // 4].r])
                                self.act(lambda: nc.scalar.copy(out=ub.t[:, jn * NW:(jn + 1) * NW],
                                                                in_=bank.t[:, 0:NW]), [bank.r], [ub.r])
                            if gi > 0:
                                qd = 2 * kk + gv
                                down(gi - 1, 2 * qd, 2 * qd + 2)
                            a = ag if gv == 0 else av
                            col = gv * NPAIR + pr
                            self.act(lambda: nc.scalar.activation(out=a.t[:, :], in_=ub.t[:, 1:T + 1],
                                                                  func=AF.Identity,
                                                                  bias=self.cb.t[:, l, col:col + 1],
                                                                  scale=self.cw.t[:, l, 1, col:col + 1]),
                                     [ub.r, self.cw.r, self.cb.r], [a.r])
                            self.dve(lambda: nc.vector.scalar_tensor_tensor(
                                out=a.t[:, :], in0=ub.t[:, 0:T], scalar=self.cw.t[:, l, 0, col:col + 1],
                                in1=a.t[:, :], op0=ALU.mult, op1=ALU.add), [ub.r, a.r, self.cw.r], [a.r])
                            self.dve(lambda: nc.vector.scalar_tensor_tensor(
                                out=a.t[:, :], in0=ub.t[:, 2:T + 2], scalar=self.cw.t[:, l, 2, col:col + 1],
                                in1=a.t[:, :], op0=ALU.mult, op1=ALU.add), [ub.r, a.r, self.cw.r], [a.r])
                            if gv == 0:
                                self.act(lambda: nc.scalar.activation(out=sg.t[:, :], in_=ag.t[:, :], func=AF.Silu),
                                         [ag.r], [sg.r])
                        self.dve(lambda: nc.vector.tensor_tensor(out=hg.t[:, kk, :], in0=sg.t[:, :], in1=av.t[:, :],
                                                                 op=ALU.mult), [sg.r, av.r], [hg.r])
                down(NG - 1, 0, KC)
            with self.scope() as st:
                self.layer_norm(st, y2, l, 1, self.banks[0:4])
                if not last:
                    for c0 in range(0, KC, 4):
                        fw.dma(fw.sp, self.XT[u, c0:c0 + 4].rearrange("c p t -> p c t"), y2.t[:, c0:c0 + 4, :],
                               reads=[y2.r], writes=[self.rXT[u]])
                else:
                    osts = [self.alloc(st, [128, D_MODEL], F32) for _ in range(2)]
                    n = 0
                    for tb in range(8):
                        ost = osts[tb % 2]
                        for g in range(4):
                            bank = self.banks[4 + n % 4]
                            n += 1
                            for jn in range(4):
                                c = g * 4 + jn
                                fw.op(fw.pe, lambda: nc.tensor.transpose(bank.t[:, jn * 128:(jn + 1) * 128],
                                                                         y2.t[:, c, tb * 128:(tb + 1) * 128],
                                                                         self.ident_f.t[:, :]),
                                      reads=[y2.r, self.ident_f.r], writes=[bank.r], signal=(jn == 3))
                            if n % 2:
                                self.act(lambda: nc.scalar.copy(out=ost.t[:, g * 512:(g + 1) * 512], in_=bank.t[:, :]),
                                         [bank.r], [ost.r])
                            else:
                                self.dve(lambda: nc.vector.tensor_copy(out=ost.t[:, g * 512:(g + 1) * 512],
                                                                       in_=bank.t[:, :]), [bank.r], [ost.r])
                        r0 = un["tok0"] + tb * 128
                        fw.dma(fw.sp, self.yout[r0:r0 + 128, :], ost.t[:, :], reads=[ost.r], writes=[self.rOut])

    def build(self):
        self.setup_consts()
        for un in self.units:
            self.phase0(un)
        sh = [un for un in self.units if un.get("shard")]
        pr = [un for un in self.units if not un.get("shard")]
        for l in range(self.n_layers):
            last = (l == self.n_layers - 1)
            for un in sh:
                self.phaseA(un, l)
                self.publish_L(un)
            for un in pr:
                self.phaseA(un, l)
            for un in sh:
                self.expand_slots(l)
                self.phaseBC(un, l)
            for un in pr:
                self.phaseBC(un, l)
            for un in pr:
                self.phaseD(un, l, last)
            for un in sh:
                self.phaseD(un, l, last)
        self.fw.finish(self.fw.sp)
        return self.nc


def _const_tables(maxS, rel_bias):
    import jax
    import jax.numpy as jnp
    cpu = jax.devices("cpu")[0]
    with jax.default_device(cpu):
        inv = 1.0 / (10000.0 ** (jnp.arange(0, 64, 2, dtype=jnp.float32) / 64))
        ang = jnp.arange(maxS, dtype=jnp.float32)[:, None] * inv[None, :]
        cos = np.asarray(jnp.cos(ang).astype(jnp.float32)).T
        sin = np.asarray(jnp.sin(ang).astype(jnp.float32)).T
        rel = jnp.arange(-1200, 1201, dtype=jnp.int32)
        half, max_exact = 16, 8
        ret = (rel > 0).astype(jnp.int32) * half
        n = jnp.abs(rel)
        large = max_exact + (jnp.log(jnp.maximum(n, 1).astype(jnp.float32) / max_exact)
                             / math.log(128 / max_exact) * (half - max_exact)).astype(jnp.int32)
        large = jnp.minimum(large, half - 1)
        bucket = np.asarray(ret + jnp.where(n < max_exact, n, large))
    idx = np.arange(128) % 32
    cosT = np.ascontiguousarray(cos[idx, :])
    sgn = np.where((np.arange(128) % 64) < 32, -1.0, 1.0).astype(np.float32)[:, None]
    sinT = np.ascontiguousarray(sin[idx, :] * sgn)
    kl = np.arange(128)[:, None]
    jj = np.arange(WTW)[None, :]
    relm = kl - jj + 512
    bidx = bucket[relm + 1200]
    valid = np.abs(relm) <= 128
    rb_ext = np.concatenate([np.asarray(rel_bias, np.float32), np.full((1, 8), NEG, np.float32)], axis=0)
    bidx = np.where(valid, bidx, 32)
    wtab = np.ascontiguousarray(rb_ext[bidx].transpose(2, 0, 1))
    return cosT.astype(np.float32), sinT.astype(np.float32), wtab.astype(np.float32)


_WNAMES = ["w_in", "sink", "q_norm_g", "w_uq", "kv_norm_g", "w_ukv", "w_o", "ln1_g", "ln1_b", "w_up",
           "conv_w", "conv_b", "w_down", "ln2_g", "ln2_b"]


def core_masks(c, n=8):
    m = np.zeros((128, 26), np.float32)
    m[:, c] = 1.0
    if c > 0:
        m[:, 8 + c - 1] = 1.0
    else:
        m[:, 24] = NEG
    if c < n - 1:
        m[:, 16 + c + 1] = 1.0
    else:
        m[:, 25] = NEG
    return m


def make_in_maps(b, x_seqs_per_core, shard_chunks, weights, rel_bias, n_layers, n_cores):
    maxS = max(b.maxS, T * n_cores if shard_chunks is not None else 0)
    cosT, sinT, wtab = _const_tables(maxS, rel_bias)
    ident = np.eye(128, dtype=np.float32)
    in_maps = []
    for c in range(n_cores):
        xs = list(x_seqs_per_core[c])
        if shard_chunks is not None:
            xs.append(shard_chunks[c])
        m = {"xin": np.ascontiguousarray(np.concatenate(xs, axis=0), dtype=np.float32),
             "cosT": np.ascontiguousarray(cosT[:, :b.maxS]), "sinT": np.ascontiguousarray(sinT[:, :b.maxS]),
             "wtab": wtab, "ident": ident}
        if shard_chunks is not None:
            m["cosS"] = np.ascontiguousarray(cosT[:, c * T:(c + 1) * T])
            m["sinS"] = np.ascontiguousarray(sinT[:, c * T:(c + 1) * T])
            m["masks"] = core_masks(c, n_cores)
        for k in _WNAMES:
            m[k] = np.ascontiguousarray(np.asarray(weights[k], np.float32)[:n_layers])
        in_maps.append(m)
    return in_maps


def kernel(x_prompt, x_sample, rel_bias, w_in, sink, q_norm_g, w_uq, kv_norm_g, w_ukv, w_o,
           ln1_g, ln1_b, w_up, conv_w, conv_b, w_down, ln2_g, ln2_b):
    weights = dict(w_in=w_in, sink=sink, q_norm_g=q_norm_g, w_uq=w_uq, kv_norm_g=kv_norm_g, w_ukv=w_ukv,
                   w_o=w_o, ln1_g=ln1_g, ln1_b=ln1_b, w_up=w_up, conv_w=conv_w, conv_b=conv_b,
                   w_down=w_down, ln2_g=ln2_g, ln2_b=ln2_b)
    x_prompt = np.asarray(x_prompt, np.float32)
    x_sample = np.asarray(x_sample, np.float32)
    n = 8
    S = x_prompt.shape[1]
    per_core = [[x_prompt[2 * c], x_prompt[2 * c + 1]] for c in range(n)]
    chunks = [x_sample[0, c * T:(c + 1) * T] for c in range(n)]
    b = Builder([S, S], DEPTH, shard=True)
    nc = b.build()
    in_maps = make_in_maps(b, per_core, chunks, weights, rel_bias, DEPTH, n)
    res = run_bass_kernel_spmd(nc, in_maps, core_ids=list(range(n)))
    ys = [res.results[c]["yout"] for c in range(n)]
    y_prompt = np.stack([ys[c][i * S:(i + 1) * S] for c in range(n) for i in range(2)], axis=0)
    y_sample = np.concatenate([ys[c][2 * S:2 * S + T] for c in range(n)], axis=0)[None]
    return (y_prompt.astype(np.float32), y_sample.astype(np.float32))
```

```python
import math
from contextlib import ExitStack

import numpy as np
import concourse.bass as bass
import concourse.mybir as mybir
from concourse.bass_utils import run_bass_kernel_spmd

F32 = mybir.dt.float32
BF16 = mybir.dt.bfloat16
AF = mybir.ActivationFunctionType
ALU = mybir.AluOpType

D_MODEL = 2048
KC = 16
T = 1024
TT = 512
N_HEADS = 8
Q_LORA = 512
KV_LORA = 256
D_FF = 5632
NPAIR = D_FF // 128
IN_COLS = 2368
DEPTH = 4
ALPHA = (2 * DEPTH) ** 0.25
LN_EPS = 1e-5
RMS_EPS = 1e-6
SCALE_A = 128 ** -0.5
SCALE_B = 192 ** -0.5
NEG = -1e30
LROWS = 832
WTW = 1152


class Reg:
    __slots__ = ("writer", "readers")

    def __init__(self):
        self.writer = None
        self.readers = {}


class Eng:
    def __init__(self, fw, name, h, inorder=False):
        self.h = h
        self.name = name
        self.sem = fw.nc.alloc_semaphore(name="e_" + name)
        self.key = fw.addsem(self.sem)
        self.count = 0
        self.known = {}
        self.inorder = inorder
        self.ninst = 0


class FW:
    def __init__(self, nc, n_dma_sems=48):
        self.nc = nc
        self.sems = []
        self.pe = Eng(self, "pe", nc.tensor, inorder=True)
        self.act = Eng(self, "act", nc.scalar)
        self.dve = Eng(self, "dve", nc.vector)
        self.pool = Eng(self, "pool", nc.gpsimd)
        self.sp = Eng(self, "sp", nc.sync)
        self.engs = (self.pe, self.act, self.dve, self.pool, self.sp)
        self.dsems = []
        for i in range(n_dma_sems):
            s = nc.alloc_semaphore(name=f"d{i}")
            self.dsems.append([self.addsem(s), 0])
        n_sw = 6
        self.ring_hw = list(range(0, n_dma_sems - n_sw))
        self.ring_sw = list(range(n_dma_sems - n_sw, n_dma_sems))
        self.next_hw = 0
        self.next_sw = 0
        self.ccsem = [self.addsem(nc.alloc_semaphore(name="cc")), 0]

    def addsem(self, s):
        self.sems.append(s)
        return len(self.sems) - 1

    def fresh(self):
        r = Reg()
        for e in self.engs:
            if e.count > 0:
                r.readers[e.key] = e.count
        for k, v in self.dsems + [self.ccsem]:
            if v > 0:
                r.readers[k] = v
        return r

    def _deps(self, eng, reads, writes, attach_last=False):
        need = {}
        for t in reads:
            if t.writer is not None:
                k, v = t.writer
                if need.get(k, 0) < v:
                    need[k] = v
        for t in writes:
            if t.writer is not None:
                k, v = t.writer
                if need.get(k, 0) < v:
                    need[k] = v
            for k, v in t.readers.items():
                if need.get(k, 0) < v:
                    need[k] = v
        todo = []
        for k, v in need.items():
            if eng.known.get(k, 0) >= v:
                continue
            if k == eng.key and eng.inorder:
                continue
            todo.append((k, v))
            eng.known[k] = v
        attach = None
        if attach_last and todo:
            attach = todo.pop()
        for k, v in todo:
            eng.h.wait_ge(self.sems[k], v)
        return attach

    def _mark(self, ev, reads, writes):
        k, v = ev
        for t in reads:
            if t.readers.get(k, 0) < v:
                t.readers[k] = v
        for t in writes:
            t.writer = ev
            t.readers = {}

    def op(self, eng, fn, reads=(), writes=(), signal=True):
        attach = self._deps(eng, reads, writes, attach_last=True)
        inst = fn()
        if attach is not None:
            inst.wait_op(self.sems[attach[0]], attach[1], "sem-ge")
        eng.ninst += 1
        if signal:
            eng.count += 1
            inst.then_inc(eng.sem, 1)
            ev = (eng.key, eng.count)
        else:
            ev = (eng.key, eng.count + 1)
        self._mark(ev, reads, writes)

    def dma(self, eng, out_ap, in_ap, reads=(), writes=(), **kw):
        if eng is self.pool:
            slot = self.dsems[self.ring_sw[self.next_sw % len(self.ring_sw)]]
            self.next_sw += 1
        else:
            slot = self.dsems[self.ring_hw[self.next_hw % len(self.ring_hw)]]
            self.next_hw += 1
        k = slot[0]
        if slot[1] > 0 and eng.known.get(k, 0) < slot[1]:
            eng.h.wait_ge(self.sems[k], slot[1])
            eng.known[k] = slot[1]
        self._deps(eng, reads, writes)
        inst = eng.h.dma_start(out=out_ap, in_=in_ap, **kw)
        slot[1] += 16
        inst.then_inc(self.sems[k], 16)
        eng.ninst += 1
        self._mark((k, slot[1]), reads, writes)

    def allreduce(self, out_ap, in_ap, reads=(), writes=(), n_cores=8):
        eng = self.pool
        k = self.ccsem[0]
        if self.ccsem[1] > 0 and eng.known.get(k, 0) < self.ccsem[1]:
            eng.h.wait_ge(self.sems[k], self.ccsem[1])
            eng.known[k] = self.ccsem[1]
        self._deps(eng, reads, writes)
        inst = eng.h.collective_compute("AllReduce", ALU.add, replica_groups=[list(range(n_cores))],
                                        ins=[in_ap], outs=[out_ap])
        self.ccsem[1] += 1
        inst.then_inc(self.sems[k], 1)
        eng.ninst += 1
        self._mark((k, self.ccsem[1]), reads, writes)

    def finish(self, eng):
        for k, v in self.dsems + [self.ccsem]:
            if v > 0 and eng.known.get(k, 0) < v:
                eng.h.wait_ge(self.sems[k], v)
                eng.known[k] = v
        for e in self.engs:
            if e is not eng and e.count > 0 and eng.known.get(e.key, 0) < e.count:
                eng.h.wait_ge(self.sems[e.key], e.count)
                eng.known[e.key] = e.count


class Buf:
    __slots__ = ("t", "r", "start", "stop")

    def __init__(self, t, r):
        self.t = t
        self.r = r
        self.start = self.stop = None


class Builder:
    def __init__(self, seq_lens, n_layers, shard=False):
        self.seq_lens = seq_lens
        self.n_layers = n_layers
        self.shard = shard
        nc = self.nc = bass.Bass("TRN2", target_bir_lowering=False)
        self.fw = FW(nc)
        self.uid = 0
        self.units = []
        tok = 0
        for s, S in enumerate(seq_lens):
            nj = S // T
            base = len(self.units)
            for j in range(nj):
                self.units.append(dict(seq=s, j=j, nj=nj, tok0=tok, pos0=j * T, u=base + j,
                                       ctx=list(range(base, base + nj))))
                tok += T
        if shard:
            self.units.append(dict(seq=len(seq_lens), j=0, nj=1, tok0=tok, pos0=0, u=len(self.units),
                                   ctx=list(range(8)), shard=True))
            tok += T
        self.ntok = tok
        NU = self.NU = len(self.units)
        L = n_layers
        di = lambda name, shape, dt=F32: nc.dram_tensor(name, shape, dt, kind="ExternalInput").ap()
        self.xin = di("xin", [self.ntok, D_MODEL])
        self.w_in = di("w_in", [L, D_MODEL, IN_COLS])
        self.sink = di("sink", [L, 8])
        self.q_norm_g = di("q_norm_g", [L, Q_LORA])
        self.w_uq = di("w_uq", [L, Q_LORA, 1536])
        self.kv_norm_g = di("kv_norm_g", [L, KV_LORA])
        self.w_ukv = di("w_ukv", [L, KV_LORA, 2048])
        self.w_o = di("w_o", [L, D_MODEL, D_MODEL])
        self.ln1_g = di("ln1_g", [L, D_MODEL])
        self.ln1_b = di("ln1_b", [L, D_MODEL])
        self.w_up = di("w_up", [L, D_MODEL, 2 * D_FF])
        self.conv_w = di("conv_w", [L, 3, 2 * D_FF])
        self.conv_b = di("conv_b", [L, 2 * D_FF])
        self.w_down = di("w_down", [L, D_FF, D_MODEL])
        self.ln2_g = di("ln2_g", [L, D_MODEL])
        self.ln2_b = di("ln2_b", [L, D_MODEL])
        maxS = max(seq_lens) if seq_lens else T
        self.maxS = maxS
        if shard:
            self.cosS = di("cosS", [128, T])
            self.sinS = di("sinS", [128, T])
            self.masksI = di("masks", [128, 26])
            dsc0 = lambda name, shape, dt: nc.dram_tensor(name, shape, dt, kind="Internal").ap()
            self.GIN = dsc0("GIN", [8 * LROWS, T], BF16)
            self.GOUT = dsc0("GOUT", [8 * LROWS, T], BF16)
            self.HBIN = dsc0("HBIN", [8 * 128, 32], F32)
            self.HBOUT = dsc0("HBOUT", [8 * 128, 32], F32)
            self.EKs = dsc0("EKs", [8, 8, 128, T], BF16)
            self.EVs = dsc0("EVs", [8, T, 1024], BF16)
            self.rGIN, self.rGOUT, self.rHBIN, self.rHBOUT, self.rEs = Reg(), Reg(), Reg(), Reg(), Reg()
        self.cosT = di("cosT", [128, maxS])
        self.sinT = di("sinT", [128, maxS])
        self.wtab = di("wtab", [8, 128, WTW])
        self.identI = di("ident", [128, 128])
        self.yout = nc.dram_tensor("yout", [self.ntok, D_MODEL], F32, kind="ExternalOutput").ap()
        dsc = lambda name, shape, dt: nc.dram_tensor(name, shape, dt, kind="Internal").ap()
        self.XT = dsc("XT", [NU, KC, 128, T], F32)
        self.X1 = dsc("X1", [NU, KC, 128, T], F32)
        self.Lb = dsc("Lb", [NU, LROWS, T], BF16)
        self.QA = dsc("QA", [NU, 8, 128, T], BF16)
        self.QN = dsc("QN", [NU, 8, 128, T], BF16)
        self.QR = dsc("QR", [NU, 4, 128, T], BF16)
        self.EK = dsc("EK", [NU, 8, 128, T], BF16)
        self.EV = dsc("EV", [NU, T, 1024], BF16)
        self.WT = dsc("WT", [8, 128, WTW], BF16)
        R = lambda: Reg()
        self.rXT = [R() for _ in range(NU)]
        self.rX1 = [R() for _ in range(NU)]
        self.rL = [R() for _ in range(NU)]
        self.rQ = [R() for _ in range(NU)]
        self.rE = [R() for _ in range(NU)]
        self.rWT = R()
        self.rOut = R()
        self.banks = [Buf(nc.alloc_psum_tensor(f"bank{i}", [128, 512], F32), Reg()) for i in range(8)]
        self.bank_rr = 0

    def _arena_init(self):
        nc = self.nc
        free = nc.sbuf_bytes_remaining
        free = free() if callable(free) else free
        size = (int(free) - 64) // 32 * 32
        beg, end = nc.bump_sbuf(size)
        self.ar_beg, self.ar_end = int(beg), int(end)
        self.ar_ptr = self.ar_beg
        self.ar_occ = []

    def scope(self):
        b = self

        class _Scope:
            def __enter__(self_):
                if not hasattr(b, "ar_ptr"):
                    b._arena_init()
                self_.mark = b.ar_ptr
                return self_

            def __exit__(self_, *exc):
                b.ar_ptr = self_.mark
                return False
        return _Scope()

    def alloc(self, st, shape, dt):
        start = (self.ar_ptr + 31) // 32 * 32
        buf = self.alloc_at(start, shape, dt)
        self.ar_ptr = buf.stop
        return buf

    def alloc_at(self, start, shape, dt):
        self.uid += 1
        esz = 4 if dt == F32 else 2
        nbytes = esz
        for d in shape[1:]:
            nbytes *= d
        start = (start + 31) // 32 * 32
        stop = start + nbytes
        assert stop <= self.ar_end, f"arena overflow: need {stop - self.ar_beg} have {self.ar_end - self.ar_beg}"
        r = Reg()
        keep = []
        for (s0, e0, reg) in self.ar_occ:
            if e0 <= start or s0 >= stop:
                keep.append((s0, e0, reg))
                continue
            evs = list(reg.readers.items())
            if reg.writer is not None:
                evs.append(reg.writer)
            for k, v in evs:
                if r.readers.get(k, 0) < v:
                    r.readers[k] = v
            if not (s0 >= start and e0 <= stop):
                keep.append((s0, e0, reg))
        keep.append((start, stop, r))
        self.ar_occ = keep
        t = self.nc.alloc_sbuf_tensor_at(f"t{self.uid}", shape, dt, offset=start)
        b = Buf(t, r)
        b.start, b.stop = start, stop
        return b

    def chunk_regs(self, buf, n):
        out = []
        step = (buf.stop - buf.start) // n
        for c in range(n):
            r = Reg()
            r.readers = dict(buf.r.readers)
            r.writer = buf.r.writer
            out.append(r)
            self.ar_occ.append((buf.start + c * step, buf.start + (c + 1) * step, r))
        return out

    def palloc(self, shape, dt):
        self.uid += 1
        return Buf(self.nc.alloc_sbuf_tensor(f"p{self.uid}", shape, dt), Reg())

    def mm(self, bank, out_ap, lhsT, rhs, start, stop, reads, signal=None):
        nc = self.nc
        self.fw.op(self.fw.pe, lambda: nc.tensor.matmul(out_ap, lhsT=lhsT, rhs=rhs, start=start, stop=stop),
                   reads=reads, writes=[bank.r], signal=(stop if signal is None else signal))

    def act(self, fn, reads, writes):
        self.fw.op(self.fw.act, fn, reads=reads, writes=writes)

    def dve(self, fn, reads, writes):
        self.fw.op(self.fw.dve, fn, reads=reads, writes=writes)

    def setup_consts(self):
        nc, fw = self.nc, self.fw
        L = self.n_layers
        self.ident_f = self.palloc([128, 128], F32)
        self.ident_b = self.palloc([128, 128], BF16)
        self.ones_b = self.palloc([128, 128], BF16)
        self.ones_f = self.palloc([128, 128], F32)
        self.lnp = self.palloc([128, L, 4, KC], F32)
        self.cw = self.palloc([128, L, 3, 88], F32)
        self.cb = self.palloc([128, L, 88], F32)
        self.qg = self.palloc([128, L, 4], F32)
        self.kvg = self.palloc([128, L, 2], F32)
        self.es = self.palloc([128, L, 8], F32)
        self.eps_ln = self.palloc([128, 1], F32)
        self.eps_rms = self.palloc([128, 1], F32)
        fw.dma(fw.sp, self.ident_f.t[:, :], self.identI, writes=[self.ident_f.r])
        if self.shard:
            self.masks = self.palloc([128, 26], F32)
            fw.dma(fw.sp, self.masks.t[:, :], self.masksI, writes=[self.masks.r])
        self.act(lambda: nc.scalar.copy(out=self.ident_b.t[:, :], in_=self.ident_f.t[:, :]),
                 [self.ident_f.r], [self.ident_b.r])
        self.dve(lambda: nc.vector.memset(self.ones_b.t[:, :], 1.0), [], [self.ones_b.r])
        self.dve(lambda: nc.vector.memset(self.ones_f.t[:, :], 1.0), [], [self.ones_f.r])
        self.dve(lambda: nc.vector.memset(self.eps_ln.t[:, :], LN_EPS), [], [self.eps_ln.r])
        self.dve(lambda: nc.vector.memset(self.eps_rms.t[:, :], RMS_EPS), [], [self.eps_rms.r])
        for l in range(L):
            for i, src in enumerate((self.ln1_g, self.ln1_b, self.ln2_g, self.ln2_b)):
                fw.dma(fw.sp, self.lnp.t[:, l, i, :], src[l].rearrange("(c p) -> p c", p=128),
                       writes=[self.lnp.r], allow_slow_non_contiguous=True)
            for k in range(3):
                fw.dma(fw.sp, self.cw.t[:, l, k, :], self.conv_w[l, k].rearrange("(c p) -> p c", p=128),
                       writes=[self.cw.r], allow_slow_non_contiguous=True)
            fw.dma(fw.sp, self.cb.t[:, l, :], self.conv_b[l].rearrange("(c p) -> p c", p=128),
                   writes=[self.cb.r], allow_slow_non_contiguous=True)
            fw.dma(fw.sp, self.qg.t[:, l, :], self.q_norm_g[l].rearrange("(c p) -> p c", p=128),
                   writes=[self.qg.r], allow_slow_non_contiguous=True)
            fw.dma(fw.sp, self.kvg.t[:, l, :], self.kv_norm_g[l].rearrange("(c p) -> p c", p=128),
                   writes=[self.kvg.r], allow_slow_non_contiguous=True)
            fw.dma(fw.sp, self.es.t[:, l, :], self.sink[l].partition_broadcast(128), writes=[self.es.r])
        self.act(lambda: nc.scalar.activation(out=self.es.t[:, :, :], in_=self.es.t[:, :, :], func=AF.Exp),
                 [self.es.r], [self.es.r])
        with self.scope() as st:
            for h in range(8):
                tf = self.alloc(st, [128, WTW], F32)
                tb = self.alloc(st, [128, WTW], BF16)
                fw.dma(fw.sp, tf.t[:, :], self.wtab[h], writes=[tf.r])
                self.act(lambda: nc.scalar.mul(out=tb.t[:, :], in_=tf.t[:, :], mul=float(1.0 / SCALE_A)),
                         [tf.r], [tb.r])
                fw.dma(fw.act, self.WT[h], tb.t[:, :], reads=[tb.r], writes=[self.rWT])

    def phase0(self, un):
        nc, fw = self.nc, self.fw
        u = un["u"]
        with self.scope() as st:
            stage = self.alloc(st, [128, KC, T], F32)
            xbs = [self.alloc(st, [128, D_MODEL], F32) for _ in range(2)]
            n = 0
            for tb in range(8):
                xb = xbs[tb % 2]
                fw.dma(fw.sp, xb.t[:, :], self.xin[un["tok0"] + tb * 128: un["tok0"] + (tb + 1) * 128, :],
                       writes=[xb.r])
                for g in range(4):
                    bank = self.banks[n % 8]
                    n += 1
                    for j in range(4):
                        c = g * 4 + j
                        fw.op(fw.pe, lambda: nc.tensor.transpose(bank.t[:, j * 128:(j + 1) * 128],
                                                                 xb.t[:, c * 128:(c + 1) * 128],
                                                                 self.ident_f.t[:, :]),
                              reads=[xb.r, self.ident_f.r], writes=[bank.r], signal=(j == 3))
                    src = bank.t[:, :].rearrange("p (j t) -> p j t", j=4)
                    dst = stage.t[:, g * 4:(g + 1) * 4, tb * 128:(tb + 1) * 128]
                    if n % 2:
                        self.act(lambda: nc.scalar.copy(out=dst, in_=src), [bank.r], [stage.r])
                    else:
                        self.dve(lambda: nc.vector.tensor_copy(out=dst, in_=src), [bank.r], [stage.r])
            for c0 in range(0, KC, 4):
                fw.dma(fw.sp, self.XT[u, c0:c0 + 4].rearrange("c p t -> p c t"), stage.t[:, c0:c0 + 4, :],
                       reads=[stage.r], writes=[self.rXT[u]])

    def fm_chunk(self, lhs_fn, rhs_fn, nk, reads, M=128, ntiles=2, n=TT):
        banks = []
        for tt in range(ntiles):
            bank = self.banks[self.bank_rr % 8]
            self.bank_rr += 1
            for kc in range(nk):
                self.mm(bank, bank.t[0:M, 0:n], lhs_fn(kc), rhs_fn(kc, tt), kc == 0, kc == nk - 1,
                        reads(kc) if callable(reads) else reads)
            banks.append(bank)
        return banks

    def next_bank(self):
        bank = self.banks[self.bank_rr % 8]
        self.bank_rr += 1
        return bank

    def rms_stats(self, sq, nchunk, rs, eps_buf, inv_n):
        nc = self.nc
        for tt in range(2):
            bank = self.next_bank()
            for c in range(nchunk):
                self.mm(bank, bank.t[:, :], self.ones_b.t[:, :], sq.t[:, c, tt * TT:(tt + 1) * TT],
                        c == 0, c == nchunk - 1, [self.ones_b.r, sq.r])
            self.act(lambda: nc.scalar.activation(out=rs.t[:, tt * TT:(tt + 1) * TT], in_=bank.t[:, :],
                                                  func=AF.Sqrt, bias=eps_buf.t[:, 0:1], scale=inv_n),
                     [bank.r, eps_buf.r], [rs.r])
        self.dve(lambda: nc.vector.reciprocal(out=rs.t[:, :], in_=rs.t[:, :]), [rs.r], [rs.r])

    def phaseA(self, un, l):
        nc, fw = self.nc, self.fw
        u = un["u"]
        with self.scope() as st:
            xbq = [self.alloc(st, [128, 4, T], BF16) for _ in range(4)]
            for q in range(4):
                fw.dma(fw.pool, xbq[q].t[:, :, :], self.XT[u, 4 * q:4 * q + 4].rearrange("c p t -> p c t"),
                       reads=[self.rXT[u]], writes=[xbq[q].r])

            xsl = lambda kc, sl: xbq[kc // 4].t[:, kc % 4, sl]
            wgs = [self.alloc(st, [128, KC, 256], BF16) for _ in range(3)]
            stages = [self.alloc(st, [128, T], BF16) for _ in range(4)]
            cq = self.alloc(st, [128, 4, T], F32)
            sq = self.alloc(st, [128, 4, T], BF16)
            cqn = self.alloc(st, [128, 4, T], BF16)
            ckvn = self.alloc(st, [128, 2, T], BF16)
            rs = self.alloc(st, [128, T], F32)
            cs = self.alloc(st, [128, T], F32)
            sn = self.alloc(st, [128, T], F32)
            tmp1 = self.alloc(st, [128, TT], F32)
            tmp2 = self.alloc(st, [128, TT], F32)
            vst = self.alloc(st, [128, 8, 256], BF16)
            if un.get("shard"):
                fw.dma(fw.sp, cs.t[:, :], self.cosS, writes=[cs.r])
                fw.dma(fw.sp, sn.t[:, :], self.sinS, writes=[sn.r])
            else:
                fw.dma(fw.sp, cs.t[:, :], self.cosT[:, un["pos0"]:un["pos0"] + T], writes=[cs.r])
                fw.dma(fw.sp, sn.t[:, :], self.sinT[:, un["pos0"]:un["pos0"] + T], writes=[sn.r])
            w_in = self.w_in[l]
            wi = [0]
            sti = [0]

            def load_wg(c0, ncols):
                wg = wgs[wi[0] % 3]
                wi[0] += 1
                fw.dma(fw.pool, wg.t[:, :, 0:ncols], w_in[:, c0:c0 + ncols].rearrange("(c p) n -> p c n", p=128),
                       writes=[wg.r])
                return wg

            def evac_store(banks, dst_ap, dst_reg, M=128):
                stg = stages[sti[0] % 4]
                sti[0] += 1
                for tt, bank in enumerate(banks):
                    self.act(lambda: nc.scalar.copy(out=stg.t[0:M, tt * TT:(tt + 1) * TT], in_=bank.t[0:M, :]),
                             [bank.r], [stg.r])
                fw.dma(fw.act, dst_ap, stg.t[0:M, :], reads=[stg.r], writes=[dst_reg])

            for g in range(5):
                wg = load_wg(g * 256, 256)
                for j in range(2):
                    banks = self.fm_chunk(lambda kc: wg.t[:, kc, j * 128:(j + 1) * 128],
                                          lambda kc, tt: xsl(kc, slice(tt * TT, (tt + 1) * TT)), KC,
                                          lambda kc: [wg.r, xbq[kc // 4].r])
                    if g < 4:
                        evac_store(banks, self.QA[u, 2 * g + j], self.rQ[u])
                    else:
                        evac_store(banks, self.Lb[u, j * 128:(j + 1) * 128, :], self.rL[u])
            wg = load_wg(1280, 256)
            for tb in range(8):
                bank = self.next_bank()
                for kc in range(KC):
                    self.mm(bank, bank.t[:, 0:256], xsl(kc, slice(tb * 128, (tb + 1) * 128)), wg.t[:, kc, 0:256],
                            kc == 0, kc == KC - 1, [wg.r, xbq[kc // 4].r])
                self.act(lambda: nc.scalar.copy(out=vst.t[:, tb, :], in_=bank.t[:, 0:256]), [bank.r], [vst.r])
            va_view = self.Lb[u, 576:832, :].rearrange("r (a d) -> (r a) d", d=256)
            fw.dma(fw.act, va_view.rearrange("(tb p) d -> p tb d", p=128), vst.t[:, :, :],
                   reads=[vst.r], writes=[self.rL[u]])
            for g in range(2):
                wg = load_wg(1536 + g * 256, 256)
                for j in range(2):
                    c = 2 * g + j
                    banks = self.fm_chunk(lambda kc: wg.t[:, kc, j * 128:(j + 1) * 128],
                                          lambda kc, tt: xsl(kc, slice(tt * TT, (tt + 1) * TT)), KC,
                                          lambda kc: [wg.r, xbq[kc // 4].r])
                    for tt, bank in enumerate(banks):
                        self.act(lambda: nc.scalar.copy(out=cq.t[:, c, tt * TT:(tt + 1) * TT], in_=bank.t[:, :]),
                                 [bank.r], [cq.r])
                        self.act(lambda: nc.scalar.activation(out=sq.t[:, c, tt * TT:(tt + 1) * TT],
                                                              in_=bank.t[:, :], func=AF.Square),
                                 [bank.r], [sq.r])
            self.rms_stats(sq, 4, rs, self.eps_rms, 1.0 / Q_LORA)
            for c in range(4):
                self.dve(lambda: nc.vector.scalar_tensor_tensor(out=cqn.t[:, c, :], in0=cq.t[:, c, :],
                                                                scalar=self.qg.t[:, l, c:c + 1], in1=rs.t[:, :],
                                                                op0=ALU.mult, op1=ALU.mult),
                         [cq.r, rs.r, self.qg.r], [cqn.r])
            wg = load_wg(2048, 256)
            for c in range(2):
                banks = self.fm_chunk(lambda kc: wg.t[:, kc, c * 128:(c + 1) * 128],
                                      lambda kc, tt: xsl(kc, slice(tt * TT, (tt + 1) * TT)), KC,
                                          lambda kc: [wg.r, xbq[kc // 4].r])
                for tt, bank in enumerate(banks):
                    self.act(lambda: nc.scalar.copy(out=cq.t[:, c, tt * TT:(tt + 1) * TT], in_=bank.t[:, :]),
                             [bank.r], [cq.r])
                    self.act(lambda: nc.scalar.activation(out=sq.t[:, c, tt * TT:(tt + 1) * TT],
                                                          in_=bank.t[:, :], func=AF.Square),
                             [bank.r], [sq.r])
            self.rms_stats(sq, 2, rs, self.eps_rms, 1.0 / KV_LORA)
            for c in range(2):
                self.dve(lambda: nc.vector.scalar_tensor_tensor(out=ckvn.t[:, c, :], in0=cq.t[:, c, :],
                                                                scalar=self.kvg.t[:, l, c:c + 1], in1=rs.t[:, :],
                                                                op0=ALU.mult, op1=ALU.mult),
                         [cq.r, rs.r, self.kvg.r], [ckvn.r])
                fw.dma(fw.sp, self.Lb[u, 256 + c * 128:256 + (c + 1) * 128, :], ckvn.t[:, c, :],
                       reads=[ckvn.r], writes=[self.rL[u]])
            wg = wgs[wi[0] % 3]
            wi[0] += 1
            src = lambda a, b: w_in[:, a:b].rearrange("(c p) n -> p c n", p=128)
            fw.dma(fw.pool, wg.t[:, :, 0:64], src(2304, 2368), writes=[wg.r])
            fw.dma(fw.pool, wg.t[:, :, 64:96], src(2336, 2368), writes=[wg.r])
            fw.dma(fw.pool, wg.t[:, :, 96:128], src(2304, 2336), writes=[wg.r])
            bA = self.fm_chunk(lambda kc: wg.t[:, kc, 0:64], lambda kc, tt: xsl(kc, slice(tt * TT, (tt + 1) * TT)),
                               KC, lambda kc: [wg.r, xbq[kc // 4].r], M=64)
            bB = self.fm_chunk(lambda kc: wg.t[:, kc, 64:128], lambda kc, tt: xsl(kc, slice(tt * TT, (tt + 1) * TT)),
                               KC, lambda kc: [wg.r, xbq[kc // 4].r], M=64)
            stg = stages[sti[0] % 4]
            sti[0] += 1
            for tt in range(2):
                self.rope(bA[tt], bB[tt], cs, sn, tmp1, tmp2, stg, tt, 64)
            fw.dma(fw.sp, self.Lb[u, 512:576, :], stg.t[0:64, :], reads=[stg.r], writes=[self.rL[u]])

            wqn = self.alloc(st, [128, 4, 8, 128], BF16)
            wqr = self.alloc(st, [128, 4, 8, 64], BF16)
            wqs = self.alloc(st, [128, 4, 8, 64], BF16)
            wq = self.w_uq[l].rearrange("(c p) (h d) -> c p h d", p=128, d=192)
            for kc in range(4):
                fw.dma(fw.pool, wqn.t[:, kc, :, :], wq[kc, :, :, 0:128], writes=[wqn.r])
                fw.dma(fw.pool, wqr.t[:, kc, :, :], wq[kc, :, :, 128:192], writes=[wqr.r])
                fw.dma(fw.pool, wqs.t[:, kc, :, 0:32], wq[kc, :, :, 160:192], writes=[wqs.r])
                fw.dma(fw.pool, wqs.t[:, kc, :, 32:64], wq[kc, :, :, 128:160], writes=[wqs.r])
            for h in range(8):
                banks = self.fm_chunk(lambda kc: wqn.t[:, kc, h, :],
                                      lambda kc, tt: cqn.t[:, kc, tt * TT:(tt + 1) * TT], 4, [wqn.r, cqn.r])
                evac_store(banks, self.QN[u, h], self.rQ[u])
            for j in range(4):
                bA = self.fm_chunk(lambda kc: wqr.t[:, kc, 2 * j:2 * j + 2, :].rearrange("p h d -> p (h d)"),
                                   lambda kc, tt: cqn.t[:, kc, tt * TT:(tt + 1) * TT], 4, [wqr.r, cqn.r])
                bB = self.fm_chunk(lambda kc: wqs.t[:, kc, 2 * j:2 * j + 2, :].rearrange("p h d -> p (h d)"),
                                   lambda kc, tt: cqn.t[:, kc, tt * TT:(tt + 1) * TT], 4, [wqs.r, cqn.r])
                stg = stages[sti[0] % 4]
                sti[0] += 1
                for tt in range(2):
                    self.rope(bA[tt], bB[tt], cs, sn, tmp1, tmp2, stg, tt, 128)
                fw.dma(fw.sp, self.QR[u, j], stg.t[:, :], reads=[stg.r], writes=[self.rQ[u]])

            if un.get("shard"):
                return
            wkn = self.alloc(st, [128, 2, 8, 128], BF16)
            wkv = self.alloc(st, [128, 2, 8, 128], BF16)
            wk = self.w_ukv[l].rearrange("(c p) (h d) -> c p h d", p=128, d=256)
            for kc in range(2):
                fw.dma(fw.pool, wkn.t[:, kc, :, :], wk[kc, :, :, 0:128], writes=[wkn.r])
                fw.dma(fw.pool, wkv.t[:, kc, :, :], wk[kc, :, :, 128:256], writes=[wkv.r])
            for h in range(8):
                banks = self.fm_chunk(lambda kc: wkn.t[:, kc, h, :],
                                      lambda kc, tt: ckvn.t[:, kc, tt * TT:(tt + 1) * TT], 2, [wkn.r, ckvn.r])
                evac_store(banks, self.EK[u, h], self.rE[u])
            for tb in range(8):
                stg = stages[sti[0] % 4]
                sti[0] += 1
                for half in range(2):
                    bank = self.next_bank()
                    for kc in range(2):
                        self.mm(bank, bank.t[:, :], ckvn.t[:, kc, tb * 128:(tb + 1) * 128],
                                wkv.t[:, kc, half * 4:(half + 1) * 4, :].rearrange("p h d -> p (h d)"),
                                kc == 0, kc == 1, [wkv.r, ckvn.r])
                    self.act(lambda: nc.scalar.copy(out=stg.t[:, half * TT:(half + 1) * TT], in_=bank.t[:, :]),
                             [bank.r], [stg.r])
                fw.dma(fw.act, self.EV[u, tb * 128:(tb + 1) * 128, :], stg.t[:, :], reads=[stg.r],
                       writes=[self.rE[u]])

    def publish_L(self, un):
        nc, fw = self.nc, self.fw
        u = un["u"]
        with self.scope() as st:
            pubs = [self.alloc(st, [128, T], BF16) for _ in range(2)]
            pms = [self.alloc(st, [128, T], BF16) for _ in range(3)]
            n = 0
            for bi, r0 in enumerate(range(0, LROWS, 128)):
                nr = min(128, LROWS - r0)
                pub = pubs[bi % 2]
                fw.dma(fw.sp, pub.t[0:nr, :], self.Lb[u, r0:r0 + nr, :], reads=[self.rL[u]], writes=[pub.r])
                for sl in range(8):
                    pm = pms[n % 3]
                    n += 1
                    self.act(lambda: nc.scalar.activation(out=pm.t[0:nr, :], in_=pub.t[0:nr, :], func=AF.Identity,
                                                          scale=self.masks.t[0:nr, sl:sl + 1]),
                             [pub.r, self.masks.r], [pm.r])
                    fw.dma(fw.act, self.GIN[sl * LROWS + r0: sl * LROWS + r0 + nr, :], pm.t[0:nr, :],
                           reads=[pm.r], writes=[self.rGIN])
        fw.allreduce(self.GOUT, self.GIN, reads=[self.rGIN], writes=[self.rGOUT])

    def publish_halo(self, st, y, yr):
        nc, fw = self.nc, self.fw
        hb = self.alloc(st, [128, KC, 2], F32)
        self.act(lambda: nc.scalar.copy(out=hb.t[:, :, 0:1], in_=y.t[:, :, 0:1]), yr, [hb.r])
        self.act(lambda: nc.scalar.copy(out=hb.t[:, :, 1:2], in_=y.t[:, :, T - 1:T]), yr, [hb.r])
        hms = [self.alloc(st, [128, 2 * KC], F32) for _ in range(2)]
        for sl in range(8):
            hm = hms[sl % 2]
            self.act(lambda: nc.scalar.activation(out=hm.t[:, :], in_=hb.t[:, :, :].rearrange("p c e -> p (c e)"),
                                                  func=AF.Identity, scale=self.masks.t[:, sl:sl + 1]),
                     [hb.r, self.masks.r], [hm.r])
            fw.dma(fw.act, self.HBIN[sl * 128:(sl + 1) * 128, :], hm.t[:, :], reads=[hm.r], writes=[self.rHBIN])
        fw.allreduce(self.HBOUT, self.HBIN, reads=[self.rHBIN], writes=[self.rHBOUT])

    def masked_select(self, acc, width, sel, mcol0):
        nc = self.nc
        for sl in range(8):
            if sl == 0:
                self.dve(lambda: nc.vector.tensor_scalar(out=acc.t[:, 0:width], in0=sel.t[:, 0, 0:width],
                                                         scalar1=self.masks.t[:, mcol0:mcol0 + 1], scalar2=None,
                                                         op0=ALU.mult), [sel.r, self.masks.r], [acc.r])
            else:
                self.dve(lambda: nc.vector.scalar_tensor_tensor(
                    out=acc.t[:, 0:width], in0=sel.t[:, sl, 0:width],
                    scalar=self.masks.t[:, mcol0 + sl:mcol0 + sl + 1], in1=acc.t[:, 0:width],
                    op0=ALU.mult, op1=ALU.add), [sel.r, self.masks.r, acc.r], [acc.r])

    def expand_slots(self, l):
        nc, fw = self.nc, self.fw
        with self.scope() as st:
            wkn = self.alloc(st, [128, 2, 8, 128], BF16)
            wkv = self.alloc(st, [128, 2, 8, 128], BF16)
            wk = self.w_ukv[l].rearrange("(c p) (h d) -> c p h d", p=128, d=256)
            for kc in range(2):
                fw.dma(fw.pool, wkn.t[:, kc, :, :], wk[kc, :, :, 0:128], writes=[wkn.r])
                fw.dma(fw.pool, wkv.t[:, kc, :, :], wk[kc, :, :, 128:256], writes=[wkv.r])
            cks = [self.alloc(st, [128, 2, T], BF16) for _ in range(2)]
            stages = [self.alloc(st, [128, T], BF16) for _ in range(8)]
            r2s = []
            for stg in stages:
                r2 = Reg()
                r2.readers = dict(stg.r.readers)
                r2s.append(r2)
            sti = 0
            for sl in range(8):
                ckvn = cks[sl % 2]
                r0 = sl * LROWS + 256
                fw.dma(fw.sp, ckvn.t[:, :, :], self.GOUT[r0:r0 + 256, :].rearrange("(c p) t -> p c t", p=128),
                       reads=[self.rGOUT], writes=[ckvn.r])
                for h in range(8):
                    banks = self.fm_chunk(lambda kc: wkn.t[:, kc, h, :],
                                          lambda kc, tt: ckvn.t[:, kc, tt * TT:(tt + 1) * TT], 2, [wkn.r, ckvn.r])
                    stg = stages[sti % 8]
                    r2 = r2s[sti % 8]
                    sti += 1
                    for tt, bank in enumerate(banks):
                        if tt == 0:
                            self.act(lambda: nc.scalar.copy(out=stg.t[:, tt * TT:(tt + 1) * TT], in_=bank.t[:, :]),
                                     [bank.r], [stg.r])
                        else:
                            self.dve(lambda: nc.vector.tensor_copy(out=stg.t[:, tt * TT:(tt + 1) * TT],
                                                                   in_=bank.t[:, :]), [bank.r], [r2])
                    fw.dma(fw.sp, self.EKs[sl, h], stg.t[:, :], reads=[stg.r, r2], writes=[self.rEs])
                for tb in range(8):
                    stg = stages[sti % 8]
                    r2 = r2s[sti % 8]
                    sti += 1
                    for half in range(2):
                        bank = self.next_bank()
                        for kc in range(2):
                            self.mm(bank, bank.t[:, :], ckvn.t[:, kc, tb * 128:(tb + 1) * 128],
                                    wkv.t[:, kc, half * 4:(half + 1) * 4, :].rearrange("p h d -> p (h d)"),
                                    kc == 0, kc == 1, [wkv.r, ckvn.r])
                        if half == 0:
                            self.act(lambda: nc.scalar.copy(out=stg.t[:, half * TT:(half + 1) * TT],
                                                            in_=bank.t[:, :]), [bank.r], [stg.r])
                        else:
                            self.dve(lambda: nc.vector.tensor_copy(out=stg.t[:, half * TT:(half + 1) * TT],
                                                                   in_=bank.t[:, :]), [bank.r], [r2])
                    fw.dma(fw.sp, self.EVs[sl, tb * 128:(tb + 1) * 128, :], stg.t[:, :], reads=[stg.r, r2],
                           writes=[self.rEs])

    def rope(self, bankA, bankB, cs, sn, tmp1, tmp2, stg, tt, M):
        nc = self.nc
        sl = slice(tt * TT, (tt + 1) * TT)
        self.dve(lambda: nc.vector.tensor_tensor(out=tmp1.t[0:M, :], in0=bankA.t[0:M, :], in1=cs.t[0:M, sl],
                                                 op=ALU.mult), [bankA.r, cs.r], [tmp1.r])
        self.dve(lambda: nc.vector.tensor_tensor(out=tmp2.t[0:M, :], in0=bankB.t[0:M, :], in1=sn.t[0:M, sl],
                                                 op=ALU.mult), [bankB.r, sn.r], [tmp2.r])
        self.dve(lambda: nc.vector.tensor_tensor(out=stg.t[0:M, sl], in0=tmp1.t[0:M, :], in1=tmp2.t[0:M, :],
                                                 op=ALU.add), [tmp1.r, tmp2.r], [stg.r])

    def attn_begin(self, scale, pts, accO, accL):
        self.ap_scale = scale
        self.ap_pts = pts
        self.ap_accO = accO
        self.ap_accL = accL
        self.ap_q = []
        self.ap_i = 0

    def attn_push(self, tl):
        nc = self.nc
        bank = self.sb[self.sb_rr % len(self.sb)]
        self.sb_rr += 1
        nm = len(tl["s_mms"])
        for mi, (lhsT, rhs, reads) in enumerate(tl["s_mms"]):
            self.mm(bank, bank.t[:, :], lhsT, rhs, mi == 0, mi == nm - 1, reads)
        pt = self.ap_pts[self.ap_i % len(self.ap_pts)]
        self.ap_i += 1
        scale = self.ap_scale
        if tl.get("bias") is not None:
            self.act(lambda: nc.scalar.activation(out=pt.t[:, :], in_=bank.t[:, :], func=AF.Exp, scale=scale,
                                                  bias=tl["bias"]), [bank.r, self.masks.r], [pt.r])
        else:
            self.act(lambda: nc.scalar.activation(out=pt.t[:, :], in_=bank.t[:, :], func=AF.Exp, scale=scale),
                     [bank.r], [pt.r])
        self.ap_q.append((tl, pt))
        if len(self.ap_q) > 2:
            self._attn_pv(*self.ap_q.pop(0))

    def attn_flush(self):
        while self.ap_q:
            self._attn_pv(*self.ap_q.pop(0))

    def _attn_pv(self, tl, pt):
        qt = tl["qt"]
        accO, accL = self.ap_accO, self.ap_accL
        self.mm(accO[qt], accO[qt].t[:, :], tl["v_lhsT"], pt.t[:, :], tl["first"], tl["last"],
                tl["v_reads"] + [pt.r])
        self.mm(accL[qt], accL[qt].t[:, :], self.ones_b.t[:, :], pt.t[:, :], tl["first"], tl["last"],
                [self.ones_b.r, pt.r])

    def attn_epilogue(self, attn, idx, accO, accL, eset, sink_ap=None, sink_reg=None):
        nc = self.nc
        for qt in range(2):
            so, sl = eset[qt]
            self.act(lambda: nc.scalar.copy(out=so.t[:, :], in_=accO[qt].t[:, :]), [accO[qt].r], [so.r])
            if sink_ap is not None:
                self.act(lambda: nc.scalar.activation(out=sl.t[:, :], in_=accL[qt].t[:, :], func=AF.Identity,
                                                      bias=sink_ap, scale=1.0),
                         [accL[qt].r, sink_reg], [sl.r])
            else:
                self.act(lambda: nc.scalar.copy(out=sl.t[:, :], in_=accL[qt].t[:, :]), [accL[qt].r], [sl.r])
        for qt in range(2):
            so, sl = eset[qt]
            self.dve(lambda: nc.vector.reciprocal(out=sl.t[:, :], in_=sl.t[:, :]), [sl.r], [sl.r])
            self.dve(lambda: nc.vector.tensor_tensor(out=attn.t[:, idx, qt * TT:(qt + 1) * TT],
                                                     in0=so.t[:, :], in1=sl.t[:, :], op=ALU.mult),
                     [so.r, sl.r], [attn.r])

    def phaseBC(self, un, l):
        nc, fw = self.nc, self.fw
        u = un["u"]
        ctx = un["ctx"]
        j, nj = un["j"], un["nj"]
        prev_u = u - 1 if j > 0 else None
        next_u = u + 1 if j < nj - 1 else None
        shard = bool(un.get("shard"))
        if shard:
            G3 = self.GOUT.rearrange("(s r) t -> s r t", s=8)
            kr_src = lambda c: (G3[c, 512:576, :], self.rGOUT)
            ek_src = lambda c, h: (self.EKs[c, h], self.rEs)
            ev_src = lambda c, h: (self.EVs[c, :, h * 128:(h + 1) * 128], self.rEs)
        else:
            kr_src = lambda c: (self.Lb[c, 512:576, :], self.rL[c])
            ek_src = lambda c, h: (self.EK[c, h], self.rE[c])
            ev_src = lambda c, h: (self.EV[c, :, h * 128:(h + 1) * 128], self.rE[c])
        with self.scope() as st_outer:
            attn = self.alloc(st_outer, [128, 16, T], BF16)
            with self.scope() as st:
                nctx = len(ctx)
                kr = self.alloc(st, [64, nctx, T], BF16)
                for ci, c in enumerate(ctx):
                    kap, kreg = kr_src(c)
                    fw.dma(fw.sp, kr.t[:, ci, :], kap, reads=[kreg], writes=[kr.r])
                kax = self.alloc(st, [128, 2, 10, 128], BF16)
                vax = self.alloc(st, [128, 10, 256], BF16)

                def va_view(c):
                    return self.Lb[c, 576:832, :].rearrange("r (a d) -> (r a) d", d=256)
                for kv in range(2):
                    fw.dma(fw.sp, kax.t[:, kv, 1:9, :].rearrange("p b k -> p (b k)"),
                           self.Lb[u, kv * 128:(kv + 1) * 128, :], reads=[self.rL[u]], writes=[kax.r])
                    if prev_u is not None:
                        fw.dma(fw.sp, kax.t[:, kv, 0, :], self.Lb[prev_u, kv * 128:(kv + 1) * 128, 896:1024],
                               reads=[self.rL[prev_u]], writes=[kax.r])
                    if next_u is not None:
                        fw.dma(fw.sp, kax.t[:, kv, 9, :], self.Lb[next_u, kv * 128:(kv + 1) * 128, 0:128],
                               reads=[self.rL[next_u]], writes=[kax.r])
                fw.dma(fw.sp, vax.t[:, 1:9, :], va_view(u).rearrange("(tb p) d -> p tb d", p=128),
                       reads=[self.rL[u]], writes=[vax.r])
                if prev_u is not None:
                    fw.dma(fw.sp, vax.t[:, 0, :], va_view(prev_u)[896:1024, :], reads=[self.rL[prev_u]],
                           writes=[vax.r])
                if next_u is not None:
                    fw.dma(fw.sp, vax.t[:, 9, :], va_view(next_u)[0:128, :], reads=[self.rL[next_u]],
                           writes=[vax.r])
                if shard:
                    selk = self.alloc(st, [128, 8, 128], BF16)
                    selv = self.alloc(st, [128, 8, 256], BF16)
                    accf = self.alloc(st, [128, 256], F32)
                    VAv = G3[:, 576:832, :].rearrange("s r (a d) -> s (r a) d", d=256)
                    for blk, lo, mcol0 in ((0, 896, 8), (9, 0, 16)):
                        for kv in range(2):
                            fw.dma(fw.sp, selk.t[:, :, :],
                                   G3[:, kv * 128:(kv + 1) * 128, lo:lo + 128].rearrange("s p k -> p s k"),
                                   reads=[self.rGOUT], writes=[selk.r])
                            self.masked_select(accf, 128, selk, mcol0)
                            self.act(lambda: nc.scalar.copy(out=kax.t[:, kv, blk, :], in_=accf.t[:, 0:128]),
                                     [accf.r], [kax.r])
                        fw.dma(fw.sp, selv.t[:, :, :], VAv[:, lo:lo + 128, :].rearrange("s p d -> p s d"),
                               reads=[self.rGOUT], writes=[selv.r])
                        self.masked_select(accf, 256, selv, mcol0)
                        self.act(lambda: nc.scalar.copy(out=vax.t[:, blk, :], in_=accf.t[:, 0:256]),
                                 [accf.r], [vax.r])
                pts = [self.alloc(st, [128, TT], BF16) for _ in range(4)]
                esets = [[(self.alloc(st, [128, TT], F32), self.alloc(st, [128, TT], F32)) for _ in range(2)]
                         for _ in range(2)]
                qas = [self.alloc(st, [128, T], BF16) for _ in range(2)]
                wts = [self.alloc(st, [128, WTW], BF16) for _ in range(2)]
                qns = [self.alloc(st, [128, T], BF16) for _ in range(2)]
                qrs = [self.alloc(st, [64, T], BF16) for _ in range(2)]
                eks = [self.alloc(st, [128, T], BF16) for _ in range(3)]
                evs = [self.alloc(st, [128, 8, 128], BF16) for _ in range(3)]
                accO = self.banks[0:2]
                accL = self.banks[2:4]
                self.sb = self.banks[4:8]
                self.sb_rr = 0
                for h in range(8):
                    kvh = h // 4
                    qa = qas[h % 2]
                    wt = wts[h % 2]
                    fw.dma(fw.sp, qa.t[:, :], self.QA[u, h], reads=[self.rQ[u]], writes=[qa.r])
                    fw.dma(fw.sp, wt.t[:, :], self.WT[h], reads=[self.rWT], writes=[wt.r])
                    self.attn_begin(SCALE_A, pts, accO, accL)
                    for qt in range(2):
                        blks = [qt * 4 + o for o in range(6)]
                        if not shard:
                            blks = [b for b in blks if not ((b == 0 and prev_u is None) or (b == 9 and next_u is None))]
                        for b in blks:
                            o = b - qt * 4
                            self.attn_push(dict(
                                qt=qt,
                                s_mms=[(kax.t[:, kvh, b, :], qa.t[:, qt * TT:(qt + 1) * TT], [kax.r, qa.r]),
                                       (self.ident_b.t[:, :], wt.t[:, 640 - 128 * o:1152 - 128 * o],
                                        [self.ident_b.r, wt.r])],
                                v_lhsT=vax.t[:, b, kvh * 128:(kvh + 1) * 128], v_reads=[vax.r],
                                bias=(self.masks.t[:, 24:25] if (shard and b == 0) else
                                      self.masks.t[:, 25:26] if (shard and b == 9) else None),
                                first=(b == blks[0]), last=(b == blks[-1])))
                    self.attn_flush()
                    self.attn_epilogue(attn, h, accO, accL, esets[h % 2], self.es.t[:, l, h:h + 1], self.es.r)
                li = 0
                for h in range(8):
                    qn = qns[h % 2]
                    qr = qrs[h % 2]
                    fw.dma(fw.sp, qn.t[:, :], self.QN[u, h], reads=[self.rQ[u]], writes=[qn.r])
                    fw.dma(fw.sp, qr.t[:, :], self.QR[u, h // 2, (h % 2) * 64:(h % 2) * 64 + 64, :],
                           reads=[self.rQ[u]], writes=[qr.r])
                    self.attn_begin(SCALE_B, pts, accO, accL)
                    for ci, c in enumerate(ctx):
                        ek = eks[li % 3]
                        ev = evs[li % 3]
                        li += 1
                        eap, ereg = ek_src(c, h)
                        fw.dma(fw.sp, ek.t[:, :], eap, reads=[ereg], writes=[ek.r])
                        vap, vreg = ev_src(c, h)
                        fw.dma(fw.sp, ev.t[:, :, :], vap.rearrange("(kb p) d -> p kb d", p=128),
                               reads=[vreg], writes=[ev.r])
                        for kb in range(8):
                            for qt in range(2):
                                self.attn_push(dict(
                                    qt=qt,
                                    s_mms=[(ek.t[:, kb * 128:(kb + 1) * 128], qn.t[:, qt * TT:(qt + 1) * TT],
                                            [ek.r, qn.r]),
                                           (kr.t[0:64, ci, kb * 128:(kb + 1) * 128],
                                            qr.t[0:64, qt * TT:(qt + 1) * TT], [kr.r, qr.r])],
                                    v_lhsT=ev.t[:, kb, :], v_reads=[ev.r],
                                    first=(ci == 0 and kb == 0), last=(ci == nctx - 1 and kb == 7)))
                    self.attn_flush()
                    self.attn_epilogue(attn, 8 + h, accO, accL, esets[h % 2])
            with self.scope() as st:
                need_c = (KC * T * 4) + 2 * (KC * 256 * 2) + 2 * (T * 4) + 4 * (T * 2) + 2 * (T * 4) + 2048
                self.ar_ptr = max(self.ar_ptr, (self.ar_end - need_c) // 32 * 32)
                y = self.alloc(st, [128, KC, T], F32)
                wos = [self.alloc(st, [128, KC, 256], BF16) for _ in range(2)]
                xrs = [self.alloc(st, [128, T], F32) for _ in range(2)]
                yr = self.chunk_regs(y, KC)
                ln = self.ln_begin(st, y, yr, l, 0, self.banks[4:8])
                w_o = self.w_o[l]
                mains = self.banks[0:4]
                mrr = 0
                for g in range(8):
                    wo = wos[g % 2]
                    fw.dma(fw.pool, wo.t[:, :, :], w_o[:, g * 256:(g + 1) * 256].rearrange("(c p) n -> p c n", p=128),
                           writes=[wo.r])
                    for jj in range(2):
                        o = 2 * g + jj
                        xr = xrs[o % 2]
                        fw.dma(fw.sp, xr.t[:, :], self.XT[u, o], reads=[self.rXT[u]], writes=[xr.r])
                        for tt in range(2):
                            bank = mains[mrr % 4]
                            mrr += 1
                            for kc in range(KC):
                                self.mm(bank, bank.t[:, :], wo.t[:, kc, jj * 128:(jj + 1) * 128],
                                        attn.t[:, kc, tt * TT:(tt + 1) * TT], kc == 0, kc == KC - 1,
                                        [wo.r, attn.r])
                            self.dve(lambda: nc.vector.scalar_tensor_tensor(
                                out=y.t[:, o, tt * TT:(tt + 1) * TT], in0=xr.t[:, tt * TT:(tt + 1) * TT],
                                scalar=float(ALPHA), in1=bank.t[:, :], op0=ALU.mult, op1=ALU.add),
                                [xr.r, bank.r], [yr[o]])
                        if o > 0:
                            self.ln_stats(ln, o - 1)
                self.ln_stats(ln, KC - 1)
                self.ln_finish(ln)
                if shard:
                    self.publish_halo(st, y, yr)
                for c0 in range(0, KC, 4):
                    fw.dma(fw.sp, self.X1[u, c0:c0 + 4].rearrange("c p t -> p c t"), y.t[:, c0:c0 + 4, :],
                           reads=yr[c0:c0 + 4], writes=[self.rX1[u]])

    def ln_begin(self, st, y, yr, l, which, banks4, stat_at=None, fin_at=None):
        def mk(at):
            if at is None:
                return lambda shape, dt: self.alloc(st, shape, dt)
            pos = [at]

            def al(shape, dt):
                b = self.alloc_at(pos[0], shape, dt)
                pos[0] = b.stop
                return b
            return al
        a1, a2 = mk(stat_at), mk(fin_at)
        ln = dict(y=y, yr=yr, l=l, which=which, psM=banks4[0:2], psQ=banks4[2:4],
                  ybs=[a1([128, T], BF16) for _ in range(2)], yqs=[a1([128, T], BF16) for _ in range(2)], a2=a2)
        return ln

    def ln_stats(self, ln, c):
        nc = self.nc
        y, psM, psQ = ln["y"], ln["psM"], ln["psQ"]
        yb = ln["ybs"][c % 2]
        yq = ln["yqs"][c % 2]
        yr = ln["yr"]
        self.act(lambda: nc.scalar.copy(out=yb.t[:, :], in_=y.t[:, c, :]), [yr[c]], [yb.r])
        self.act(lambda: nc.scalar.activation(out=yq.t[:, :], in_=y.t[:, c, :], func=AF.Square), [yr[c]], [yq.r])
        for tt in range(2):
            self.mm(psM[tt], psM[tt].t[:, :], self.ones_b.t[:, :], yb.t[:, tt * TT:(tt + 1) * TT],
                    c == 0, c == KC - 1, [self.ones_b.r, yb.r])
            self.mm(psQ[tt], psQ[tt].t[:, :], self.ones_b.t[:, :], yq.t[:, tt * TT:(tt + 1) * TT],
                    c == 0, c == KC - 1, [self.ones_b.r, yq.r], signal=True)

    def ln_finish(self, ln):
        nc = self.nc
        y, psM, psQ, l, which = ln["y"], ln["psM"], ln["psQ"], ln["l"], ln["which"]
        mean = ln["a2"]([128, T], F32)
        rstd = ln["a2"]([128, T], F32)
        inv = 1.0 / D_MODEL
        for tt in range(2):
            sl = slice(tt * TT, (tt + 1) * TT)
            self.act(lambda: nc.scalar.mul(out=mean.t[:, sl], in_=psM[tt].t[:, :], mul=inv), [psM[tt].r], [mean.r])
            self.dve(lambda: nc.vector.tensor_tensor(out=rstd.t[:, sl], in0=mean.t[:, sl], in1=mean.t[:, sl],
                                                     op=ALU.mult), [mean.r], [rstd.r])
            self.dve(lambda: nc.vector.scalar_tensor_tensor(out=rstd.t[:, sl], in0=psQ[tt].t[:, :], scalar=inv,
                                                            in1=rstd.t[:, sl], op0=ALU.mult, op1=ALU.subtract),
                     [psQ[tt].r, rstd.r], [rstd.r])
        self.act(lambda: nc.scalar.activation(out=rstd.t[:, :], in_=rstd.t[:, :], func=AF.Sqrt,
                                              bias=self.eps_ln.t[:, 0:1], scale=1.0),
                 [rstd.r, self.eps_ln.r], [rstd.r])
        self.dve(lambda: nc.vector.reciprocal(out=rstd.t[:, :], in_=rstd.t[:, :]), [rstd.r], [rstd.r])
        gi, bi = 2 * which, 2 * which + 1
        yr = ln["yr"]
        for c in range(KC):
            self.dve(lambda: nc.vector.tensor_tensor(out=y.t[:, c, :], in0=y.t[:, c, :], in1=mean.t[:, :],
                                                     op=ALU.subtract), [yr[c], mean.r], [yr[c]])
            self.dve(lambda: nc.vector.tensor_tensor(out=y.t[:, c, :], in0=y.t[:, c, :], in1=rstd.t[:, :],
                                                     op=ALU.mult), [yr[c], rstd.r], [yr[c]])
            self.act(lambda: nc.scalar.activation(out=y.t[:, c, :], in_=y.t[:, c, :], func=AF.Identity,
                                                  bias=self.lnp.t[:, l, bi, c:c + 1],
                                                  scale=self.lnp.t[:, l, gi, c:c + 1]),
                     [yr[c], self.lnp.r], [yr[c]])

    def phaseD(self, un, l, last):
        nc, fw = self.nc, self.fw
        u = un["u"]
        j, nj = un["j"], un["nj"]
        prev_u = u - 1 if j > 0 else None
        next_u = u + 1 if j < nj - 1 else None
        NW = 342
        with self.scope() as st_outer:
            y2 = self.alloc(st_outer, [128, KC, T], F32)
            y2r = self.chunk_regs(y2, KC)
            with self.scope() as st:
                x1q = [self.alloc(st, [128, 4, T + 2], BF16) for _ in range(4)]
                for q in range(4):
                    c0 = 4 * q
                    fw.dma(fw.pool, x1q[q].t[:, :, 1:T + 1], self.X1[u, c0:c0 + 4].rearrange("c p t -> p c t"),
                           reads=[self.rX1[u]], writes=[x1q[q].r])
                if un.get("shard"):
                    hbo = self.alloc(st, [128, 8, 2 * KC], F32)
                    hacc = self.alloc(st, [128, 2 * KC], F32)
                    fw.dma(fw.sp, hbo.t[:, :, :], self.HBOUT.rearrange("(s p) e -> p s e", p=128),
                           reads=[self.rHBOUT], writes=[hbo.r])
                    for mcol0, e, col in ((8, 1, 0), (16, 0, T + 1)):
                        self.masked_select(hacc, 2 * KC, hbo, mcol0)
                        for q in range(4):
                            self.act(lambda: nc.scalar.copy(
                                out=x1q[q].t[:, :, col:col + 1],
                                in_=hacc.t[:, :].rearrange("p (c e) -> p c e", e=2)[:, 4 * q:4 * q + 4, e:e + 1]),
                                [hacc.r], [x1q[q].r])
                elif prev_u is not None:
                    for q in range(4):
                        fw.dma(fw.pool, x1q[q].t[:, :, 0:1],
                               self.X1[prev_u, 4 * q:4 * q + 4, :, T - 1:T].rearrange("c p t -> p c t"),
                               reads=[self.rX1[prev_u]], writes=[x1q[q].r], allow_slow_non_contiguous=True)
                else:
                    for q in range(4):
                        self.dve(lambda: nc.vector.memset(x1q[q].t[:, :, 0:1], 0.0), [], [x1q[q].r])
                if un.get("shard"):
                    pass
                elif next_u is not None:
                    for q in range(4):
                        fw.dma(fw.pool, x1q[q].t[:, :, T + 1:T + 2],
                               self.X1[next_u, 4 * q:4 * q + 4, :, 0:1].rearrange("c p t -> p c t"),
                               reads=[self.rX1[next_u]], writes=[x1q[q].r], allow_slow_non_contiguous=True)
                else:
                    for q in range(4):
                        self.dve(lambda: nc.vector.memset(x1q[q].t[:, :, T + 1:T + 2], 0.0), [], [x1q[q].r])
                def init_y2():
                    for c0 in range(0, KC, 4):
                        fw.dma(fw.sp, y2.t[:, c0:c0 + 4, :], self.X1[u, c0:c0 + 4].rearrange("c p t -> p c t"),
                               reads=[self.rX1[u]], writes=y2r[c0:c0 + 4])
                    for c in range(KC):
                        self.act(lambda: nc.scalar.mul(out=y2.t[:, c, :], in_=y2.t[:, c, :], mul=float(ALPHA)),
                                 [y2r[c]], [y2r[c]])
                wus = [self.alloc(st, [128, KC, 2, 256], BF16) for _ in range(2)]
                wds = [self.alloc(st, [128, 4, D_MODEL], BF16) for _ in range(2)]
                us = [self.alloc(st, [128, T + 2], F32) for _ in range(2)]
                ag = self.alloc(st, [128, T], F32)
                av = self.alloc(st, [128, T], F32)
                sg = self.alloc(st, [128, T], F32)
                hgs = [self.alloc(st, [128, 4, T], BF16) for _ in range(2)]
                w_up = self.w_up[l]
                w_dn = self.w_down[l]
                usets = [self.banks[0:3], self.banks[3:6]]
                dbanks = self.banks[6:8]
                ci = 0
                drr = 0
                wui = 0
                GP = 4
                NG = NPAIR // GP

                def down(gi, o_lo, o_hi):
                    nonlocal drr
                    wd = wds[gi % 2]
                    hg = hgs[gi % 2]
                    for o in range(o_lo, o_hi):
                        for tt in range(2):
                            bank = dbanks[drr % 2]
                            drr += 1
                            for k in range(GP):
                                self.mm(bank, bank.t[:, :], wd.t[:, k, o * 128:(o + 1) * 128],
                                        hg.t[:, k, tt * TT:(tt + 1) * TT], k == 0, k == GP - 1, [wd.r, hg.r])
                            self.dve(lambda: nc.vector.tensor_tensor(
                                out=y2.t[:, o, tt * TT:(tt + 1) * TT], in0=bank.t[:, :],
                                in1=y2.t[:, o, tt * TT:(tt + 1) * TT], op=ALU.add), [bank.r, y2r[o]], [y2r[o]])

                for gi in range(NG):
                    wd = wds[gi % 2]
                    hg = hgs[gi % 2]
                    p0 = GP * gi
                    for kk in range(GP):
                        pr = p0 + kk
                        if kk % 2 == 0:
                            wu = wus[wui % 2]
                            wui += 1
                            for gv in range(2):
                                c0 = gv * D_FF + pr * 128
                                fw.dma(fw.pool, wu.t[:, :, gv, :],
                                       w_up[:, c0:c0 + 256].rearrange("(c p) n -> p c n", p=128), writes=[wu.r])
                        if kk == 2:
                            fw.dma(fw.pool, wd.t[:, :, :],
                                   w_dn[p0 * 128:(p0 + GP) * 128, :].rearrange("(k p) n -> p k n", p=128),
                                   writes=[wd.r])
                        k = kk % 2
                        for gv in range(2):
                            bset = usets[ci % 2]
                            ub = us[ci % 2]
                            ci += 1
                            for jn in range(3):
                                bank = bset[jn]
                                for kc in range(KC):
                                    self.mm(bank, bank.t[:, 0:NW], wu.t[:, kc, gv, k * 128:(k + 1) * 128],
                                            x1q[kc // 4].t[:, kc % 4, jn * NW:(jn + 1) * NW], kc == 0, kc == KC - 1,
                                            [wu.r, x1q[kc // 4].r])
                                self.act(lambda: nc.scalar.copy(out=ub.t[:, jn * NW:(jn + 1) * NW],
                                                                in_=bank.t[:, 0:NW]), [bank.r], [ub.r])
                            if gi > 0:
                                qd = 2 * kk + gv
                                down(gi - 1, 2 * qd, 2 * qd + 2)
                            a = ag if gv == 0 else av
                            col = gv * NPAIR + pr
                            self.act(lambda: nc.scalar.activation(out=a.t[:, :], in_=ub.t[:, 1:T + 1],
                                                                  func=AF.Identity,
                                                                  bias=self.cb.t[:, l, col:col + 1],
                                                                  scale=self.cw.t[:, l, 1, col:col + 1]),
                                     [ub.r, self.cw.r, self.cb.r], [a.r])
                            self.dve(lambda: nc.vector.scalar_tensor_tensor(
                                out=a.t[:, :], in0=ub.t[:, 0:T], scalar=self.cw.t[:, l, 0, col:col + 1],
                                in1=a.t[:, :], op0=ALU.mult, op1=ALU.add), [ub.r, a.r, self.cw.r], [a.r])
                            self.dve(lambda: nc.vector.scalar_tensor_tensor(
                                out=a.t[:, :], in0=ub.t[:, 2:T + 2], scalar=self.cw.t[:, l, 2, col:col + 1],
                                in1=a.t[:, :], op0=ALU.mult, op1=ALU.add), [ub.r, a.r, self.cw.r], [a.r])
                            if gv == 0:
                                self.act(lambda: nc.scalar.activation(out=sg.t[:, :], in_=ag.t[:, :], func=AF.Silu),
                                         [ag.r], [sg.r])
                        self.dve(lambda: nc.vector.tensor_tensor(out=hg.t[:, kk, :], in0=sg.t[:, :], in1=av.t[:, :],
                                                                 op=ALU.mult), [sg.r, av.r], [hg.r])
                    if gi == 0:
                        init_y2()
                ln = self.ln_begin(st, y2, y2r, l, 1, self.banks[0:4], stat_at=hgs[(NG - 2) % 2].start,
                                   fin_at=hgs[(NG - 1) % 2].start)
                for o in range(KC):
                    down(NG - 1, o, o + 1)
                    if o > 0:
                        self.ln_stats(ln, o - 1)
                self.ln_stats(ln, KC - 1)
                ost_at = us[0].start
            with self.scope() as st:
                self.ln_finish(ln)
                if not last:
                    for c0 in range(0, KC, 4):
                        fw.dma(fw.sp, self.XT[u, c0:c0 + 4].rearrange("c p t -> p c t"), y2.t[:, c0:c0 + 4, :],
                               reads=y2r[c0:c0 + 4], writes=[self.rXT[u]])
                else:
                    osts = [self.alloc_at(ost_at, [128, D_MODEL], F32)]
                    osts.append(self.alloc_at(osts[0].stop, [128, D_MODEL], F32))
                    n = 0
                    for tb in range(8):
                        ost = osts[tb % 2]
                        for g in range(4):
                            bank = self.banks[4 + n % 4]
                            n += 1
                            for jn in range(4):
                                c = g * 4 + jn
                                fw.op(fw.pe, lambda: nc.tensor.transpose(bank.t[:, jn * 128:(jn + 1) * 128],
                                                                         y2.t[:, c, tb * 128:(tb + 1) * 128],
                                                                         self.ident_f.t[:, :]),
                                      reads=[y2r[c], self.ident_f.r], writes=[bank.r], signal=(jn == 3))
                            if n % 2:
                                self.act(lambda: nc.scalar.copy(out=ost.t[:, g * 512:(g + 1) * 512], in_=bank.t[:, :]),
                                         [bank.r], [ost.r])
                            else:
                                self.dve(lambda: nc.vector.tensor_copy(out=ost.t[:, g * 512:(g + 1) * 512],
                                                                       in_=bank.t[:, :]), [bank.r], [ost.r])
                        r0 = un["tok0"] + tb * 128
                        fw.dma(fw.sp, self.yout[r0:r0 + 128, :], ost.t[:, :], reads=[ost.r], writes=[self.rOut])

    def build(self):
        self.setup_consts()
        for un in self.units:
            self.phase0(un)
        sh = [un for un in self.units if un.get("shard")]
        pr = [un for un in self.units if not un.get("shard")]
        for l in range(self.n_layers):
            last = (l == self.n_layers - 1)
            for un in sh:
                self.phaseA(un, l)
                self.publish_L(un)
            for un in pr:
                self.phaseA(un, l)
            for un in sh:
                self.expand_slots(l)
                self.phaseBC(un, l)
            for un in pr:
                self.phaseBC(un, l)
            for un in pr:
                self.phaseD(un, l, last)
            for un in sh:
                self.phaseD(un, l, last)
        self.fw.finish(self.fw.sp)
        return self.nc


def _const_tables(maxS, rel_bias):
    import jax
    import jax.numpy as jnp
    cpu = jax.devices("cpu")[0]
    with jax.default_device(cpu):
        inv = 1.0 / (10000.0 ** (jnp.arange(0, 64, 2, dtype=jnp.float32) / 64))
        ang = jnp.arange(maxS, dtype=jnp.float32)[:, None] * inv[None, :]
        cos = np.asarray(jnp.cos(ang).astype(jnp.float32)).T
        sin = np.asarray(jnp.sin(ang).astype(jnp.float32)).T
        rel = jnp.arange(-1200, 1201, dtype=jnp.int32)
        half, max_exact = 16, 8
        ret = (rel > 0).astype(jnp.int32) * half
        n = jnp.abs(rel)
        large = max_exact + (jnp.log(jnp.maximum(n, 1).astype(jnp.float32) / max_exact)
                             / math.log(128 / max_exact) * (half - max_exact)).astype(jnp.int32)
        large = jnp.minimum(large, half - 1)
        bucket = np.asarray(ret + jnp.where(n < max_exact, n, large))
    idx = np.arange(128) % 32
    cosT = np.ascontiguousarray(cos[idx, :])
    sgn = np.where((np.arange(128) % 64) < 32, -1.0, 1.0).astype(np.float32)[:, None]
    sinT = np.ascontiguousarray(sin[idx, :] * sgn)
    kl = np.arange(128)[:, None]
    jj = np.arange(WTW)[None, :]
    relm = kl - jj + 512
    bidx = bucket[relm + 1200]
    valid = np.abs(relm) <= 128
    rb_ext = np.concatenate([np.asarray(rel_bias, np.float32), np.full((1, 8), NEG, np.float32)], axis=0)
    bidx = np.where(valid, bidx, 32)
    wtab = np.ascontiguousarray(rb_ext[bidx].transpose(2, 0, 1))
    return cosT.astype(np.float32), sinT.astype(np.float32), wtab.astype(np.float32)


_WNAMES = ["w_in", "sink", "q_norm_g", "w_uq", "kv_norm_g", "w_ukv", "w_o", "ln1_g", "ln1_b", "w_up",
           "conv_w", "conv_b", "w_down", "ln2_g", "ln2_b"]


def core_masks(c, n=8):
    m = np.zeros((128, 26), np.float32)
    m[:, c] = 1.0
    if c > 0:
        m[:, 8 + c - 1] = 1.0
    else:
        m[:, 24] = NEG
    if c < n - 1:
        m[:, 16 + c + 1] = 1.0
    else:
        m[:, 25] = NEG
    return m


def make_in_maps(b, x_seqs_per_core, shard_chunks, weights, rel_bias, n_layers, n_cores):
    maxS = max(b.maxS, T * n_cores if shard_chunks is not None else 0)
    cosT, sinT, wtab = _const_tables(maxS, rel_bias)
    ident = np.eye(128, dtype=np.float32)
    in_maps = []
    for c in range(n_cores):
        xs = list(x_seqs_per_core[c])
        if shard_chunks is not None:
            xs.append(shard_chunks[c])
        m = {"xin": np.ascontiguousarray(np.concatenate(xs, axis=0), dtype=np.float32),
             "cosT": np.ascontiguousarray(cosT[:, :b.maxS]), "sinT": np.ascontiguousarray(sinT[:, :b.maxS]),
             "wtab": wtab, "ident": ident}
        if shard_chunks is not None:
            m["cosS"] = np.ascontiguousarray(cosT[:, c * T:(c + 1) * T])
            m["sinS"] = np.ascontiguousarray(sinT[:, c * T:(c + 1) * T])
            m["masks"] = core_masks(c, n_cores)
        for k in _WNAMES:
            m[k] = np.ascontiguousarray(np.asarray(weights[k], np.float32)[:n_layers])
        in_maps.append(m)
    return in_maps


def kernel(x_prompt, x_sample, rel_bias, w_in, sink, q_norm_g, w_uq, kv_norm_g, w_ukv, w_o,
           ln1_g, ln1_b, w_up, conv_w, conv_b, w_down, ln2_g, ln2_b):
    weights = dict(w_in=w_in, sink=sink, q_norm_g=q_norm_g, w_uq=w_uq, kv_norm_g=kv_norm_g, w_ukv=w_ukv,
                   w_o=w_o, ln1_g=ln1_g, ln1_b=ln1_b, w_up=w_up, conv_w=conv_w, conv_b=conv_b,
                   w_down=w_down, ln2_g=ln2_g, ln2_b=ln2_b)
    x_prompt = np.asarray(x_prompt, np.float32)
    x_sample = np.asarray(x_sample, np.float32)
    n = 8
    S = x_prompt.shape[1]
    per_core = [[x_prompt[2 * c], x_prompt[2 * c + 1]] for c in range(n)]
    chunks = [x_sample[0, c * T:(c + 1) * T] for c in range(n)]
    b = Builder([S, S], DEPTH, shard=True)
    nc = b.build()
    in_maps = make_in_maps(b, per_core, chunks, weights, rel_bias, DEPTH, n)
    res = run_bass_kernel_spmd(nc, in_maps, core_ids=list(range(n)))
    ys = [res.results[c]["yout"] for c in range(n)]
    y_prompt = np.stack([ys[c][i * S:(i + 1) * S] for c in range(n) for i in range(2)], axis=0)
    y_sample = np.concatenate([ys[c][2 * S:2 * S + T] for c in range(n)], axis=0)[None]
    return (y_prompt.astype(np.float32), y_sample.astype(np.float32))
```
